# Optimizing a Trainium2 kernel written in Bass

```python
import math
import jax, jax.numpy as jnp
from jax import lax
import numpy as np

D_MODEL = 2048
BATCH = 8
SEQ = 2048
DEPTH = 4

GRID_W = 64
CTX_LEN = 256
N_MIXERS = 2
N_FOURIER_LAYERS = (DEPTH + 1) // 2
N_SSD_LAYERS = DEPTH // 2
FOURIER_GROUPS = 8
SSD_EXPAND = 2
SSD_D_INNER = SSD_EXPAND * D_MODEL
SSD_HEAD_DIM = 64
SSD_HEADS = SSD_D_INNER // SSD_HEAD_DIM
SSD_GROUPS = 8
SSD_STATE = 128
SSD_CONV = 3
SSD_CHUNK = 128
SSD_GN = SSD_GROUPS * SSD_STATE
SSD_STATE_COLS = SSD_D_INNER + SSD_GN + 2 * SSD_HEADS
SSD_IN_DIM = SSD_STATE_COLS + SSD_GN + SSD_D_INNER
SSD_CONV_DIM = SSD_D_INNER + 2 * SSD_GN
D_FF = 5632
FFN_CONV = 3
EPS = 1e-6
MOD_SCALE = 0.5
DT_MIN = 1e-3
DT_MAX = 1e-1
A_MIN = 1.0
A_MAX = 16.0

kernel_name = "hybrid_fourier_ssd_dit_trunk"


def rmsnorm(x, g):
    xf = x.astype(jnp.float32)
    y = xf * lax.rsqrt(jnp.mean(xf * xf, axis=-1, keepdims=True) + EPS)
    return (y * g.astype(jnp.float32)).astype(x.dtype)


def modulate(h, shift, scale):
    return h * (1 + scale) + shift


def dwconv1d(u, w, b):
    K = w.shape[0]
    pad = K // 2
    L = u.shape[1]
    up = jnp.pad(u, ((0, 0), (pad, pad), (0, 0)))
    out = up[:, 0:L] * w[0]
    for k in range(1, K):
        out = out + up[:, k:k + L] * w[k]
    return out + b


def dwconv2d_grid(u, w, b):
    Bsz, L, C = u.shape
    rows = L // GRID_W
    kh, kw = w.shape[0], w.shape[1]
    g = jnp.pad(u.reshape(Bsz, rows, GRID_W, C), ((0, 0), (kh // 2, kh // 2), (kw // 2, kw // 2), (0, 0)))
    out = b
    for i in range(kh):
        for j in range(kw):
            out = out + g[:, i:i + rows, j:j + GRID_W] * w[i, j]
    return out.reshape(Bsz, L, C)


def fourier_mix(h, w):
    Bsz, L, D = h.shape
    hg = h.astype(jnp.float32).reshape(Bsz, L, FOURIER_GROUPS, D // FOURIER_GROUPS)
    f = jnp.fft.fft2(hg, axes=(1, 3), norm="ortho").real
    return f.reshape(Bsz, L, D).astype(h.dtype) @ w


def conv_ffn(h, w_up, w_down, conv_fn):
    gate, val = jnp.split(h @ w_up, 2, axis=-1)
    return (jax.nn.silu(conv_fn(gate)) * val) @ w_down


def ssd_chunked(xs, dt, a, bm, cm, h0, with_output):
    f32 = jnp.float32
    Bsz, L, H, P = xs.shape
    G, N = bm.shape[-2], bm.shape[-1]
    R = H // G
    nc = L // SSD_CHUNK
    dtc = dt.astype(f32).reshape(Bsz, nc, SSD_CHUNK, G, R)
    xdt = xs.astype(f32).reshape(Bsz, nc, SSD_CHUNK, G, R, P) * dtc[..., None]
    acs = jnp.cumsum(dtc * a.astype(f32).reshape(G, R), axis=2)
    bmc = bm.astype(f32).reshape(Bsz, nc, SSD_CHUNK, G, N)
    decay_to_end = jnp.exp(acs[:, :, -1:] - acs)
    states = jnp.einsum('bcsgn,bcsgr,bcsgrp->bcgrpn', bmc, decay_to_end, xdt)

    def step(h, inp):
        s, d = inp
        return h * jnp.exp(d)[..., None, None] + s, h

    h_init = h0.astype(f32).reshape(Bsz, G, R, P, N)
    final, h_in = lax.scan(step, h_init, (jnp.moveaxis(states, 1, 0), jnp.moveaxis(acs[:, :, -1], 1, 0)))
    final = final.reshape(Bsz, H, P, N)
    if not with_output:
        return None, final
    h_in = jnp.moveaxis(h_in, 0, 1)
    cmc = cm.astype(f32).reshape(Bsz, nc, SSD_CHUNK, G, N)
    acs_t = jnp.moveaxis(acs, 2, -1)
    seg = acs_t[..., :, None] - acs_t[..., None, :]
    lower = jnp.tril(jnp.ones((SSD_CHUNK, SSD_CHUNK), dtype=bool))
    decay = jnp.exp(jnp.where(lower, seg, -jnp.inf))
    scores = jnp.einsum('bclgn,bcsgn->bcgls', cmc, bmc)
    y_diag = jnp.einsum('bcgls,bcgrls,bcsgrp->bclgrp', scores, decay, xdt)
    y_off = jnp.einsum('bclgn,bcgrpn,bclgr->bclgrp', cmc, h_in, jnp.exp(acs))
    return (y_diag + y_off).reshape(Bsz, L, H, P), final


def ssd_branch_inputs(h, w_in, conv_w, conv_b, full):
    Bsz, L, _ = h.shape
    xb_dim = SSD_D_INNER + SSD_GN
    p = h @ (w_in if full else w_in[:, :SSD_STATE_COLS])
    xb = jax.nn.silu(dwconv1d(p[..., :xb_dim], conv_w[:, :xb_dim], conv_b[:xb_dim]))
    xs = xb[..., :SSD_D_INNER].reshape(Bsz, L, SSD_HEADS, SSD_HEAD_DIM)
    bm = xb[..., SSD_D_INNER:].reshape(Bsz, L, SSD_GROUPS, SSD_STATE)
    dt_raw = p[..., xb_dim:SSD_STATE_COLS].reshape(Bsz, L, 2, SSD_HEADS)
    if not full:
        return xs, bm, dt_raw, None, None
    cpre = p[..., SSD_STATE_COLS:SSD_STATE_COLS + SSD_GN]
    cm = jax.nn.silu(dwconv1d(cpre, conv_w[:, xb_dim:], conv_b[xb_dim:])).reshape(Bsz, L, SSD_GROUPS, SSD_STATE)
    z = p[..., SSD_STATE_COLS + SSD_GN:]
    return xs, bm, dt_raw, cm, z


def bidir_ssd(xs, bm, cm, dt_raw, dt_bias, a_log, d_skip, h0_fwd, h0_bwd, with_output):
    f32 = jnp.float32
    dt = jax.nn.softplus(dt_raw.astype(f32) + dt_bias.astype(f32))
    a = -jnp.exp(a_log.astype(f32))

    def rev(t):
        return None if t is None else jnp.flip(t, axis=1)

    y_f, h_f = ssd_chunked(xs, dt[:, :, 0], a[0], bm, cm, h0_fwd, with_output)
    y_b, h_b = ssd_chunked(rev(xs), rev(dt[:, :, 1]), a[1], rev(bm), rev(cm), h0_bwd, with_output)
    if not with_output:
        return None, h_f, h_b
    skip = (d_skip[0] + d_skip[1]).astype(f32)[:, None] * xs.astype(f32)
    return y_f + rev(y_b) + skip, h_f, h_b


def ssd_gated_out(y, z, norm_g, w_out):
    Bsz, L = y.shape[0], y.shape[1]
    g = y.reshape(Bsz, L, SSD_GROUPS, SSD_D_INNER // SSD_GROUPS) * jax.nn.silu(
        z.astype(jnp.float32)).reshape(Bsz, L, SSD_GROUPS, SSD_D_INNER // SSD_GROUPS)
    g = g * lax.rsqrt(jnp.mean(g * g, axis=-1, keepdims=True) + EPS)
    g = g.reshape(Bsz, L, SSD_D_INNER) * norm_g.astype(jnp.float32)
    return g.astype(z.dtype) @ w_out


def ssd_mixer(a_lat, a_ctx, w_in, conv_w, conv_b, dt_bias, a_log, d_skip, norm_g, w_out, ctx_out):
    Bsz = a_lat.shape[0]
    zeros = jnp.zeros((Bsz, SSD_HEADS, SSD_HEAD_DIM, SSD_STATE), jnp.float32)
    cx, cb, cdt, cc, cz = ssd_branch_inputs(a_ctx, w_in, conv_w, conv_b, ctx_out)
    y_ctx, h_f, h_b = bidir_ssd(cx, cb, cc, cdt, dt_bias, a_log, d_skip, zeros, zeros, ctx_out)
    lx, lb, ldt, lc, lz = ssd_branch_inputs(a_lat, w_in, conv_w, conv_b, True)
    y_lat, _, _ = bidir_ssd(lx, lb, lc, ldt, dt_bias, a_log, d_skip, h_f, h_b, True)
    out_lat = ssd_gated_out(y_lat, lz, norm_g, w_out)
    out_ctx = ssd_gated_out(y_ctx, cz, norm_g, w_out) if ctx_out else None
    return out_lat, out_ctx


def setup_inputs(seed: int = 0) -> dict:
    key = jax.random.key(seed)
    ks = jax.random.split(key, 24)
    f32 = jnp.float32
    n_a, n_b = N_FOURIER_LAYERS, N_SSD_LAYERS

    def dense(k, shape, fan_in, scale=1.0):
        return jax.random.normal(k, shape, f32) * (scale * fan_in ** -0.5)

    def gain(k, shape):
        return 1.0 + 0.05 * jax.random.normal(k, shape, f32)

    def small(k, shape):
        return 0.02 * jax.random.normal(k, shape, f32)

    dt = jnp.exp(jax.random.uniform(ks[13], (n_b, 2, SSD_HEADS), f32, math.log(DT_MIN), math.log(DT_MAX)))
    dt_bias = dt + jnp.log(-jnp.expm1(-dt))
    a_log = jnp.log(jax.random.uniform(ks[14], (n_b, 2, SSD_HEADS), f32, A_MIN, A_MAX))
    return {
        "x": jax.random.normal(ks[0], (BATCH, SEQ, D_MODEL), f32),
        "c": jax.random.normal(ks[1], (BATCH, D_MODEL), f32),
        "ctx": jax.random.normal(ks[2], (BATCH, CTX_LEN, D_MODEL), f32),
        "c_ctx": jax.random.normal(ks[3], (D_MODEL,), f32),
        "w_mod": dense(ks[4], (DEPTH, D_MODEL, 6 * D_MODEL), D_MODEL, MOD_SCALE),
        "b_mod": small(ks[5], (DEPTH, 6 * D_MODEL)),
        "norm_mix_g": gain(ks[6], (DEPTH, D_MODEL)),
        "norm_ffn_g": gain(ks[7], (DEPTH, D_MODEL)),
        "four_w": dense(ks[8], (n_a, D_MODEL, D_MODEL), D_MODEL),
        "ssd_w_in": dense(ks[9], (n_b, D_MODEL, SSD_IN_DIM), D_MODEL),
        "ssd_conv_w": dense(ks[10], (n_b, SSD_CONV, SSD_CONV_DIM), SSD_CONV),
        "ssd_conv_b": small(ks[11], (n_b, SSD_CONV_DIM)),
        "ssd_dt_bias": dt_bias,
        "ssd_a_log": a_log,
        "ssd_d": 1.0 + 0.1 * jax.random.normal(ks[12], (n_b, 2, SSD_HEADS), f32),
        "ssd_norm_g": gain(ks[15], (n_b, SSD_D_INNER)),
        "ssd_w_out": dense(ks[16], (n_b, SSD_D_INNER, D_MODEL), SSD_D_INNER),
        "ffn_w_up": dense(ks[17], (DEPTH, D_MODEL, 2 * D_FF), D_MODEL),
        "ffn_conv_w": dense(ks[18], (DEPTH, FFN_CONV, FFN_CONV, D_FF), FFN_CONV * FFN_CONV),
        "ffn_conv_b": small(ks[19], (DEPTH, D_FF)),
        "ffn_w_down": dense(ks[20], (DEPTH, D_FF, D_MODEL), D_FF),
        "final_g": gain(ks[21], (D_MODEL,)),
    }


def reference(x, c, ctx, c_ctx, w_mod, b_mod, norm_mix_g, norm_ffn_g, four_w, ssd_w_in, ssd_conv_w,
              ssd_conv_b, ssd_dt_bias, ssd_a_log, ssd_d, ssd_norm_g, ssd_w_out, ffn_w_up, ffn_conv_w,
              ffn_conv_b, ffn_w_down, final_g):
    s_lat = jax.nn.silu(c)
    s_ctx = jax.nn.silu(c_ctx)
    for i in range(DEPTH):
        last = i == DEPTH - 1
        is_ssd = (i % N_MIXERS) == 1
        j = i // N_MIXERS
        sh1, sc1, g1, sh2, sc2, g2 = jnp.split((s_lat @ w_mod[i] + b_mod[i])[:, None, :], 6, axis=-1)
        a_lat = modulate(rmsnorm(x, norm_mix_g[i]), sh1, sc1)
        if (not last) or is_ssd:
            n_cols = 2 * D_MODEL if last else 6 * D_MODEL
            m_ctx = jnp.split(s_ctx @ w_mod[i][:, :n_cols] + b_mod[i][:n_cols], n_cols // D_MODEL)
            a_ctx = modulate(rmsnorm(ctx, norm_mix_g[i]), m_ctx[0], m_ctx[1])
        if is_ssd:
            y_lat, y_ctx = ssd_mixer(a_lat, a_ctx, ssd_w_in[j], ssd_conv_w[j], ssd_conv_b[j], ssd_dt_bias[j],
                                     ssd_a_log[j], ssd_d[j], ssd_norm_g[j], ssd_w_out[j], not last)
        else:
            y_lat = fourier_mix(a_lat, four_w[j])
            y_ctx = None if last else fourier_mix(a_ctx, four_w[j])
        x = x + g1 * y_lat
        b_lat = modulate(rmsnorm(x, norm_ffn_g[i]), sh2, sc2)
        x = x + g2 * conv_ffn(b_lat, ffn_w_up[i], ffn_w_down[i],
                              lambda u: dwconv2d_grid(u, ffn_conv_w[i], ffn_conv_b[i]))
        if not last:
            ctx = ctx + m_ctx[2] * y_ctx
            b_ctx = modulate(rmsnorm(ctx, norm_ffn_g[i]), m_ctx[3], m_ctx[4])
            ctx = ctx + m_ctx[5] * conv_ffn(b_ctx, ffn_w_up[i], ffn_w_down[i],
                                            lambda u: dwconv1d(u, ffn_conv_w[i][FFN_CONV // 2], ffn_conv_b[i]))
    return rmsnorm(x, final_g)
```

```python
import math
from contextlib import ExitStack

import ml_dtypes
import numpy as np

import concourse.bass as bass
import concourse.mybir as mybir
from concourse.bass_utils import run_bass_kernel_spmd

F32 = mybir.dt.float32
BF16 = mybir.dt.bfloat16
AF = mybir.ActivationFunctionType
ALU = mybir.AluOpType
AX = mybir.AxisListType

D = 2048
L = 2048
LC = 256
T = L + LC
DFF = 5632
NL = 4
KC = 16
DIN = 4096
GN = 1024
H = 64
SSD_IN = 10368
STATE_COLS = DIN + GN + 2 * H
EPS = 1e-6
TT = [(0, 512), (512, 512), (1024, 512), (1536, 512), (2048, 256)]
NCH = 18
GS = 8
GPL = 34 * 66
GPW = GPL + 258


class Res:
    __slots__ = ("w", "r", "name", "excl")

    def __init__(self, name="", excl=False):
        self.w = None
        self.r = {}
        self.name = name
        self.excl = excl


class _Q:
    def __init__(self, name, eng, sem):
        self.name = name
        self.eng = eng
        self.sem = sem
        self.cnt = 0
        self.seen = {}
        self.slots = []
        self.di = 0


class Sched:
    NSLOT = 12

    def __init__(self, nc, es):
        self.nc = nc
        self.q = {}
        for name, eng in (("pe", nc.tensor), ("act", nc.scalar), ("dve", nc.vector), ("pool", nc.gpsimd), ("sp", nc.sync)):
            self.q[name] = _Q(name, eng, es.enter_context(nc.semaphore("s_" + name)))
        for qn in ("sp", "pool"):
            for i in range(self.NSLOT):
                self.q[qn].slots.append([es.enter_context(nc.semaphore(f"d_{qn}{i}")), 0, f"d_{qn}{i}"])
        self.banks = []
        self.bi = 0

    def _wait(self, q, tok):
        key, sem, val = tok
        if q.seen.get(key, 0) >= val:
            return
        q.eng.wait_ge(sem, val)
        q.seen[key] = val

    def _deps(self, q, reads, writes, skip_self, skip_same=False):
        for r in reads:
            if r.w is not None and not (skip_self and r.w[0] == q.name):
                self._wait(q, r.w)
            if r.excl:
                for tok in r.r.values():
                    if tok[0] != q.name:
                        self._wait(q, tok)
        for w in writes:
            if w.w is not None and not (skip_same and w.w[0] == q.name):
                self._wait(q, w.w)
            for tok in w.r.values():
                if not (skip_same and tok[0] == q.name):
                    self._wait(q, tok)

    def _mark(self, tok, reads, writes):
        for r in reads:
            old = r.r.get(tok[0])
            if old is None or old[2] < tok[2]:
                r.r[tok[0]] = tok
        for w in writes:
            w.w = tok
            w.r = {}

    def op(self, E, fn, reads=(), writes=(), inc=True):
        q = self.q[E]
        self._deps(q, reads, writes, E == "pe", True)
        ins = fn()
        if inc:
            ins.then_inc(q.sem, 1)
            q.cnt += 1
            tok = (q.name, q.sem, q.cnt)
        else:
            tok = (q.name, q.sem, q.cnt + 1)
        self._mark(tok, reads, writes)
        return tok

    def dma(self, Qn, out, in_, reads=(), writes=(), **kw):
        q = self.q[Qn]
        slot = q.slots[q.di % self.NSLOT]
        q.di += 1
        if slot[1] > 0:
            self._wait(q, (slot[2], slot[0], 16 * slot[1]))
        self._deps(q, reads, writes, False)
        q.eng.dma_start(out=out, in_=in_, **kw).then_inc(slot[0], 16)
        slot[1] += 1
        tok = (slot[2], slot[0], 16 * slot[1])
        self._mark(tok, reads, writes)
        return tok

    def barrier(self):
        toks = []
        for q in self.q.values():
            if q.cnt > 0:
                toks.append((q.name, q.sem, q.cnt))
            for s in q.slots:
                if s[1] > 0:
                    toks.append((s[2], s[0], 16 * s[1]))
        for q in self.q.values():
            for t in toks:
                if t[0] != q.name:
                    self._wait(q, t)

    def bank(self):
        b = self.banks[self.bi % len(self.banks)]
        self.bi += 1
        return b


class _Stop(Exception):
    pass


class Phase:
    uid = 0

    def __init__(self, S):
        self.S = S
        self.es = ExitStack()
        self.n = 0

    def __enter__(self):
        self.es.__enter__()
        return self

    def sb(self, shape, dt, name="t"):
        Phase.uid += 1
        t = self.es.enter_context(self.S.nc.sbuf_tensor(f"{name}_{Phase.uid}", list(shape), dt))
        return t, Res(name)

    def __exit__(self, *a):
        self.S.barrier()
        self.es.__exit__(None, None, None)
        return False


def build(upto=999, final=True, ssd_upto=99):
    nc = bass.Bass("TRN2", target_bir_lowering=False)

    def din(name, shape, dt=F32):
        return nc.dram_tensor(name, list(shape), dt, kind="ExternalInput").ap()

    xc_in = din("xc", [D, T])
    cv_in = din("cv", [128, KC, 2])
    w_mod = din("w_mod", [NL, D, 6 * D])
    bmod_in = din("bmodT", [128, NL, 96])
    gmix_in = din("gmixT", [128, NL, KC])
    gffn_in = din("gffnT", [128, NL, KC])
    gfin_in = din("gfinT", [128, KC])
    four_w = din("four_w", [2, D, D])
    w_in = din("ssd_w_in", [2, D, SSD_IN])
    scw_in = din("scwT", [128, 2, 48, 3])
    scb_in = din("scbT", [128, 2, 48])
    dtb_in = din("dtb_bc", [128, 2, 128])
    alog_in = din("alog_bc", [128, 2, 128])
    dsk_in = din("dsk_bc", [128, 2, 128])
    sng_in = din("sngT", [128, 2, 32])
    w_out = din("ssd_w_out", [2, DIN, D])
    w_up = din("ffn_w_up", [NL, D, 2 * DFF])
    fcw_in = din("fcwT", [128, NL, 44, 9])
    fcb_in = din("fcbT", [128, NL, 44])
    w_down = din("ffn_w_down", [NL, DFF, D])
    cs1_in = din("cs1", [128, 2, 512], BF16)
    cs2c_in = din("cs2c", [128, 2, 512], BF16)
    cl_in = din("cl", [L, L], BF16)
    nsl_in = din("nsl", [L, L], BF16)
    identb_in = din("identb", [128, 128], BF16)
    cf_in = din("cf32", [128, 4, 128])
    out_d = nc.dram_tensor("out", [D, T], F32, kind="ExternalOutput").ap()

    XT = nc.dram_tensor("XT", [D, T], F32).ap()
    UVd = nc.dram_tensor("UVd", [T, 8, 512], BF16).ap()
    XBC = nc.dram_tensor("XBC", [48 * 128, T], BF16).ap()
    Zd = nc.dram_tensor("Zd", [T, DIN], BF16).ap()
    XSd = nc.dram_tensor("XSd", [T, DIN], BF16).ap()
    BTMd = nc.dram_tensor("BTMd", [T, GN], BF16).ap()
    HFd = nc.dram_tensor("HFd", [NCH, 128, DIN], BF16).ap()
    GTd = nc.dram_tensor("GTd", [DIN, T], BF16).ap()

    with ExitStack() as es:
        S = Sched(nc, es)
        for i in range(8):
            S.banks.append((es.enter_context(nc.psum_tensor(f"bank{i}", [128, 512], F32)), Res(f"bank{i}", excl=True)))

        def psb(shape, dt, name):
            return es.enter_context(nc.sbuf_tensor("sb_" + name, list(shape), dt)), Res(name)

        ones_bf, r_ones = psb([128, 128], BF16, "ones_bf")
        ones_f, r_onesf = psb([128, 128], F32, "ones_f")
        identb, r_identb = psb([128, 128], BF16, "identb")
        cf, r_cf = psb([128, 4, 128], F32, "cf")
        mkb, r_mkb = psb([128, 2, 128], BF16, "mkb")
        sT, r_sT = psb([128, KC, 2], BF16, "sT")
        MOD, r_MOD = psb([128, NL, 96, 2], F32, "MOD")
        A1, r_A1 = psb([128, NL, KC, 2], F32, "A1")
        A2, r_A2 = psb([128, NL, KC, 2], F32, "A2")
        AFN, r_AFN = psb([128, KC, 2], F32, "AFN")
        ZB, r_ZB = psb([128, KC, 2], F32, "ZB")
        bmod, r_bmod = psb([128, NL, 96], F32, "bmod")
        gmix, r_gmix = psb([128, NL, KC], F32, "gmix")
        gffn, r_gffn = psb([128, NL, KC], F32, "gffn")
        gfin, r_gfin = psb([128, KC], F32, "gfin")
        fcw, r_fcw = psb([128, NL, 44, 9], F32, "fcw")
        fcb, r_fcb = psb([128, NL, 44], F32, "fcb")
        scw, r_scw = psb([128, 2, 48, 3], F32, "scw")
        scb, r_scb = psb([128, 2, 48], F32, "scb")
        cvt, r_cvt = psb([128, KC, 2], F32, "cvt")

        V = nc.vector
        A = nc.scalar
        PE = nc.tensor
        r_XT = [Res(f"XT{m}") for m in range(KC)]

        S.op("dve", lambda: V.memset(ones_bf[:], 1.0), writes=[r_ones])
        S.op("dve", lambda: V.memset(ones_f[:], 1.0), writes=[r_onesf])
        S.op("dve", lambda: V.memset(ZB[:], 0.0), writes=[r_ZB])
        for dst, rr, src in ((identb, r_identb, identb_in), (cf, r_cf, cf_in), (bmod, r_bmod, bmod_in),
                             (gmix, r_gmix, gmix_in), (gffn, r_gffn, gffn_in), (gfin, r_gfin, gfin_in),
                             (fcw, r_fcw, fcw_in), (fcb, r_fcb, fcb_in), (scw, r_scw, scw_in), (scb, r_scb, scb_in),
                             (cvt, r_cvt, cv_in)):
            S.dma("sp", dst[:], src, writes=[rr])
        S.op("dve", lambda: V.tensor_copy(out=mkb[:], in_=cf[:, 2:4, :]), reads=[r_cf], writes=[r_mkb])
        S.op("act", lambda: A.activation(out=sT[:], in_=cvt[:], func=AF.Silu), reads=[r_cvt], writes=[r_sT])
        for m in range(KC):
            S.dma("sp", XT[m * 128:(m + 1) * 128, :], xc_in[m * 128:(m + 1) * 128, :], writes=[r_XT[m]])

        def stage_mod():
            with Phase(S) as ph:
                wb = [ph.sb([128, KC, 512], BF16, "modw") for _ in range(3)]
                for l in range(NL):
                    bk, rbk = S.bank()
                    for r4 in range(24):
                        w, rw = wb[(l * 24 + r4) % 3]
                        S.dma("pool", w[:], w_mod[l, :, r4 * 512:(r4 + 1) * 512].rearrange("(kc p) c -> p kc c", p=128), writes=[rw])
                        for rr in range(4):
                            r = r4 * 4 + rr
                            for kc in range(KC):
                                S.op("pe", lambda: PE.matmul(bk[:, 2 * r:2 * r + 2], w[:, kc, rr * 128:(rr + 1) * 128], sT[:, kc, :],
                                                             start=(kc == 0), stop=(kc == KC - 1)),
                                     reads=[rw, r_sT], writes=[rbk], inc=(kc == KC - 1))
                    for j in range(2):
                        S.op("dve", lambda: V.tensor_tensor(out=MOD[:, l, :, j], in0=bk[:, 0:192].rearrange("p (r j) -> p r j", j=2)[:, :, j],
                                                            in1=bmod[:, l, :], op=ALU.add),
                             reads=[rbk, r_bmod], writes=[r_MOD])
                    sq = math.sqrt(float(D))
                    for j in range(2):
                        for (AA, rA, gg, rg, off) in ((A1, r_A1, gmix, r_gmix, 16), (A2, r_A2, gffn, r_gffn, 64)):
                            S.op("dve", lambda: V.tensor_scalar(out=AA[:, l, :, j], in0=MOD[:, l, off:off + 16, j], scalar1=1.0, scalar2=sq,
                                                                op0=ALU.add, op1=ALU.mult), reads=[r_MOD], writes=[rA])
                            S.op("dve", lambda: V.tensor_tensor(out=AA[:, l, :, j], in0=AA[:, l, :, j], in1=gg[:, l, :], op=ALU.mult),
                                 reads=[rA, rg], writes=[rA])
                for j in range(2):
                    S.op("dve", lambda: V.tensor_scalar(out=AFN[:, :, j], in0=gfin[:, :], scalar1=math.sqrt(float(D)), scalar2=None, op0=ALU.mult),
                         reads=[r_gfin], writes=[r_AFN])

        def norm_phase(aT, r_aT, scale_ap, bias_ap, rs, out_f32=None):
            with Phase(S) as ph:
                xb = [ph.sb([128, T], F32, "nx") for _ in range(3)]
                sqb = [ph.sb([128, T], BF16, "nsq") for _ in range(2)]
                rstd, r_rstd = ph.sb([128, T], F32, "rstd")
                ob = [ph.sb([128, T], F32, "nout") for _ in range(2)] if out_f32 is not None else None
                bks = [S.bank() for _ in range(5)]
                for c in range(KC):
                    x, rx = xb[c % 3]
                    s, rsq = sqb[c % 2]
                    S.dma("sp", x[:], XT[c * 128:(c + 1) * 128, :], reads=[r_XT[c]], writes=[rx])
                    S.op("act", lambda: A.activation(out=s[:], in_=x[:], func=AF.Square), reads=[rx], writes=[rsq])
                    for ti, (t0, n) in enumerate(TT):
                        bk, rbk = bks[ti]
                        S.op("pe", lambda: PE.matmul(bk[:, :n], ones_bf[:], s[:, t0:t0 + n], start=(c == 0), stop=(c == KC - 1)),
                             reads=[rsq, r_ones], writes=[rbk], inc=(ti == 4))
                for ti, (t0, n) in enumerate(TT):
                    bk, rbk = bks[ti]
                    S.op("dve", lambda: V.tensor_scalar(out=rstd[:, t0:t0 + n], in0=bk[:, :n], scalar1=float(D) * EPS, scalar2=None,
                                                        op0=ALU.add), reads=[rbk], writes=[r_rstd])
                    S.op("act", lambda: A.activation(out=rstd[:, t0:t0 + n], in_=rstd[:, t0:t0 + n], func=AF.Sqrt), reads=[r_rstd], writes=[r_rstd])
                    S.op("dve", lambda: V.reciprocal(out=rstd[:, t0:t0 + n], in_=rstd[:, t0:t0 + n]), reads=[r_rstd], writes=[r_rstd])
                for c in range(KC):
                    x, rx = xb[(KC + c) % 3]
                    S.dma("sp", x[:], XT[c * 128:(c + 1) * 128, :], reads=[r_XT[c]], writes=[rx])
                    S.op("dve", lambda: V.tensor_tensor(out=x[:], in0=x[:], in1=rstd[:], op=ALU.mult), reads=[rx, r_rstd], writes=[rx])
                    if out_f32 is None:
                        for j, (a, b) in enumerate(((0, L), (L, T))):
                            S.op("act", lambda: A.activation(out=aT[:, c, a:b], in_=x[:, a:b], func=AF.Identity,
                                                             bias=bias_ap(c, j), scale=scale_ap(c, j)),
                                 reads=[rx] + rs, writes=[r_aT[c]])
                    else:
                        o, ro = ob[c % 2]
                        S.op("act", lambda: A.activation(out=o[:], in_=x[:], func=AF.Identity, bias=bias_ap(c, 0), scale=scale_ap(c, 0)),
                             reads=[rx] + rs, writes=[ro])
                        S.dma("sp", out_f32[c * 128:(c + 1) * 128, :], o[:], reads=[ro])

        def linear(ph, Wsrc, cols, kcn, inT, r_in, epilogue, wtag="lw", nbuf=3, wfix=None):
            wb = [ph.sb([128, kcn, 128], BF16, wtag) for _ in range(nbuf)]
            for mi, col in enumerate(cols):
                w, rw = wb[mi % nbuf]
                S.dma("pool", w[:], Wsrc(col).rearrange("(kc p) c -> p kc c", p=128), writes=[rw])
                if wfix is not None:
                    wfix(w, rw)
                bl = []
                for ti, (t0, n) in enumerate(TT):
                    bk, rbk = S.bank()
                    for kc in range(kcn):
                        S.op("pe", lambda: PE.matmul(bk[:, :n], w[:, kc, :], inT[:, kc, t0:t0 + n], start=(kc == 0), stop=(kc == kcn - 1)),
                             reads=[rw, r_in[kc] if isinstance(r_in, list) else r_in], writes=[rbk], inc=(kc == kcn - 1))
                    bl.append((bk, rbk))
                epilogue(mi, col, bl)

        def resid_epilogue(ph, gate_ap):
            xt = [ph.sb([128, T], F32, "rxt") for _ in range(2)]

            def ep(mi, col, bl):
                m = col // 128
                x, rx = xt[mi % 2]
                S.dma("sp", x[:], XT[m * 128:(m + 1) * 128, :], reads=[r_XT[m]], writes=[rx])
                for ti, (t0, n) in enumerate(TT):
                    bk, rbk = bl[ti]
                    j = 0 if t0 < L else 1
                    S.op("dve", lambda: V.scalar_tensor_tensor(out=x[:, t0:t0 + n], in0=bk[:, :n], scalar=gate_ap(m, j), in1=x[:, t0:t0 + n],
                                                               op0=ALU.mult, op1=ALU.add), reads=[rbk, rx, r_MOD], writes=[rx])
                S.dma("sp", XT[m * 128:(m + 1) * 128, :], x[:], reads=[rx], writes=[r_XT[m]])
            return ep

        def stage_ffn(l):
            with Phase(S) as pho:
                aT, _ = pho.sb([128, KC, T], BF16, "aT")
                r_aT = [Res() for _ in range(KC)]
                norm_phase(aT, r_aT, lambda c, j: A2[:, l, c, j:j + 1], lambda c, j: MOD[:, l, 48 + c, j:j + 1], [r_A2, r_MOD])
                hT, _ = pho.sb([128, GS, T], BF16, "hT")
                groups = [list(range(g0, min(g0 + GS, 44))) for g0 in range(0, 44, GS)]
                for grp in groups:
                    r_h = [Res() for _ in grp]
                    with Phase(S) as ph:
                        gp = [ph.sb([128, GPW], F32, "gp") for _ in range(2)]
                        acc = [ph.sb([128, T], F32, "acc") for _ in range(2)]
                        vb = [ph.sb([128, T], F32, "vb") for _ in range(2)]
                        wg = [ph.sb([128, KC, 128], BF16, "wg") for _ in range(2)]
                        wv = [ph.sb([128, KC, 128], BF16, "wv") for _ in range(2)]
                        for g_, rg_ in gp:
                            S.op("dve", lambda: V.memset(g_[:], 0.0), writes=[rg_])
                        def u_vars(ji):
                            return gp[ji % 2], acc[ji % 2], vb[ji % 2], wg[ji % 2], wv[ji % 2]

                        def u_pe(ji, j):
                            (g_, rg_), (ac, rac), (vv, rvv), (wgt, rwg), (wvt, rwv) = u_vars(ji)
                            S.dma("pool", wgt[:], w_up[l, :, j * 128:(j + 1) * 128].rearrange("(kc p) c -> p kc c", p=128), writes=[rwg])
                            S.dma("pool", wvt[:], w_up[l, :, DFF + j * 128:DFF + (j + 1) * 128].rearrange("(kc p) c -> p kc c", p=128), writes=[rwv])
                            glat = g_[:, 0:GPL].rearrange("p (r c) -> p r c", c=66)
                            for ti, (t0, n) in enumerate(TT):
                                bg, rbg = S.bank()
                                for kc in range(KC):
                                    S.op("pe", lambda: PE.matmul(bg[:, :n], wgt[:, kc, :], aT[:, kc, t0:t0 + n], start=(kc == 0), stop=(kc == KC - 1)),
                                         reads=[rwg, r_aT[kc]], writes=[rbg], inc=(kc == KC - 1))
                                if t0 < L:
                                    S.op("act", lambda: A.activation(out=glat[:, 1 + 8 * ti:9 + 8 * ti, 1:65],
                                                                     in_=bg[:, :512].rearrange("p (r c) -> p r c", c=64), func=AF.Copy),
                                         reads=[rbg], writes=[rg_])
                                else:
                                    S.op("act", lambda: A.activation(out=g_[:, GPL + 1:GPL + 257], in_=bg[:, :256], func=AF.Copy),
                                         reads=[rbg], writes=[rg_])
                                bv, rbv = S.bank()
                                for kc in range(KC):
                                    S.op("pe", lambda: PE.matmul(bv[:, :n], wvt[:, kc, :], aT[:, kc, t0:t0 + n], start=(kc == 0), stop=(kc == KC - 1)),
                                         reads=[rwv, r_aT[kc]], writes=[rbv], inc=(kc == KC - 1))
                                S.op("act", lambda: A.activation(out=vv[:, t0:t0 + n], in_=bv[:, :n], func=AF.Copy), reads=[rbv], writes=[rvv])

                        def u_post(ji, j):
                            (g_, rg_), (ac, rac), (vv, rvv), (wgt, rwg), (wvt, rwv) = u_vars(ji)
                            glat = g_[:, 0:GPL].rearrange("p (r c) -> p r c", c=66)
                            alat = ac[:, 0:L].rearrange("p (r c) -> p r c", c=64)
                            first = True
                            for di in range(3):
                                for dj in range(3):
                                    wsc = fcw[:, l, j, di * 3 + dj:di * 3 + dj + 1]
                                    src = glat[:, di:di + 32, dj:dj + 64]
                                    if first:
                                        S.op("dve", lambda: V.tensor_scalar(out=alat, in0=src, scalar1=wsc, scalar2=fcb[:, l, j:j + 1],
                                                                            op0=ALU.mult, op1=ALU.add), reads=[rg_, r_fcw, r_fcb], writes=[rac])
                                        first = False
                                    else:
                                        S.op("dve", lambda: V.scalar_tensor_tensor(out=alat, in0=src, scalar=wsc, in1=alat, op0=ALU.mult, op1=ALU.add),
                                             reads=[rg_, rac, r_fcw], writes=[rac])
                            for dj in range(3):
                                wsc = fcw[:, l, j, 3 + dj:3 + dj + 1]
                                src = g_[:, GPL + dj:GPL + dj + 256]
                                if dj == 0:
                                    S.op("dve", lambda: V.tensor_scalar(out=ac[:, L:T], in0=src, scalar1=wsc, scalar2=fcb[:, l, j:j + 1],
                                                                        op0=ALU.mult, op1=ALU.add), reads=[rg_, r_fcw, r_fcb], writes=[rac])
                                else:
                                    S.op("dve", lambda: V.scalar_tensor_tensor(out=ac[:, L:T], in0=src, scalar=wsc, in1=ac[:, L:T], op0=ALU.mult, op1=ALU.add),
                                         reads=[rg_, rac, r_fcw], writes=[rac])
                            S.op("act", lambda: A.activation(out=ac[:], in_=ac[:], func=AF.Silu), reads=[rac], writes=[rac])
                            S.op("dve", lambda: V.tensor_tensor(out=hT[:, ji, :], in0=ac[:], in1=vv[:], op=ALU.mult), reads=[rac, rvv], writes=[r_h[ji]])

                        for ji, j in enumerate(grp):
                            u_pe(ji, j)
                            if ji > 0:
                                u_post(ji - 1, grp[ji - 1])
                        u_post(len(grp) - 1, grp[-1])
                    with Phase(S) as ph:
                        linear(ph, lambda col: w_down[l, grp[0] * 128:(grp[-1] + 1) * 128, col:col + 128], [m * 128 for m in range(KC)], len(grp),
                               hT, r_h, resid_epilogue(ph, lambda m, j: MOD[:, l, 80 + m, j:j + 1]), wtag="wd")

        def stage_fourier(l):
            jf = l // 2
            with Phase(S) as pho:
                fT, _ = pho.sb([128, KC, T], BF16, "fT")
                r_fT = [Res() for _ in range(KC)]
                with Phase(S) as ph1:
                    aT, _ = ph1.sb([128, KC, T], BF16, "aT")
                    r_aT = [Res() for _ in range(KC)]
                    norm_phase(aT, r_aT, lambda c, j: A1[:, l, c, j:j + 1], lambda c, j: MOD[:, l, c, j:j + 1], [r_A1, r_MOD])
                    cs1, r_cs1 = ph1.sb([128, 2, 512], BF16, "cs1")
                    S.dma("sp", cs1[:], cs1_in, writes=[r_cs1])
                    st = [ph1.sb([128, 8, 512], BF16, "uvst") for _ in range(2)]
                    for tt in range(NCH):
                        s_, rs_ = st[tt % 2]
                        for g in range(8):
                            bk, rbk = S.bank()
                            for kc in range(2):
                                S.op("pe", lambda: PE.matmul(bk[:, :], aT[:, 2 * g + kc, tt * 128:(tt + 1) * 128], cs1[:, kc, :], start=(kc == 0), stop=(kc == 1)),
                                     reads=[r_aT[2 * g + kc], r_cs1], writes=[rbk], inc=(kc == 1))
                            if g % 2 == 0:
                                S.op("act", lambda: A.activation(out=s_[:, g, :], in_=bk[:, :], func=AF.Copy), reads=[rbk], writes=[rs_])
                            else:
                                S.op("dve", lambda: V.tensor_copy(out=s_[:, g, :], in_=bk[:, :]), reads=[rbk], writes=[rs_])
                        S.dma("sp", UVd[tt * 128:(tt + 1) * 128, :, :], s_[:], reads=[rs_], writes=[r_UV])
                with Phase(S) as ph2:
                    tb = [(ph2.sb([128, KC, 512], BF16, "cl"), ph2.sb([128, KC, 512], BF16, "nsl")) for _ in range(2)]
                    uv = [ph2.sb([128, KC, 512], BF16, "uvg") for _ in range(2)]
                    cnt = 0
                    for tp in range(4):
                        (clt, rcl), (nst, rns) = tb[tp % 2]
                        S.dma("sp", clt[:], cl_in[:, tp * 512:(tp + 1) * 512].rearrange("(tt p) c -> p tt c", p=128), writes=[rcl])
                        S.dma("sp", nst[:], nsl_in[:, tp * 512:(tp + 1) * 512].rearrange("(tt p) c -> p tt c", p=128), writes=[rns])
                        for g in range(8):
                            u_, ru_ = uv[cnt % 2]
                            cnt += 1
                            S.dma("sp", u_[:], UVd[0:L, g, :].rearrange("(tt p) c -> p tt c", p=128), reads=[r_UV], writes=[ru_])
                            for hh in range(2):
                                bk, rbk = S.bank()
                                for tt in range(KC):
                                    S.op("pe", lambda: PE.matmul(bk[:, :], u_[:, tt, hh * 128:(hh + 1) * 128], clt[:, tt, :], start=(tt == 0), stop=False),
                                         reads=[ru_, rcl], writes=[rbk], inc=False)
                                    S.op("pe", lambda: PE.matmul(bk[:, :], u_[:, tt, 256 + hh * 128:256 + (hh + 1) * 128], nst[:, tt, :], start=False, stop=(tt == KC - 1)),
                                         reads=[ru_, rns], writes=[rbk], inc=(tt == KC - 1))
                                cp = 2 * g + hh
                                if hh == 0:
                                    S.op("act", lambda: A.activation(out=fT[:, cp, tp * 512:(tp + 1) * 512], in_=bk[:, :], func=AF.Copy), reads=[rbk], writes=[r_fT[cp]])
                                else:
                                    S.op("dve", lambda: V.tensor_copy(out=fT[:, cp, tp * 512:(tp + 1) * 512], in_=bk[:, :]), reads=[rbk], writes=[r_fT[cp]])
                    c2, r_c2 = ph2.sb([128, 2, 512], BF16, "cs2c")
                    S.dma("sp", c2[:], cs2c_in, writes=[r_c2])
                    uc, r_uc = ph2.sb([128, 2, 8, 512], BF16, "uvc")
                    for tt in range(2):
                        S.dma("sp", uc[:, tt, :, :], UVd[L + tt * 128:L + (tt + 1) * 128, :, :], reads=[r_UV], writes=[r_uc])
                    for cp in range(KC):
                        g, hh = cp // 2, cp % 2
                        bk, rbk = S.bank()
                        for tt in range(2):
                            S.op("pe", lambda: PE.matmul(bk[:, :256], uc[:, tt, g, hh * 128:(hh + 1) * 128], c2[:, tt, 0:256], start=(tt == 0), stop=False),
                                 reads=[r_uc, r_c2], writes=[rbk], inc=False)
                            S.op("pe", lambda: PE.matmul(bk[:, :256], uc[:, tt, g, 256 + hh * 128:256 + (hh + 1) * 128], c2[:, tt, 256:512], start=False, stop=(tt == 1)),
                                 reads=[r_uc, r_c2], writes=[rbk], inc=(tt == 1))
                        S.op("act", lambda: A.activation(out=fT[:, cp, L:T], in_=bk[:, :256], func=AF.Copy), reads=[rbk], writes=[r_fT[cp]])
                with Phase(S) as ph3:
                    linear(ph3, lambda col: four_w[jf, :, col:col + 128], [m * 128 for m in range(KC)], KC, fT, r_fT,
                           resid_epilogue(ph3, lambda m, j: MOD[:, l, 32 + m, j:j + 1]), wtag="fw")


        def stage_ssd(l):
            js = l // 2
            ZC0 = STATE_COLS + GN
            cols_xbc = [i * 128 for i in range(32)] + [DIN + i * 128 for i in range(8)] + [STATE_COLS + i * 128 for i in range(8)]
            r_XBC = [Res() for _ in range(48)]
            r_Z = [Res() for _ in range(NCH)]
            r_XS = [Res() for _ in range(NCH)]
            r_BTM = [Res() for _ in range(NCH)]
            r_HF = [Res() for _ in range(NCH)]
            r_GT = [Res() for _ in range(NCH)]

            def bc(ap, n0, n1):
                return ap.unsqueeze(2).to_broadcast([128, n0, n1])

            def v3(ap, n1=64):
                return ap.rearrange("s (h p) -> s h p", p=n1)

            with Phase(S) as pho:
                dt_tm, r_dt = pho.sb([128, NCH, 128], F32, "dt_tm")
                with Phase(S) as ph1:
                    aT, _ = ph1.sb([128, KC, T], BF16, "aT")
                    r_aT = [Res() for _ in range(KC)]
                    norm_phase(aT, r_aT, lambda c, j: A1[:, l, c, j:j + 1], lambda c, j: MOD[:, l, c, j:j + 1], [r_A1, r_MOD])
                    with Phase(S) as ph:
                        P1 = [ph.sb([128, 2308], F32, "p1") for _ in range(2)]
                        acc = [ph.sb([128, T], F32, "cacc") for _ in range(2)]
                        stg = [ph.sb([128, T], BF16, "cst") for _ in range(2)]
                        for p_, rp_ in P1:
                            S.op("dve", lambda: V.memset(p_[:], 0.0), writes=[rp_])

                        def ep_conv(mi, col, bl):
                            p_, rp_ = P1[mi % 2]
                            ac, rac = acc[mi % 2]
                            st_, rst = stg[mi % 2]
                            for ti, (t0, n) in enumerate(TT):
                                bk, rbk = bl[ti]
                                dst = p_[:, 1 + t0:1 + t0 + n] if t0 < L else p_[:, 2051:2307]
                                S.op("act", lambda: A.activation(out=dst, in_=bk[:, :n], func=AF.Copy), reads=[rbk], writes=[rp_])
                            for (o0, o1, base) in ((0, L, 0), (L, T, 2050)):
                                wd_ = o1 - o0
                                S.op("dve", lambda: V.tensor_scalar(out=ac[:, o0:o1], in0=p_[:, base:base + wd_], scalar1=scw[:, js, mi, 0:1],
                                                                    scalar2=scb[:, js, mi:mi + 1], op0=ALU.mult, op1=ALU.add),
                                     reads=[rp_, r_scw, r_scb], writes=[rac])
                                for k in (1, 2):
                                    S.op("dve", lambda: V.scalar_tensor_tensor(out=ac[:, o0:o1], in0=p_[:, base + k:base + k + wd_], scalar=scw[:, js, mi, k:k + 1],
                                                                               in1=ac[:, o0:o1], op0=ALU.mult, op1=ALU.add),
                                         reads=[rp_, rac, r_scw], writes=[rac])
                            S.op("act", lambda: A.activation(out=st_[:], in_=ac[:], func=AF.Silu), reads=[rac], writes=[rst])
                            S.dma("sp", XBC[mi * 128:(mi + 1) * 128, :], st_[:], reads=[rst], writes=[r_XBC[mi]])

                        linear(ph, lambda col: w_in[js, :, col:col + 128], cols_xbc, KC, aT, r_aT, ep_conv, wtag="win")
                    if ssd_upto < 2:
                        raise _Stop()
                    with Phase(S) as ph:
                        wz = [ph.sb([128, KC, 512], BF16, "wz") for _ in range(2)]
                        zst = [ph.sb([128, 512], BF16, "zst") for _ in range(4)]
                        k = 0
                        for ct in range(8):
                            w, rw = wz[ct % 2]
                            S.dma("pool", w[:], w_in[js, :, ZC0 + ct * 512:ZC0 + (ct + 1) * 512].rearrange("(kc p) c -> p kc c", p=128), writes=[rw])
                            for tt in range(NCH):
                                bk, rbk = S.bank()
                                for kc in range(KC):
                                    S.op("pe", lambda: PE.matmul(bk[:, :], aT[:, kc, tt * 128:(tt + 1) * 128], w[:, kc, :], start=(kc == 0), stop=(kc == KC - 1)),
                                         reads=[rw, r_aT[kc]], writes=[rbk], inc=(kc == KC - 1))
                                z_, rz_ = zst[k % 4]
                                k += 1
                                S.op("act", lambda: A.activation(out=z_[:], in_=bk[:, :], func=AF.Silu), reads=[rbk], writes=[rz_])
                                S.dma("sp", Zd[tt * 128:(tt + 1) * 128, ct * 512:(ct + 1) * 512], z_[:], reads=[rz_], writes=[r_Z[tt]])
                        wdt, rwdt = ph.sb([128, KC, 128], BF16, "wdt")
                        dtb, rdtb = ph.sb([128, 128], F32, "dtb")
                        S.dma("sp", dtb[:], dtb_in[:, js, :], writes=[rdtb])
                        S.dma("pool", wdt[:], w_in[js, :, DIN + GN:DIN + GN + 128].rearrange("(kc p) c -> p kc c", p=128), writes=[rwdt])
                        tmp = [ph.sb([128, 128], F32, "dtt") for _ in range(2)]
                        for tt in range(NCH):
                            bk, rbk = S.bank()
                            for kc in range(KC):
                                S.op("pe", lambda: PE.matmul(bk[:, 0:128], aT[:, kc, tt * 128:(tt + 1) * 128], wdt[:, kc, :], start=(kc == 0), stop=(kc == KC - 1)),
                                     reads=[rwdt, r_aT[kc]], writes=[rbk], inc=(kc == KC - 1))
                            t_, rt_ = tmp[tt % 2]
                            S.op("dve", lambda: V.tensor_tensor(out=t_[:], in0=bk[:, 0:128], in1=dtb[:], op=ALU.add), reads=[rbk, rdtb], writes=[rt_])
                            S.op("act", lambda: A.activation(out=t_[:], in_=t_[:], func=AF.Exp), reads=[rt_], writes=[rt_])
                            S.op("act", lambda: A.activation(out=dt_tm[:, tt, :], in_=t_[:], func=AF.Ln, bias=ones_f[:, 0:1]), reads=[rt_, r_onesf], writes=[r_dt])

                if ssd_upto < 3:
                    raise _Stop()
                with Phase(S) as pd:
                    nacsl, r_nacsl = pd.sb([128, NCH, 128], F32, "nacsl")
                    ea, r_ea = pd.sb([128, NCH, 128], F32, "ea")
                    dtw, r_dtw = pd.sb([128, NCH, 128], F32, "dtw")
                    cd, r_cd = pd.sb([128, NCH, 128], F32, "cd")
                    acs2 = [pd.sb([128, T], BF16, "acs2") for _ in range(2)]
                    sel2, r_sel2 = pd.sb([128, 64], BF16, "sel2")
                    S.op("dve", lambda: V.tensor_copy(out=sel2[0:64, :], in_=identb[0:64, 0:64]), reads=[r_identb], writes=[r_sel2])
                    S.op("dve", lambda: V.tensor_copy(out=sel2[64:128, :], in_=identb[64:128, 64:128]), reads=[r_identb], writes=[r_sel2])
                    epsc, r_epsc = pd.sb([128, 1], F32, "epsc")
                    S.op("dve", lambda: V.memset(epsc[:], EPS), writes=[r_epsc])
                    abc, r_abc = pd.sb([128, 128], F32, "abc")
                    dsk, r_dsk = pd.sb([128, 128], F32, "dsk")
                    dsum, r_dsum = pd.sb([128, 64], F32, "dsum")
                    identf, r_identf = pd.sb([128, 128], F32, "identf")
                    S.op("dve", lambda: V.tensor_tensor(out=identf[:], in0=cf[:, 0, :], in1=cf[:, 1, :], op=ALU.mult), reads=[r_cf], writes=[r_identf])
                    S.dma("sp", abc[:], alog_in[:, js, :], writes=[r_abc])
                    S.dma("sp", dsk[:], dsk_in[:, js, :], writes=[r_dsk])
                    S.op("act", lambda: A.activation(out=abc[:], in_=abc[:], func=AF.Exp), reads=[r_abc], writes=[r_abc])
                    S.op("dve", lambda: V.tensor_scalar(out=abc[:], in0=abc[:], scalar1=-1.0, scalar2=None, op0=ALU.mult), reads=[r_abc], writes=[r_abc])
                    S.op("dve", lambda: V.tensor_tensor(out=dsum[:], in0=dsk[:, 0:64], in1=dsk[:, 64:128], op=ALU.add), reads=[r_dsk], writes=[r_dsum])
                    with Phase(S) as pp:
                        dta, r_dta = pp.sb([128, NCH, 128], F32, "dta")
                        nacs, r_nacs = pp.sb([128, NCH, 128], F32, "nacs")
                        lnd = [pp.sb([128, 128], F32, "lnd") for _ in range(2)]
                        acsT = [pp.sb([128, T], F32, "acsT") for _ in range(2)]
                        r1_, r_r1 = pp.sb([128, T], F32, "r1")
                        dup = [pp.sb([128, 2, 128], F32, "dup") for _ in range(2)]
                        tw = [pp.sb([128, 128], F32, "tw") for _ in range(2)]
                        for c in range(NCH):
                            S.op("dve", lambda: V.tensor_tensor(out=dta[:, c, :], in0=dt_tm[:, c, :], in1=abc[:], op=ALU.mult), reads=[r_dt, r_abc], writes=[r_dta])
                            (bA, rbA), (bB, rbB), (bC, rbC), (bD, rbD) = S.bank(), S.bank(), S.bank(), S.bank()
                            du, rdu = dup[c % 2]
                            for d in range(2):
                                S.op("dve", lambda: V.tensor_copy(out=du[:, d, :].rearrange("s (r h) -> s r h", r=2),
                                                                  in_=dta[:, c, d * 64:(d + 1) * 64].unsqueeze(1).to_broadcast([128, 2, 64])),
                                     reads=[r_dta], writes=[rdu])
                            S.op("pe", lambda: PE.matmul(bA[:, 0:64], cf[:, 0, :], dta[:, c, 0:64], start=True, stop=True), reads=[r_cf, r_dta], writes=[rbA], inc=False)
                            S.op("pe", lambda: PE.matmul(bA[:, 64:128], cf[:, 1, :], dta[:, c, 64:128], start=True, stop=True), reads=[r_cf, r_dta], writes=[rbA], inc=False)
                            S.op("pe", lambda: PE.matmul(bB[:, 0:128], ones_f[:], dta[:, c, :], start=True, stop=True), reads=[r_onesf, r_dta], writes=[rbB], inc=False)
                            S.op("pe", lambda: PE.matmul(bC[:, 0:128], du[:, 0, :], cf[:, 0, :], start=True, stop=True), reads=[r_cf, rdu], writes=[rbC], inc=False)
                            S.op("pe", lambda: PE.matmul(bD[:, 0:128], du[:, 1, :], cf[:, 1, :], start=True, stop=True), reads=[r_cf, rdu], writes=[rbD], inc=True)
                            S.op("dve", lambda: V.tensor_scalar(out=nacs[:, c, :], in0=bA[:, 0:128], scalar1=-1.0, scalar2=None, op0=ALU.mult), reads=[rbA], writes=[r_nacs])
                            S.op("act", lambda: A.activation(out=ea[:, c, :], in_=bA[:, 0:128], func=AF.Exp), reads=[rbA], writes=[r_ea])
                            ld_, rld_ = lnd[c % 2]
                            S.op("act", lambda: A.activation(out=ld_[:], in_=dt_tm[:, c, :], func=AF.Ln), reads=[r_dt], writes=[rld_])
                            S.op("dve", lambda: V.tensor_tensor(out=nacsl[:, c, :], in0=nacs[:, c, :], in1=ld_[:], op=ALU.add), reads=[r_nacs, rld_], writes=[r_nacsl])
                            t_, rt_ = tw[c % 2]
                            S.op("dve", lambda: V.tensor_tensor(out=t_[:], in0=bB[:, 0:128], in1=nacs[:, c, :], op=ALU.add), reads=[rbB, r_nacs], writes=[rt_])
                            S.op("act", lambda: A.activation(out=t_[:], in_=t_[:], func=AF.Exp), reads=[rt_], writes=[rt_])
                            S.op("dve", lambda: V.tensor_tensor(out=dtw[:, c, :], in0=dt_tm[:, c, :], in1=t_[:], op=ALU.mult), reads=[r_dt, rt_], writes=[r_dtw])
                            S.op("act", lambda: A.activation(out=cd[:, c, :], in_=bB[:, 0:128], func=AF.Exp), reads=[rbB], writes=[r_cd])
                            S.op("act", lambda: A.activation(out=acsT[0][0][:, c * 128:(c + 1) * 128], in_=bC[:, 0:128], func=AF.Copy), reads=[rbC], writes=[acsT[0][1]])
                            S.op("dve", lambda: V.tensor_copy(out=acsT[1][0][:, c * 128:(c + 1) * 128], in_=bD[:, 0:128]), reads=[rbD], writes=[acsT[1][1]])

                        for d in range(2):
                            (a_, ra_), (a2, ra2) = acsT[d], acs2[d]
                            S.op("dve", lambda: V.tensor_copy(out=a2[:], in_=a_[:]), reads=[ra_], writes=[ra2])
                            S.op("dve", lambda: V.tensor_tensor(out=r1_[64:128, :], in0=a_[64:128, :], in1=a2[64:128, :], op=ALU.subtract), reads=[ra_, ra2], writes=[r_r1])
                            S.op("dve", lambda: V.tensor_copy(out=a2[64:128, :], in_=r1_[64:128, :]), reads=[r_r1], writes=[ra2])

                    if ssd_upto < 4:
                        raise _Stop()
                    with Phase(S) as pf:
                        Sf, r_Sf = pf.sb([128, DIN], F32, "Sf")
                        Sbf = [pf.sb([128, DIN], BF16, "Sbf") for _ in range(2)]
                        xsT, r_xsT = pf.sb([128, 32, 128], BF16, "xsT")
                        BT, r_BT = pf.sb([128, 8, 128], BF16, "BT")
                        xs_tm, _ = pf.sb([128, DIN], BF16, "xs_tm")
                        r_xs = [Res() for _ in range(8)]
                        B_tm, _ = pf.sb([128, GN], BF16, "B_tm")
                        r_Btm = [Res() for _ in range(2)]
                        xw = [pf.sb([128, 512], BF16, "xw") for _ in range(2)]
                        S.op("dve", lambda: V.memset(Sf[:], 0.0), writes=[r_Sf])
                        S.op("dve", lambda: V.memset(Sbf[0][0][:], 0.0), writes=[Sbf[0][1]])
                        order = [16, 17] + list(range(16))
                        for oi, c in enumerate(order):
                            c0, c1 = c * 128, (c + 1) * 128
                            S.dma("sp", xsT[:], XBC[0:DIN, c0:c1].rearrange("(fc p) t -> p fc t", p=128), reads=r_XBC[0:32], writes=[r_xsT])
                            S.dma("sp", BT[:], XBC[DIN:DIN + GN, c0:c1].rearrange("(fc p) t -> p fc t", p=128), reads=r_XBC[32:40], writes=[r_BT])
                            for fb in range(10):
                                bk, rbk = S.bank()
                                pb = bk[:].bitcast(BF16)
                                for i in range(4):
                                    src = xsT[:, fb * 4 + i, :] if fb < 8 else BT[:, (fb - 8) * 4 + i, :]
                                    S.op("pe", lambda: PE.transpose(out=pb[:, i * 128:(i + 1) * 128], in_=src, identity=identb[:]),
                                         reads=[r_xsT if fb < 8 else r_BT, r_identb], writes=[rbk], inc=(i == 3))
                                dst = xs_tm[:, fb * 512:(fb + 1) * 512] if fb < 8 else B_tm[:, (fb - 8) * 512:(fb - 7) * 512]
                                rd = r_xs[fb] if fb < 8 else r_Btm[fb - 8]
                                if fb % 2 == 0:
                                    S.op("act", lambda: A.activation(out=dst, in_=pb[:, 0:512], func=AF.Copy), reads=[rbk], writes=[rd])
                                else:
                                    S.op("dve", lambda: V.tensor_copy(out=dst, in_=pb[:, 0:512]), reads=[rbk], writes=[rd])
                            S.dma("sp", XSd[c0:c1, :], xs_tm[:], reads=r_xs, writes=[r_XS[c]])
                            S.dma("sp", BTMd[c0:c1, :], B_tm[:], reads=r_Btm, writes=[r_BTM[c]])
                            cur, rcur = Sbf[oi % 2]
                            nxt, rnxt = Sbf[(oi + 1) % 2]
                            S.dma("sp", HFd[c], cur[:], reads=[rcur], writes=[r_HF[c]])
                            for g in range(8):
                                gs = slice(g * 512, (g + 1) * 512)
                                x_, rx_ = xw[g % 2]
                                S.op("dve", lambda: V.tensor_tensor(out=v3(x_[:]), in0=v3(xs_tm[:, gs]), in1=bc(dtw[:, c, g * 8:(g + 1) * 8], 8, 64), op=ALU.mult),
                                     reads=[r_xs[g], r_dtw], writes=[rx_])
                                bk, rbk = S.bank()
                                S.op("pe", lambda: PE.matmul(bk[:, :], B_tm[:, g * 128:(g + 1) * 128], x_[:], start=True, stop=True),
                                     reads=[r_Btm[g // 4], rx_], writes=[rbk])
                                S.op("dve", lambda: V.tensor_tensor(out=v3(Sf[:, gs]), in0=v3(Sf[:, gs]), in1=bc(cd[:, c, g * 8:(g + 1) * 8], 8, 64), op=ALU.mult),
                                     reads=[r_Sf, r_cd], writes=[r_Sf])
                                S.op("dve", lambda: V.tensor_tensor(out=Sf[:, gs], in0=Sf[:, gs], in1=bk[:, :], op=ALU.add), reads=[r_Sf, rbk], writes=[r_Sf])
                                S.op("act", lambda: A.activation(out=nxt[:, gs], in_=Sf[:, gs], func=AF.Copy), reads=[r_Sf], writes=[rnxt])

                    if ssd_upto < 5:
                        raise _Stop()
                    with Phase(S) as pb_:
                        Sb_, r_Sb = pb_.sb([128, DIN], F32, "Sb")
                        Sbb, r_Sbb = pb_.sb([128, DIN], BF16, "Sbb")
                        xs2 = [pb_.sb([128, DIN], BF16, "xs_tm") for _ in range(2)]
                        Bt2 = [pb_.sb([128, GN], BF16, "B_tm") for _ in range(2)]
                        BT2 = [pb_.sb([128, 8, 128], BF16, "BT") for _ in range(2)]
                        CT2 = [pb_.sb([128, 8, 128], BF16, "CT") for _ in range(2)]
                        hf, r_hf = pb_.sb([128, DIN], BF16, "hf")
                        z_tm, r_z = pb_.sb([128, DIN], BF16, "z_tm")
                        scm, r_scm = pb_.sb([128, 2, 8, 128], BF16, "scm")
                        Db = [pb_.sb([128, 512], BF16, "Db") for _ in range(4)]
                        M4 = [pb_.sb([128, 512], BF16, "M4") for _ in range(4)]
                        xw = [pb_.sb([128, 512], BF16, "xw") for _ in range(2)]
                        t1b = [pb_.sb([128, 512], F32, "t1") for _ in range(2)]
                        t2b = [pb_.sb([128, 512], F32, "t2") for _ in range(2)]
                        gyb = [pb_.sb([128, 512], F32, "gy") for _ in range(2)]
                        t3b = [pb_.sb([128, 512], F32, "t3") for _ in range(2)]
                        ss, r_ss = pb_.sb([128, 8], F32, "ss")
                        g_tm, r_gtm = pb_.sb([128, DIN], BF16, "g_tm")
                        gTst, r_gTst = pb_.sb([128, 32, 128], BF16, "gTst")
                        r_Sbg = [Res() for _ in range(8)]
                        r_Sbbg = [Res() for _ in range(8)]
                        sq2 = [pb_.sb([128, 512], F32, "sq2") for _ in range(2)]
                        S.op("dve", lambda: V.memset(Sb_[:], 0.0), writes=r_Sbg)
                        S.op("dve", lambda: V.memset(Sbb[:], 0.0), writes=r_Sbbg)
                        (bOf, rbOf), (bOb, rbOb), (bSt, rbSt), (bY, rbY) = S.banks[4], S.banks[5], S.banks[6], S.banks[7]
                        order = [17, 16] + list(range(15, -1, -1))

                        def loads_a(ci):
                            c = order[ci]
                            c0, c1 = c * 128, (c + 1) * 128
                            S.dma("sp", BT2[ci % 2][0][:], XBC[DIN:DIN + GN, c0:c1].rearrange("(fc p) t -> p fc t", p=128), reads=r_XBC[32:40], writes=[BT2[ci % 2][1]])
                            S.dma("sp", CT2[ci % 2][0][:], XBC[DIN + GN:DIN + 2 * GN, c0:c1].rearrange("(fc p) t -> p fc t", p=128), reads=r_XBC[40:48], writes=[CT2[ci % 2][1]])
                            S.dma("sp", xs2[ci % 2][0][:], XSd[c0:c1, :], reads=[r_XS[c]], writes=[xs2[ci % 2][1]])
                            S.dma("sp", Bt2[ci % 2][0][:], BTMd[c0:c1, :], reads=[r_BTM[c]], writes=[Bt2[ci % 2][1]])

                        def load_hf(ci):
                            c = order[ci]
                            S.dma("sp", hf[:], HFd[c], reads=[r_HF[c]], writes=[r_hf])

                        def load_z(ci):
                            c = order[ci]
                            S.dma("sp", z_tm[:], Zd[c * 128:(c + 1) * 128, :], reads=[r_Z[c]], writes=[r_z])

                        loads_a(0)
                        load_hf(0)
                        load_z(0)
                        for ci, c in enumerate(order):
                            c0, c1 = c * 128, (c + 1) * 128
                            (xs_tm, r_xs), (B_tm, r_Btm), (BT, r_BT), (CT, r_CT) = xs2[ci % 2], Bt2[ci % 2], BT2[ci % 2], CT2[ci % 2]
                            if ci + 1 < len(order):
                                loads_a(ci + 1)
                            for half in range(2):
                                bkS, rbkS = S.banks[half]
                                for gi in range(4):
                                    g = half * 4 + gi
                                    S.op("pe", lambda: PE.matmul(bkS[:, gi * 128:(gi + 1) * 128], BT[:, g, :], CT[:, g, :], start=True, stop=True),
                                         reads=[r_BT, r_CT], writes=[rbkS], inc=(gi == 3))
                                for d in range(2):
                                    S.op("dve", lambda: V.tensor_tensor(out=scm[:, d, half * 4:(half + 1) * 4, :], in0=bkS[:, :].rearrange("s (g l) -> s g l", l=128),
                                                                        in1=mkb[:, d, :].unsqueeze(1).to_broadcast([128, 4, 128]), op=ALU.mult),
                                         reads=[rbkS, r_mkb], writes=[r_scm])

                            def RW(k):
                                g, pair = k // 4, k % 4
                                bk, rbk = S.banks[k % 4]
                                for hh in range(2):
                                    for d in range(2):
                                        h = g * 8 + pair * 2 + hh
                                        i = hh * 2 + d
                                        S.op("pe", lambda: PE.matmul(bk[:, i * 128:(i + 1) * 128], sel2[:, h:h + 1].to_broadcast([128, 128]),
                                                                     acs2[d][0][:, c0:c1], start=True, stop=True),
                                             reads=[r_sel2, acs2[d][1]], writes=[rbk], inc=(i == 3))

                            def SA(k):
                                g, pair = k // 4, k % 4
                                h0 = g * 8 + pair * 2
                                bk, rbk = S.banks[k % 4]
                                d_, rd_ = Db[k % 4]
                                for hh in range(2):
                                    for d in range(2):
                                        i = hh * 2 + d
                                        col = d * 64 + h0 + hh
                                        S.op("act", lambda: A.activation(out=d_[:, i * 128:(i + 1) * 128], in_=bk[:, i * 128:(i + 1) * 128], func=AF.Exp,
                                                                         bias=nacsl[:, c, col:col + 1], scale=1.0),
                                             reads=[rbk, r_nacsl], writes=[rd_])

                            def SC(k):
                                g, pair = k // 4, k % 4
                                d_, rd_ = Db[k % 4]
                                m_, rm_ = M4[k % 4]
                                for a in range(2):
                                    S.op("dve", lambda: V.scalar_tensor_tensor(out=m_[:, a * 256:(a + 1) * 256].rearrange("s (d l) -> s d l", d=2),
                                                                               in0=d_[:, a * 256:(a + 1) * 256].rearrange("s (d l) -> s d l", d=2), scalar=1.0e4,
                                                                               in1=scm[:, :, g, :], op0=ALU.min, op1=ALU.mult),
                                         reads=[rd_, r_scm], writes=[rm_])
                                for hh in range(2):
                                    r = pair * 2 + hh
                                    h = g * 8 + r
                                    xh = xs_tm[:, h * 64:(h + 1) * 64]
                                    for d in range(2):
                                        i = hh * 2 + d
                                        S.op("pe", lambda: PE.matmul(bY[:, r * 64:(r + 1) * 64], m_[:, i * 128:(i + 1) * 128], xh, start=(d == 0), stop=(d == 1)),
                                             reads=[rm_, r_xs], writes=[rbY], inc=(d == 1))

                            def grp_start(g):
                                gs = slice(g * 512, (g + 1) * 512)
                                S.op("pe", lambda: PE.matmul(bOf[:, :], CT[:, g, :], hf[:, gs], start=True, stop=True), reads=[r_CT, r_hf], writes=[rbOf])
                                S.op("pe", lambda: PE.matmul(bOb[:, :], CT[:, g, :], Sbb[:, gs], start=True, stop=True), reads=[r_CT, r_Sbbg[g]], writes=[rbOb])
                                x_, rx_ = xw[g % 2]
                                S.op("pool", lambda: nc.gpsimd.tensor_tensor(out=v3(x_[:]), in0=v3(xs_tm[:, gs]),
                                                                             in1=bc(dtw[:, c, 64 + g * 8:64 + (g + 1) * 8], 8, 64), op=ALU.mult),
                                     reads=[r_xs, r_dtw], writes=[rx_])
                                S.op("pe", lambda: PE.matmul(bSt[:, :], B_tm[:, g * 128:(g + 1) * 128], x_[:], start=True, stop=True),
                                     reads=[r_Btm, rx_], writes=[rbSt])
                                t3, rt3 = t3b[g % 2]
                                S.op("pool", lambda: nc.gpsimd.tensor_tensor(out=v3(t3[:]), in0=v3(xs_tm[:, gs]), in1=bc(dsum[:, g * 8:(g + 1) * 8], 8, 64), op=ALU.mult),
                                     reads=[r_xs, r_dsum], writes=[rt3])

                            def grp_early(g):
                                gs = slice(g * 512, (g + 1) * 512)
                                (t1, rt1), (t2, rt2), (t3, rt3) = t1b[g % 2], t2b[g % 2], t3b[g % 2]
                                S.op("dve", lambda: V.tensor_tensor(out=v3(t1[:]), in0=v3(bOf[:, :]), in1=bc(ea[:, c, g * 8:(g + 1) * 8], 8, 64), op=ALU.mult),
                                     reads=[rbOf, r_ea], writes=[rt1])
                                S.op("dve", lambda: V.tensor_tensor(out=v3(t2[:]), in0=v3(bOb[:, :]), in1=bc(ea[:, c, 64 + g * 8:64 + (g + 1) * 8], 8, 64), op=ALU.mult),
                                     reads=[rbOb, r_ea], writes=[rt2])
                                S.op("pool", lambda: nc.gpsimd.tensor_tensor(out=t2[:], in0=t2[:], in1=t3[:], op=ALU.add), reads=[rt2, rt3], writes=[rt2])
                                S.op("dve", lambda: V.tensor_tensor(out=v3(Sb_[:, gs]), in0=v3(Sb_[:, gs]), in1=bc(cd[:, c, 64 + g * 8:64 + (g + 1) * 8], 8, 64), op=ALU.mult),
                                     reads=[r_Sbg[g], r_cd], writes=[r_Sbg[g]])
                                S.op("dve", lambda: V.tensor_tensor(out=Sb_[:, gs], in0=Sb_[:, gs], in1=bSt[:, :], op=ALU.add), reads=[r_Sbg[g], rbSt], writes=[r_Sbg[g]])
                                S.op("act", lambda: A.activation(out=Sbb[:, gs], in_=Sb_[:, gs], func=AF.Copy), reads=[r_Sbg[g]], writes=[r_Sbbg[g]])

                            def grp_endA(g):
                                gs = slice(g * 512, (g + 1) * 512)
                                (t1, rt1), (t2, rt2), (gy, rgy), (sq_, rsq_) = t1b[g % 2], t2b[g % 2], gyb[g % 2], sq2[g % 2]
                                S.op("dve", lambda: V.tensor_tensor(out=t1[:], in0=t1[:], in1=bY[:, :], op=ALU.add), reads=[rt1, rbY], writes=[rt1])
                                S.op("pool", lambda: nc.gpsimd.tensor_tensor(out=t1[:], in0=t1[:], in1=t2[:], op=ALU.add), reads=[rt1, rt2], writes=[rt1])
                                S.op("pool", lambda: nc.gpsimd.tensor_tensor(out=gy[:], in0=t1[:], in1=z_tm[:, gs], op=ALU.mult), reads=[rt1, r_z], writes=[rgy])
                                S.op("act", lambda: A.activation(out=sq_[:], in_=gy[:], func=AF.Square), reads=[rgy], writes=[rsq_])

                            def grp_endB(g):
                                gs = slice(g * 512, (g + 1) * 512)
                                (gy, rgy), (sq_, rsq_) = gyb[g % 2], sq2[g % 2]
                                S.op("dve", lambda: V.reduce_sum(out=ss[:, g:g + 1], in_=sq_[:], axis=AX.X), reads=[rsq_], writes=[r_ss])
                                S.op("act", lambda: A.activation(out=ss[:, g:g + 1], in_=ss[:, g:g + 1], func=AF.Ln, bias=epsc[:, 0:1], scale=1.0 / 512.0),
                                     reads=[r_ss, r_epsc], writes=[r_ss])
                                S.op("act", lambda: A.activation(out=ss[:, g:g + 1], in_=ss[:, g:g + 1], func=AF.Exp, scale=-0.5), reads=[r_ss], writes=[r_ss])
                                S.op("act", lambda: A.activation(out=g_tm[:, gs], in_=gy[:], func=AF.Identity, scale=ss[:, g:g + 1]), reads=[rgy, r_ss], writes=[r_gtm])

                            for k in range(4):
                                RW(k)
                            SA(0)
                            SA(1)
                            for k in range(32):
                                g = k // 4
                                if k % 4 == 0:
                                    grp_start(g)
                                    if g == 7 and ci + 1 < len(order):
                                        load_hf(ci + 1)
                                if k % 4 == 1:
                                    grp_early(g)
                                    if g > 0:
                                        grp_endB(g - 1)
                                if k + 2 < 32:
                                    SA(k + 2)
                                SC(k)
                                if k + 4 < 32:
                                    RW(k + 4)
                                if k % 4 == 3:
                                    grp_endA(g)
                            if ci + 1 < len(order):
                                load_z(ci + 1)
                            grp_endB(7)
                            for fb in range(8):
                                bk, rbk = S.banks[fb % 4]
                                pbv = bk[:].bitcast(BF16)
                                for i in range(4):
                                    fc = fb * 4 + i
                                    S.op("pe", lambda: PE.transpose(out=pbv[:, i * 128:(i + 1) * 128], in_=g_tm[:, fc * 128:(fc + 1) * 128], identity=identb[:]),
                                         reads=[r_gtm, r_identb], writes=[rbk], inc=(i == 3))
                                dst = gTst[:, fb * 4:(fb + 1) * 4, :]
                                if fb % 2 == 0:
                                    S.op("act", lambda: A.activation(out=dst, in_=pbv[:, 0:512].rearrange("f (i t) -> f i t", t=128), func=AF.Copy), reads=[rbk], writes=[r_gTst])
                                else:
                                    S.op("dve", lambda: V.tensor_copy(out=dst, in_=pbv[:, 0:512].rearrange("f (i t) -> f i t", t=128)), reads=[rbk], writes=[r_gTst])
                            S.dma("sp", GTd[:, c0:c1].rearrange("(fc p) t -> p fc t", p=128), gTst[:], reads=[r_gTst], writes=[r_GT[c]])

            if ssd_upto < 6:
                raise _Stop()
            with Phase(S) as po:
                gT, r_gT = po.sb([128, 32, T], BF16, "gT")
                sng, r_sng = po.sb([128, 32], F32, "sngT")
                S.dma("sp", sng[:], sng_in[:, js, :], writes=[r_sng])

                def wfix(w, rw):
                    S.op("dve", lambda: V.tensor_tensor(out=w[:], in0=w[:], in1=sng[:].unsqueeze(2).to_broadcast([128, 32, 128]), op=ALU.mult),
                         reads=[rw, r_sng], writes=[rw])
                S.dma("sp", gT[:], GTd.rearrange("(fc p) t -> p fc t", p=128), reads=r_GT, writes=[r_gT])
                linear(po, lambda col: w_out[js, :, col:col + 128], [m * 128 for m in range(KC)], 32, gT, r_gT,
                       resid_epilogue(po, lambda m, j: MOD[:, l, 32 + m, j:j + 1]), wtag="wo", wfix=wfix)

        r_UV = Res("UV")

        stage_mod()
        stage_i = 0
        for l in range(NL):
            if stage_i < upto:
                if l % 2 == 0:
                    stage_fourier(l)
                else:
                    try:
                        stage_ssd(l)
                    except _Stop:
                        pass
            stage_i += 1
            if stage_i < upto:
                stage_ffn(l)
            stage_i += 1

        if final and upto >= 2 * NL:
            norm_phase(None, None, lambda c, j: AFN[:, c, 0:1], lambda c, j: ZB[:, c, 0:1], [r_AFN, r_ZB], out_f32=out_d)
        else:
            with Phase(S) as ph:
                xb = [ph.sb([128, T], F32, "cp") for _ in range(2)]
                for m in range(KC):
                    x, rx = xb[m % 2]
                    S.dma("sp", x[:], XT[m * 128:(m + 1) * 128, :], reads=[r_XT[m]], writes=[rx])
                    S.dma("sp", out_d[m * 128:(m + 1) * 128, :], x[:], reads=[rx])
        S.barrier()
    return nc


_CONST = {}


def _consts():
    if _CONST:
        return _CONST
    bf = ml_dtypes.bfloat16
    k = np.arange(256)
    ang = 2.0 * np.pi * ((k[:, None] * k[None, :]) % 256) / 256.0
    c256 = np.cos(ang) / 16.0
    s256 = np.sin(ang) / 16.0
    cs1 = np.concatenate([c256, s256], axis=1)
    cs2c = np.concatenate([c256, -s256], axis=1)
    _CONST["cs1"] = np.ascontiguousarray(cs1.reshape(2, 128, 512).transpose(1, 0, 2)).astype(bf)
    _CONST["cs2c"] = np.ascontiguousarray(cs2c.reshape(2, 128, 512).transpose(1, 0, 2)).astype(bf)
    t = np.arange(L)
    angL = 2.0 * np.pi * ((t[:, None] * t[None, :]) % L) / float(L)
    _CONST["cl"] = (np.cos(angL) / math.sqrt(L)).astype(bf)
    _CONST["nsl"] = (-np.sin(angL) / math.sqrt(L)).astype(bf)
    _CONST["identb"] = np.eye(128, dtype=np.float32).astype(bf)
    s = np.arange(128)
    triU = (s[:, None] <= s[None, :]).astype(np.float32)
    triL = (s[:, None] >= s[None, :]).astype(np.float32)
    _CONST["cf32"] = np.ascontiguousarray(np.stack([triU, triL, triU, triL], axis=1)).astype(np.float32)
    return _CONST


def _pm(v, nchunk):
    v = np.asarray(v, dtype=np.float32)
    lead = v.shape[:-1]
    r = v.reshape(lead + (nchunk, 128))
    r = np.moveaxis(r, -1, 0)
    return np.ascontiguousarray(r)


def make_in_maps(inputs, cores):
    cst = _consts()
    f = lambda k: np.asarray(inputs[k], dtype=np.float32)
    shared = {
        "w_mod": f("w_mod"), "four_w": f("four_w"), "ssd_w_in": f("ssd_w_in"), "ssd_w_out": f("ssd_w_out"),
        "ffn_w_up": f("ffn_w_up"), "ffn_w_down": f("ffn_w_down"),
        "bmodT": _pm(f("b_mod"), 96), "gmixT": _pm(f("norm_mix_g"), KC), "gffnT": _pm(f("norm_ffn_g"), KC),
        "gfinT": _pm(f("final_g"), KC),
        "scwT": np.ascontiguousarray(_pm(f("ssd_conv_w"), 48).transpose(0, 1, 3, 2)),
        "scbT": _pm(f("ssd_conv_b"), 48),
        "dtb_bc": np.ascontiguousarray(np.broadcast_to(f("ssd_dt_bias").reshape(1, 2, 128), (128, 2, 128))),
        "alog_bc": np.ascontiguousarray(np.broadcast_to(f("ssd_a_log").reshape(1, 2, 128), (128, 2, 128))),
        "dsk_bc": np.ascontiguousarray(np.broadcast_to(f("ssd_d").reshape(1, 2, 128), (128, 2, 128))),
        "sngT": _pm(f("ssd_norm_g"), 32),
        "fcwT": np.ascontiguousarray(_pm(f("ffn_conv_w").reshape(NL, 9, DFF), 44).transpose(0, 1, 3, 2)),
        "fcbT": _pm(f("ffn_conv_b"), 44),
        "cs1": cst["cs1"], "cs2c": cst["cs2c"], "cl": cst["cl"], "nsl": cst["nsl"], "identb": cst["identb"], "cf32": cst["cf32"],
    }
    x, c, ctx, cc = f("x"), f("c"), f("ctx"), f("c_ctx")
    maps = []
    for b in cores:
        m = dict(shared)
        m["xc"] = np.ascontiguousarray(np.concatenate([x[b].T, ctx[b].T], axis=1))
        m["cv"] = np.ascontiguousarray(np.stack([c[b], cc], axis=-1).reshape(KC, 128, 2).transpose(1, 0, 2))
        maps.append(m)
    return maps


def kernel(**inputs):
    nc = build()
    maps = make_in_maps(inputs, list(range(8)))
    res = run_bass_kernel_spmd(nc, maps, core_ids=list(range(8)))
    out = np.stack([np.ascontiguousarray(res.results[b]["out"][:, :L].T) for b in range(8)], axis=0)
    return out.astype(np.float32)
```

```python
import math
from contextlib import ExitStack

import ml_dtypes
import numpy as np

import concourse.bass as bass
import concourse.mybir as mybir
from concourse.bass_utils import run_bass_kernel_spmd

F32 = mybir.dt.float32
BF16 = mybir.dt.bfloat16
AF = mybir.ActivationFunctionType
ALU = mybir.AluOpType
AX = mybir.AxisListType

D = 2048
L = 2048
LC = 256
T = L + LC
DFF = 5632
NL = 4
KC = 16
DIN = 4096
GN = 1024
H = 64
SSD_IN = 10368
STATE_COLS = DIN + GN + 2 * H
EPS = 1e-6
TT = [(0, 512), (512, 512), (1024, 512), (1536, 512), (2048, 256)]
NCH = 18
GS = 11
GPL = 34 * 66
GPW = GPL + 258


class Res:
    __slots__ = ("w", "r", "name", "excl")

    def __init__(self, name="", excl=False):
        self.w = None
        self.r = {}
        self.name = name
        self.excl = excl


class _Q:
    def __init__(self, name, eng, sem):
        self.name = name
        self.eng = eng
        self.sem = sem
        self.cnt = 0
        self.seen = {}
        self.slots = []
        self.di = 0


class Sched:
    NSLOT = 12

    def __init__(self, nc, es):
        self.nc = nc
        self.q = {}
        for name, eng in (("pe", nc.tensor), ("act", nc.scalar), ("dve", nc.vector), ("pool", nc.gpsimd), ("sp", nc.sync)):
            self.q[name] = _Q(name, eng, es.enter_context(nc.semaphore("s_" + name)))
        for qn in ("sp", "pool"):
            for i in range(self.NSLOT):
                self.q[qn].slots.append([es.enter_context(nc.semaphore(f"d_{qn}{i}")), 0, f"d_{qn}{i}"])
        self.banks = []
        self.bi = 0

    def _wait(self, q, tok):
        key, sem, val = tok
        if q.seen.get(key, 0) >= val:
            return
        q.eng.wait_ge(sem, val)
        q.seen[key] = val

    def _deps(self, q, reads, writes, skip_self, skip_same=False):
        for r in reads:
            if r.w is not None and not (skip_self and r.w[0] == q.name):
                self._wait(q, r.w)
            if r.excl:
                for tok in r.r.values():
                    if tok[0] != q.name:
                        self._wait(q, tok)
        for w in writes:
            if w.w is not None and not (skip_same and w.w[0] == q.name):
                self._wait(q, w.w)
            for tok in w.r.values():
                if not (skip_same and tok[0] == q.name):
                    self._wait(q, tok)

    def _mark(self, tok, reads, writes):
        for r in reads:
            old = r.r.get(tok[0])
            if old is None or old[2] < tok[2]:
                r.r[tok[0]] = tok
        for w in writes:
            w.w = tok
            w.r = {}

    def op(self, E, fn, reads=(), writes=(), inc=True):
        q = self.q[E]
        self._deps(q, reads, writes, E == "pe", True)
        ins = fn()
        if inc:
            ins.then_inc(q.sem, 1)
            q.cnt += 1
            tok = (q.name, q.sem, q.cnt)
        else:
            tok = (q.name, q.sem, q.cnt + 1)
        self._mark(tok, reads, writes)
        return tok

    def dma(self, Qn, out, in_, reads=(), writes=(), **kw):
        q = self.q[Qn]
        slot = q.slots[q.di % self.NSLOT]
        q.di += 1
        if slot[1] > 0:
            self._wait(q, (slot[2], slot[0], 16 * slot[1]))
        self._deps(q, reads, writes, False)
        q.eng.dma_start(out=out, in_=in_, **kw).then_inc(slot[0], 16)
        slot[1] += 1
        tok = (slot[2], slot[0], 16 * slot[1])
        self._mark(tok, reads, writes)
        return tok

    def barrier(self):
        toks = []
        for q in self.q.values():
            if q.cnt > 0:
                toks.append((q.name, q.sem, q.cnt))
            for s in q.slots:
                if s[1] > 0:
                    toks.append((s[2], s[0], 16 * s[1]))
        for q in self.q.values():
            for t in toks:
                if t[0] != q.name:
                    self._wait(q, t)

    def bank(self):
        b = self.banks[self.bi % len(self.banks)]
        self.bi += 1
        return b


class _Stop(Exception):
    pass


class Phase:
    uid = 0

    def __init__(self, S):
        self.S = S
        self.es = ExitStack()
        self.n = 0

    def __enter__(self):
        self.es.__enter__()
        return self

    def sb(self, shape, dt, name="t"):
        Phase.uid += 1
        t = self.es.enter_context(self.S.nc.sbuf_tensor(f"{name}_{Phase.uid}", list(shape), dt))
        return t, Res(name)

    def __exit__(self, *a):
        self.S.barrier()
        self.es.__exit__(None, None, None)
        return False


def build(upto=999, final=True, ssd_upto=99):
    nc = bass.Bass("TRN2", target_bir_lowering=False)

    def din(name, shape, dt=F32):
        return nc.dram_tensor(name, list(shape), dt, kind="ExternalInput").ap()

    xc_in = din("xc", [D, T])
    cv_in = din("cv", [128, KC, 2])
    w_mod = din("w_mod", [NL, D, 6 * D])
    bmod_in = din("bmodT", [128, NL, 96])
    gmix_in = din("gmixT", [128, NL, KC])
    gffn_in = din("gffnT", [128, NL, KC])
    gfin_in = din("gfinT", [128, KC])
    four_w = din("four_w", [2, D, D])
    w_in = din("ssd_w_in", [2, D, SSD_IN])
    scw_in = din("scwT", [128, 2, 48, 3])
    scb_in = din("scbT", [128, 2, 48])
    dtb_in = din("dtb_bc", [128, 2, 128])
    alog_in = din("alog_bc", [128, 2, 128])
    dsk_in = din("dsk_bc", [128, 2, 128])
    sng_in = din("sngT", [128, 2, 32])
    w_out = din("ssd_w_out", [2, DIN, D])
    w_up = din("ffn_w_up", [NL, D, 2 * DFF])
    fcw_in = din("fcwT", [128, NL, 44, 9])
    fcb_in = din("fcbT", [128, NL, 44])
    w_down = din("ffn_w_down", [NL, DFF, D])
    cs1_in = din("cs1", [128, 2, 512], BF16)
    cs2c_in = din("cs2c", [128, 2, 512], BF16)
    cl_in = din("cl", [L, L], BF16)
    nsl_in = din("nsl", [L, L], BF16)
    identb_in = din("identb", [128, 128], BF16)
    cf_in = din("cf32", [128, 4, 128])
    out_d = nc.dram_tensor("out", [D, T], F32, kind="ExternalOutput").ap()

    XT = nc.dram_tensor("XT", [D, T], F32).ap()
    UVd = nc.dram_tensor("UVd", [T, 8, 512], BF16).ap()
    XBC = nc.dram_tensor("XBC", [48 * 128, T], BF16).ap()
    Zd = nc.dram_tensor("Zd", [T, DIN], BF16).ap()
    XSd = nc.dram_tensor("XSd", [T, DIN], BF16).ap()
    BTMd = nc.dram_tensor("BTMd", [T, GN], BF16).ap()
    HFd = nc.dram_tensor("HFd", [NCH, 128, DIN], BF16).ap()
    GTd = nc.dram_tensor("GTd", [DIN, T], BF16).ap()

    with ExitStack() as es:
        S = Sched(nc, es)
        for i in range(8):
            S.banks.append((es.enter_context(nc.psum_tensor(f"bank{i}", [128, 512], F32)), Res(f"bank{i}", excl=True)))

        def psb(shape, dt, name):
            return es.enter_context(nc.sbuf_tensor("sb_" + name, list(shape), dt)), Res(name)

        ones_bf, r_ones = psb([128, 128], BF16, "ones_bf")
        ones_f, r_onesf = psb([128, 128], F32, "ones_f")
        identb, r_identb = psb([128, 128], BF16, "identb")
        cf, r_cf = psb([128, 4, 128], F32, "cf")
        mkb, r_mkb = psb([128, 2, 128], BF16, "mkb")
        sT, r_sT = psb([128, KC, 2], BF16, "sT")
        MOD, r_MOD = psb([128, NL, 96, 2], F32, "MOD")
        A1, r_A1 = psb([128, NL, KC, 2], F32, "A1")
        A2, r_A2 = psb([128, NL, KC, 2], F32, "A2")
        AFN, r_AFN = psb([128, KC, 2], F32, "AFN")
        ZB, r_ZB = psb([128, KC, 2], F32, "ZB")
        bmod, r_bmod = psb([128, NL, 96], F32, "bmod")
        gmix, r_gmix = psb([128, NL, KC], F32, "gmix")
        gffn, r_gffn = psb([128, NL, KC], F32, "gffn")
        gfin, r_gfin = psb([128, KC], F32, "gfin")
        scw, r_scw = psb([128, 2, 48, 3], F32, "scw")
        scb, r_scb = psb([128, 2, 48], F32, "scb")
        cvt, r_cvt = psb([128, KC, 2], F32, "cvt")

        V = nc.vector
        A = nc.scalar
        PE = nc.tensor
        r_XT = [Res(f"XT{m}") for m in range(KC)]

        S.op("dve", lambda: V.memset(ones_bf[:], 1.0), writes=[r_ones])
        S.op("dve", lambda: V.memset(ones_f[:], 1.0), writes=[r_onesf])
        S.op("dve", lambda: V.memset(ZB[:], 0.0), writes=[r_ZB])
        for dst, rr, src in ((identb, r_identb, identb_in), (cf, r_cf, cf_in), (bmod, r_bmod, bmod_in),
                             (gmix, r_gmix, gmix_in), (gffn, r_gffn, gffn_in), (gfin, r_gfin, gfin_in),
                             (scw, r_scw, scw_in), (scb, r_scb, scb_in),
                             (cvt, r_cvt, cv_in)):
            S.dma("sp", dst[:], src, writes=[rr])
        S.op("dve", lambda: V.tensor_copy(out=mkb[:], in_=cf[:, 2:4, :]), reads=[r_cf], writes=[r_mkb])
        S.op("act", lambda: A.activation(out=sT[:], in_=cvt[:], func=AF.Silu), reads=[r_cvt], writes=[r_sT])
        for m in range(KC):
            S.dma("sp", XT[m * 128:(m + 1) * 128, :], xc_in[m * 128:(m + 1) * 128, :], writes=[r_XT[m]])

        def stage_mod():
            with Phase(S) as ph:
                wb = [ph.sb([128, KC, 512], BF16, "modw") for _ in range(3)]
                for l in range(NL):
                    bk, rbk = S.bank()
                    for r4 in range(24):
                        w, rw = wb[(l * 24 + r4) % 3]
                        S.dma("pool", w[:], w_mod[l, :, r4 * 512:(r4 + 1) * 512].rearrange("(kc p) c -> p kc c", p=128), writes=[rw])
                        for rr in range(4):
                            r = r4 * 4 + rr
                            for kc in range(KC):
                                S.op("pe", lambda: PE.matmul(bk[:, 2 * r:2 * r + 2], w[:, kc, rr * 128:(rr + 1) * 128], sT[:, kc, :],
                                                             start=(kc == 0), stop=(kc == KC - 1)),
                                     reads=[rw, r_sT], writes=[rbk], inc=(kc == KC - 1))
                    for j in range(2):
                        S.op("dve", lambda: V.tensor_tensor(out=MOD[:, l, :, j], in0=bk[:, 0:192].rearrange("p (r j) -> p r j", j=2)[:, :, j],
                                                            in1=bmod[:, l, :], op=ALU.add),
                             reads=[rbk, r_bmod], writes=[r_MOD])
                    sq = math.sqrt(float(D))
                    for j in range(2):
                        for (AA, rA, gg, rg, off) in ((A1, r_A1, gmix, r_gmix, 16), (A2, r_A2, gffn, r_gffn, 64)):
                            S.op("dve", lambda: V.tensor_scalar(out=AA[:, l, :, j], in0=MOD[:, l, off:off + 16, j], scalar1=1.0, scalar2=sq,
                                                                op0=ALU.add, op1=ALU.mult), reads=[r_MOD], writes=[rA])
                            S.op("dve", lambda: V.tensor_tensor(out=AA[:, l, :, j], in0=AA[:, l, :, j], in1=gg[:, l, :], op=ALU.mult),
                                 reads=[rA, rg], writes=[rA])
                for j in range(2):
                    S.op("dve", lambda: V.tensor_scalar(out=AFN[:, :, j], in0=gfin[:, :], scalar1=math.sqrt(float(D)), scalar2=None, op0=ALU.mult),
                         reads=[r_gfin], writes=[r_AFN])

        def norm_phase(aT, r_aT, scale_ap, bias_ap, rs, out_f32=None):
            with Phase(S) as ph:
                xb = [ph.sb([128, T], F32, "nx") for _ in range(3)]
                sqb = [ph.sb([128, T], BF16, "nsq") for _ in range(2)]
                rstd, r_rstd = ph.sb([128, T], F32, "rstd")
                ob = [ph.sb([128, T], F32, "nout") for _ in range(2)] if out_f32 is not None else None
                bks = [S.bank() for _ in range(5)]
                for c in range(KC):
                    x, rx = xb[c % 3]
                    s, rsq = sqb[c % 2]
                    S.dma("sp", x[:], XT[c * 128:(c + 1) * 128, :], reads=[r_XT[c]], writes=[rx])
                    S.op("act", lambda: A.activation(out=s[:], in_=x[:], func=AF.Square), reads=[rx], writes=[rsq])
                    for ti, (t0, n) in enumerate(TT):
                        bk, rbk = bks[ti]
                        S.op("pe", lambda: PE.matmul(bk[:, :n], ones_bf[:], s[:, t0:t0 + n], start=(c == 0), stop=(c == KC - 1)),
                             reads=[rsq, r_ones], writes=[rbk], inc=(ti == 4))
                for ti, (t0, n) in enumerate(TT):
                    bk, rbk = bks[ti]
                    S.op("dve", lambda: V.tensor_scalar(out=rstd[:, t0:t0 + n], in0=bk[:, :n], scalar1=float(D) * EPS, scalar2=None,
                                                        op0=ALU.add), reads=[rbk], writes=[r_rstd])
                    S.op("act", lambda: A.activation(out=rstd[:, t0:t0 + n], in_=rstd[:, t0:t0 + n], func=AF.Sqrt), reads=[r_rstd], writes=[r_rstd])
                    S.op("dve", lambda: V.reciprocal(out=rstd[:, t0:t0 + n], in_=rstd[:, t0:t0 + n]), reads=[r_rstd], writes=[r_rstd])
                for c in range(KC):
                    x, rx = xb[(KC + c) % 3]
                    S.dma("sp", x[:], XT[c * 128:(c + 1) * 128, :], reads=[r_XT[c]], writes=[rx])
                    S.op("dve", lambda: V.tensor_tensor(out=x[:], in0=x[:], in1=rstd[:], op=ALU.mult), reads=[rx, r_rstd], writes=[rx])
                    if out_f32 is None:
                        for j, (a, b) in enumerate(((0, L), (L, T))):
                            S.op("act", lambda: A.activation(out=aT[:, c, a:b], in_=x[:, a:b], func=AF.Identity,
                                                             bias=bias_ap(c, j), scale=scale_ap(c, j)),
                                 reads=[rx] + rs, writes=[r_aT[c]])
                    else:
                        o, ro = ob[c % 2]
                        S.op("act", lambda: A.activation(out=o[:], in_=x[:], func=AF.Identity, bias=bias_ap(c, 0), scale=scale_ap(c, 0)),
                             reads=[rx] + rs, writes=[ro])
                        S.dma("sp", out_f32[c * 128:(c + 1) * 128, :], o[:], reads=[ro])

        def linear(ph, Wsrc, cols, kcn, inT, r_in, epilogue, wtag="lw", nbuf=3, wfix=None):
            wb = [ph.sb([128, kcn, 128], BF16, wtag) for _ in range(nbuf)]
            for mi, col in enumerate(cols):
                w, rw = wb[mi % nbuf]
                S.dma("pool", w[:], Wsrc(col).rearrange("(kc p) c -> p kc c", p=128), writes=[rw])
                if wfix is not None:
                    wfix(w, rw)
                bl = []
                for ti, (t0, n) in enumerate(TT):
                    bk, rbk = S.bank()
                    for kc in range(kcn):
                        S.op("pe", lambda: PE.matmul(bk[:, :n], w[:, kc, :], inT[:, kc, t0:t0 + n], start=(kc == 0), stop=(kc == kcn - 1)),
                             reads=[rw, r_in[kc] if isinstance(r_in, list) else r_in], writes=[rbk], inc=(kc == kcn - 1))
                    bl.append((bk, rbk))
                epilogue(mi, col, bl)

        def resid_epilogue(ph, gate_ap):
            xt = [ph.sb([128, T], F32, "rxt") for _ in range(2)]

            def ep(mi, col, bl):
                m = col // 128
                x, rx = xt[mi % 2]
                S.dma("sp", x[:], XT[m * 128:(m + 1) * 128, :], reads=[r_XT[m]], writes=[rx])
                for ti, (t0, n) in enumerate(TT):
                    bk, rbk = bl[ti]
                    j = 0 if t0 < L else 1
                    S.op("dve", lambda: V.scalar_tensor_tensor(out=x[:, t0:t0 + n], in0=bk[:, :n], scalar=gate_ap(m, j), in1=x[:, t0:t0 + n],
                                                               op0=ALU.mult, op1=ALU.add), reads=[rbk, rx, r_MOD], writes=[rx])
                S.dma("sp", XT[m * 128:(m + 1) * 128, :], x[:], reads=[rx], writes=[r_XT[m]])
            return ep

        def stage_ffn(l):
            with Phase(S) as pho:
                aT, _ = pho.sb([128, KC, T], BF16, "aT")
                r_aT = [Res() for _ in range(KC)]
                norm_phase(aT, r_aT, lambda c, j: A2[:, l, c, j:j + 1], lambda c, j: MOD[:, l, 48 + c, j:j + 1], [r_A2, r_MOD])
                hT, _ = pho.sb([128, GS, T], BF16, "hT")
                fcw, r_fcw = pho.sb([128, 44, 9], F32, "fcw")
                fcb, r_fcb = pho.sb([128, 44], F32, "fcb")
                S.dma("sp", fcw[:], fcw_in[:, l, :, :], writes=[r_fcw])
                S.dma("sp", fcb[:], fcb_in[:, l, :], writes=[r_fcb])
                groups = [list(range(g0, min(g0 + GS, 44))) for g0 in range(0, 44, GS)]
                for grp in groups:
                    r_h = [Res() for _ in grp]
                    with Phase(S) as ph:
                        gp = [ph.sb([128, GPW], F32, "gp") for _ in range(2)]
                        acc = [ph.sb([128, T], F32, "acc") for _ in range(1)]
                        vb = [ph.sb([128, T], F32, "vb") for _ in range(2)]
                        wg = [ph.sb([128, KC, 128], BF16, "wg") for _ in range(2)]
                        wv = [ph.sb([128, KC, 128], BF16, "wv") for _ in range(2)]
                        for g_, rg_ in gp:
                            S.op("dve", lambda: V.memset(g_[:], 0.0), writes=[rg_])
                        def u_vars(ji):
                            return gp[ji % 2], acc[0], vb[ji % 2], wg[ji % 2], wv[ji % 2]

                        def u_pe(ji, j):
                            (g_, rg_), (ac, rac), (vv, rvv), (wgt, rwg), (wvt, rwv) = u_vars(ji)
                            S.dma("pool", wgt[:], w_up[l, :, j * 128:(j + 1) * 128].rearrange("(kc p) c -> p kc c", p=128), writes=[rwg])
                            S.dma("pool", wvt[:], w_up[l, :, DFF + j * 128:DFF + (j + 1) * 128].rearrange("(kc p) c -> p kc c", p=128), writes=[rwv])
                            glat = g_[:, 0:GPL].rearrange("p (r c) -> p r c", c=66)
                            for ti, (t0, n) in enumerate(TT):
                                bg, rbg = S.bank()
                                for kc in range(KC):
                                    S.op("pe", lambda: PE.matmul(bg[:, :n], wgt[:, kc, :], aT[:, kc, t0:t0 + n], start=(kc == 0), stop=(kc == KC - 1)),
                                         reads=[rwg, r_aT[kc]], writes=[rbg], inc=(kc == KC - 1))
                                if t0 < L:
                                    S.op("act", lambda: A.activation(out=glat[:, 1 + 8 * ti:9 + 8 * ti, 1:65],
                                                                     in_=bg[:, :512].rearrange("p (r c) -> p r c", c=64), func=AF.Copy),
                                         reads=[rbg], writes=[rg_])
                                else:
                                    S.op("act", lambda: A.activation(out=g_[:, GPL + 1:GPL + 257], in_=bg[:, :256], func=AF.Copy),
                                         reads=[rbg], writes=[rg_])
                                bv, rbv = S.bank()
                                for kc in range(KC):
                                    S.op("pe", lambda: PE.matmul(bv[:, :n], wvt[:, kc, :], aT[:, kc, t0:t0 + n], start=(kc == 0), stop=(kc == KC - 1)),
                                         reads=[rwv, r_aT[kc]], writes=[rbv], inc=(kc == KC - 1))
                                S.op("act", lambda: A.activation(out=vv[:, t0:t0 + n], in_=bv[:, :n], func=AF.Copy), reads=[rbv], writes=[rvv])

                        def u_post(ji, j):
                            (g_, rg_), (ac, rac), (vv, rvv), (wgt, rwg), (wvt, rwv) = u_vars(ji)
                            glat = g_[:, 0:GPL].rearrange("p (r c) -> p r c", c=66)
                            alat = ac[:, 0:L].rearrange("p (r c) -> p r c", c=64)
                            first = True
                            for di in range(3):
                                for dj in range(3):
                                    wsc = fcw[:, j, di * 3 + dj:di * 3 + dj + 1]
                                    src = glat[:, di:di + 32, dj:dj + 64]
                                    if first:
                                        S.op("dve", lambda: V.tensor_scalar(out=alat, in0=src, scalar1=wsc, scalar2=fcb[:, j:j + 1],
                                                                            op0=ALU.mult, op1=ALU.add), reads=[rg_, r_fcw, r_fcb], writes=[rac])
                                        first = False
                                    else:
                                        S.op("dve", lambda: V.scalar_tensor_tensor(out=alat, in0=src, scalar=wsc, in1=alat, op0=ALU.mult, op1=ALU.add),
                                             reads=[rg_, rac, r_fcw], writes=[rac])
                            for dj in range(3):
                                wsc = fcw[:, j, 3 + dj:3 + dj + 1]
                                src = g_[:, GPL + dj:GPL + dj + 256]
                                if dj == 0:
                                    S.op("dve", lambda: V.tensor_scalar(out=ac[:, L:T], in0=src, scalar1=wsc, scalar2=fcb[:, j:j + 1],
                                                                        op0=ALU.mult, op1=ALU.add), reads=[rg_, r_fcw, r_fcb], writes=[rac])
                                else:
                                    S.op("dve", lambda: V.scalar_tensor_tensor(out=ac[:, L:T], in0=src, scalar=wsc, in1=ac[:, L:T], op0=ALU.mult, op1=ALU.add),
                                         reads=[rg_, rac, r_fcw], writes=[rac])
                            S.op("act", lambda: A.activation(out=ac[:], in_=ac[:], func=AF.Silu), reads=[rac], writes=[rac])
                            S.op("dve", lambda: V.tensor_tensor(out=hT[:, ji, :], in0=ac[:], in1=vv[:], op=ALU.mult), reads=[rac, rvv], writes=[r_h[ji]])

                        for ji, j in enumerate(grp):
                            u_pe(ji, j)
                            if ji > 0:
                                u_post(ji - 1, grp[ji - 1])
                        u_post(len(grp) - 1, grp[-1])
                    with Phase(S) as ph:
                        linear(ph, lambda col: w_down[l, grp[0] * 128:(grp[-1] + 1) * 128, col:col + 128], [m * 128 for m in range(KC)], len(grp),
                               hT, r_h, resid_epilogue(ph, lambda m, j: MOD[:, l, 80 + m, j:j + 1]), wtag="wd")

        def stage_fourier(l):
            jf = l // 2
            with Phase(S) as pho:
                fT, _ = pho.sb([128, KC, T], BF16, "fT")
                r_fT = [Res() for _ in range(KC)]
                with Phase(S) as ph1:
                    aT, _ = ph1.sb([128, KC, T], BF16, "aT")
                    r_aT = [Res() for _ in range(KC)]
                    norm_phase(aT, r_aT, lambda c, j: A1[:, l, c, j:j + 1], lambda c, j: MOD[:, l, c, j:j + 1], [r_A1, r_MOD])
                    cs1, r_cs1 = ph1.sb([128, 2, 512], BF16, "cs1")
                    S.dma("sp", cs1[:], cs1_in, writes=[r_cs1])
                    st = [ph1.sb([128, 8, 512], BF16, "uvst") for _ in range(2)]
                    for tt in range(NCH):
                        s_, rs_ = st[tt % 2]
                        for g in range(8):
                            bk, rbk = S.bank()
                            for kc in range(2):
                                S.op("pe", lambda: PE.matmul(bk[:, :], aT[:, 2 * g + kc, tt * 128:(tt + 1) * 128], cs1[:, kc, :], start=(kc == 0), stop=(kc == 1)),
                                     reads=[r_aT[2 * g + kc], r_cs1], writes=[rbk], inc=(kc == 1))
                            if g % 2 == 0:
                                S.op("act", lambda: A.activation(out=s_[:, g, :], in_=bk[:, :], func=AF.Copy), reads=[rbk], writes=[rs_])
                            else:
                                S.op("dve", lambda: V.tensor_copy(out=s_[:, g, :], in_=bk[:, :]), reads=[rbk], writes=[rs_])
                        S.dma("sp", UVd[tt * 128:(tt + 1) * 128, :, :], s_[:], reads=[rs_], writes=[r_UV])
                with Phase(S) as ph2:
                    tb = [(ph2.sb([128, KC, 512], BF16, "cl"), ph2.sb([128, KC, 512], BF16, "nsl")) for _ in range(2)]
                    uv = [ph2.sb([128, KC, 512], BF16, "uvg") for _ in range(2)]
                    cnt = 0
                    for tp in range(4):
                        (clt, rcl), (nst, rns) = tb[tp % 2]
                        S.dma("sp", clt[:], cl_in[:, tp * 512:(tp + 1) * 512].rearrange("(tt p) c -> p tt c", p=128), writes=[rcl])
                        S.dma("sp", nst[:], nsl_in[:, tp * 512:(tp + 1) * 512].rearrange("(tt p) c -> p tt c", p=128), writes=[rns])
                        for g in range(8):
                            u_, ru_ = uv[cnt % 2]
                            cnt += 1
                            S.dma("sp", u_[:], UVd[0:L, g, :].rearrange("(tt p) c -> p tt c", p=128), reads=[r_UV], writes=[ru_])
                            for hh in range(2):
                                bk, rbk = S.bank()
                                for tt in range(KC):
                                    S.op("pe", lambda: PE.matmul(bk[:, :], u_[:, tt, hh * 128:(hh + 1) * 128], clt[:, tt, :], start=(tt == 0), stop=False),
                                         reads=[ru_, rcl], writes=[rbk], inc=False)
                                    S.op("pe", lambda: PE.matmul(bk[:, :], u_[:, tt, 256 + hh * 128:256 + (hh + 1) * 128], nst[:, tt, :], start=False, stop=(tt == KC - 1)),
                                         reads=[ru_, rns], writes=[rbk], inc=(tt == KC - 1))
                                cp = 2 * g + hh
                                if hh == 0:
                                    S.op("act", lambda: A.activation(out=fT[:, cp, tp * 512:(tp + 1) * 512], in_=bk[:, :], func=AF.Copy), reads=[rbk], writes=[r_fT[cp]])
                                else:
                                    S.op("dve", lambda: V.tensor_copy(out=fT[:, cp, tp * 512:(tp + 1) * 512], in_=bk[:, :]), reads=[rbk], writes=[r_fT[cp]])
                    c2, r_c2 = ph2.sb([128, 2, 512], BF16, "cs2c")
                    S.dma("sp", c2[:], cs2c_in, writes=[r_c2])
                    uc, r_uc = ph2.sb([128, 2, 8, 512], BF16, "uvc")
                    for tt in range(2):
                        S.dma("sp", uc[:, tt, :, :], UVd[L + tt * 128:L + (tt + 1) * 128, :, :], reads=[r_UV], writes=[r_uc])
                    for cp in range(KC):
                        g, hh = cp // 2, cp % 2
                        bk, rbk = S.bank()
                        for tt in range(2):
                            S.op("pe", lambda: PE.matmul(bk[:, :256], uc[:, tt, g, hh * 128:(hh + 1) * 128], c2[:, tt, 0:256], start=(tt == 0), stop=False),
                                 reads=[r_uc, r_c2], writes=[rbk], inc=False)
                            S.op("pe", lambda: PE.matmul(bk[:, :256], uc[:, tt, g, 256 + hh * 128:256 + (hh + 1) * 128], c2[:, tt, 256:512], start=False, stop=(tt == 1)),
                                 reads=[r_uc, r_c2], writes=[rbk], inc=(tt == 1))
                        S.op("act", lambda: A.activation(out=fT[:, cp, L:T], in_=bk[:, :256], func=AF.Copy), reads=[rbk], writes=[r_fT[cp]])
                with Phase(S) as ph3:
                    linear(ph3, lambda col: four_w[jf, :, col:col + 128], [m * 128 for m in range(KC)], KC, fT, r_fT,
                           resid_epilogue(ph3, lambda m, j: MOD[:, l, 32 + m, j:j + 1]), wtag="fw")


        def stage_ssd(l):
            js = l // 2
            ZC0 = STATE_COLS + GN
            cols_xbc = [i * 128 for i in range(32)] + [DIN + i * 128 for i in range(8)] + [STATE_COLS + i * 128 for i in range(8)]
            r_XBC = [Res() for _ in range(48)]
            r_Z = [Res() for _ in range(NCH)]
            r_XS = [Res() for _ in range(NCH)]
            r_BTM = [Res() for _ in range(NCH)]
            r_HF = [Res() for _ in range(NCH)]
            r_GT = [Res() for _ in range(NCH)]

            def bc(ap, n0, n1):
                return ap.unsqueeze(2).to_broadcast([128, n0, n1])

            def v3(ap, n1=64):
                return ap.rearrange("s (h p) -> s h p", p=n1)

            with Phase(S) as pho:
                dt_tm, r_dt = pho.sb([128, NCH, 128], F32, "dt_tm")
                with Phase(S) as ph1:
                    aT, _ = ph1.sb([128, KC, T], BF16, "aT")
                    r_aT = [Res() for _ in range(KC)]
                    norm_phase(aT, r_aT, lambda c, j: A1[:, l, c, j:j + 1], lambda c, j: MOD[:, l, c, j:j + 1], [r_A1, r_MOD])
                    with Phase(S) as ph:
                        P1 = [ph.sb([128, 2308], F32, "p1") for _ in range(2)]
                        acc = [ph.sb([128, T], F32, "cacc") for _ in range(2)]
                        stg = [ph.sb([128, T], BF16, "cst") for _ in range(2)]
                        for p_, rp_ in P1:
                            S.op("dve", lambda: V.memset(p_[:], 0.0), writes=[rp_])

                        def ep_conv(mi, col, bl):
                            p_, rp_ = P1[mi % 2]
                            ac, rac = acc[mi % 2]
                            st_, rst = stg[mi % 2]
                            for ti, (t0, n) in enumerate(TT):
                                bk, rbk = bl[ti]
                                dst = p_[:, 1 + t0:1 + t0 + n] if t0 < L else p_[:, 2051:2307]
                                S.op("act", lambda: A.activation(out=dst, in_=bk[:, :n], func=AF.Copy), reads=[rbk], writes=[rp_])
                            for (o0, o1, base) in ((0, L, 0), (L, T, 2050)):
                                wd_ = o1 - o0
                                S.op("dve", lambda: V.tensor_scalar(out=ac[:, o0:o1], in0=p_[:, base:base + wd_], scalar1=scw[:, js, mi, 0:1],
                                                                    scalar2=scb[:, js, mi:mi + 1], op0=ALU.mult, op1=ALU.add),
                                     reads=[rp_, r_scw, r_scb], writes=[rac])
                                for k in (1, 2):
                                    S.op("dve", lambda: V.scalar_tensor_tensor(out=ac[:, o0:o1], in0=p_[:, base + k:base + k + wd_], scalar=scw[:, js, mi, k:k + 1],
                                                                               in1=ac[:, o0:o1], op0=ALU.mult, op1=ALU.add),
                                         reads=[rp_, rac, r_scw], writes=[rac])
                            S.op("act", lambda: A.activation(out=st_[:], in_=ac[:], func=AF.Silu), reads=[rac], writes=[rst])
                            S.dma("sp", XBC[mi * 128:(mi + 1) * 128, :], st_[:], reads=[rst], writes=[r_XBC[mi]])

                        linear(ph, lambda col: w_in[js, :, col:col + 128], cols_xbc, KC, aT, r_aT, ep_conv, wtag="win")
                    if ssd_upto < 2:
                        raise _Stop()
                    with Phase(S) as ph:
                        wz = [ph.sb([128, KC, 512], BF16, "wz") for _ in range(2)]
                        zst = [ph.sb([128, 512], BF16, "zst") for _ in range(4)]
                        k = 0
                        for ct in range(8):
                            w, rw = wz[ct % 2]
                            S.dma("pool", w[:], w_in[js, :, ZC0 + ct * 512:ZC0 + (ct + 1) * 512].rearrange("(kc p) c -> p kc c", p=128), writes=[rw])
                            for tt in range(NCH):
                                bk, rbk = S.bank()
                                for kc in range(KC):
                                    S.op("pe", lambda: PE.matmul(bk[:, :], aT[:, kc, tt * 128:(tt + 1) * 128], w[:, kc, :], start=(kc == 0), stop=(kc == KC - 1)),
                                         reads=[rw, r_aT[kc]], writes=[rbk], inc=(kc == KC - 1))
                                z_, rz_ = zst[k % 4]
                                k += 1
                                S.op("act", lambda: A.activation(out=z_[:], in_=bk[:, :], func=AF.Silu), reads=[rbk], writes=[rz_])
                                S.dma("sp", Zd[tt * 128:(tt + 1) * 128, ct * 512:(ct + 1) * 512], z_[:], reads=[rz_], writes=[r_Z[tt]])
                        wdt, rwdt = ph.sb([128, KC, 128], BF16, "wdt")
                        dtb, rdtb = ph.sb([128, 128], F32, "dtb")
                        S.dma("sp", dtb[:], dtb_in[:, js, :], writes=[rdtb])
                        S.dma("pool", wdt[:], w_in[js, :, DIN + GN:DIN + GN + 128].rearrange("(kc p) c -> p kc c", p=128), writes=[rwdt])
                        tmp = [ph.sb([128, 128], F32, "dtt") for _ in range(2)]
                        for tt in range(NCH):
                            bk, rbk = S.bank()
                            for kc in range(KC):
                                S.op("pe", lambda: PE.matmul(bk[:, 0:128], aT[:, kc, tt * 128:(tt + 1) * 128], wdt[:, kc, :], start=(kc == 0), stop=(kc == KC - 1)),
                                     reads=[rwdt, r_aT[kc]], writes=[rbk], inc=(kc == KC - 1))
                            t_, rt_ = tmp[tt % 2]
                            S.op("dve", lambda: V.tensor_tensor(out=t_[:], in0=bk[:, 0:128], in1=dtb[:], op=ALU.add), reads=[rbk, rdtb], writes=[rt_])
                            S.op("act", lambda: A.activation(out=t_[:], in_=t_[:], func=AF.Exp), reads=[rt_], writes=[rt_])
                            S.op("act", lambda: A.activation(out=dt_tm[:, tt, :], in_=t_[:], func=AF.Ln, bias=ones_f[:, 0:1]), reads=[rt_, r_onesf], writes=[r_dt])

                if ssd_upto < 3:
                    raise _Stop()
                with Phase(S) as pd:
                    nacsl, r_nacsl = pd.sb([128, NCH, 128], F32, "nacsl")
                    ea, r_ea = pd.sb([128, NCH, 128], F32, "ea")
                    dtw, r_dtw = pd.sb([128, NCH, 128], F32, "dtw")
                    cd, r_cd = pd.sb([128, NCH, 128], F32, "cd")
                    acs2 = [pd.sb([128, T], BF16, "acs2") for _ in range(2)]
                    sel2, r_sel2 = pd.sb([128, 64], BF16, "sel2")
                    S.op("dve", lambda: V.tensor_copy(out=sel2[0:64, :], in_=identb[0:64, 0:64]), reads=[r_identb], writes=[r_sel2])
                    S.op("dve", lambda: V.tensor_copy(out=sel2[64:128, :], in_=identb[64:128, 64:128]), reads=[r_identb], writes=[r_sel2])
                    epsc, r_epsc = pd.sb([128, 1], F32, "epsc")
                    S.op("dve", lambda: V.memset(epsc[:], EPS), writes=[r_epsc])
                    abc, r_abc = pd.sb([128, 128], F32, "abc")
                    dsk, r_dsk = pd.sb([128, 128], F32, "dsk")
                    dsum, r_dsum = pd.sb([128, 64], F32, "dsum")
                    identf, r_identf = pd.sb([128, 128], F32, "identf")
                    S.op("dve", lambda: V.tensor_tensor(out=identf[:], in0=cf[:, 0, :], in1=cf[:, 1, :], op=ALU.mult), reads=[r_cf], writes=[r_identf])
                    S.dma("sp", abc[:], alog_in[:, js, :], writes=[r_abc])
                    S.dma("sp", dsk[:], dsk_in[:, js, :], writes=[r_dsk])
                    S.op("act", lambda: A.activation(out=abc[:], in_=abc[:], func=AF.Exp), reads=[r_abc], writes=[r_abc])
                    S.op("dve", lambda: V.tensor_scalar(out=abc[:], in0=abc[:], scalar1=-1.0, scalar2=None, op0=ALU.mult), reads=[r_abc], writes=[r_abc])
                    S.op("dve", lambda: V.tensor_tensor(out=dsum[:], in0=dsk[:, 0:64], in1=dsk[:, 64:128], op=ALU.add), reads=[r_dsk], writes=[r_dsum])
                    with Phase(S) as pp:
                        dta, r_dta = pp.sb([128, NCH, 128], F32, "dta")
                        nacs, r_nacs = pp.sb([128, NCH, 128], F32, "nacs")
                        lnd = [pp.sb([128, 128], F32, "lnd") for _ in range(2)]
                        acsT = [pp.sb([128, T], F32, "acsT") for _ in range(2)]
                        r1_, r_r1 = pp.sb([128, T], F32, "r1")
                        dup = [pp.sb([128, 2, 128], F32, "dup") for _ in range(2)]
                        tw = [pp.sb([128, 128], F32, "tw") for _ in range(2)]
                        for c in range(NCH):
                            S.op("dve", lambda: V.tensor_tensor(out=dta[:, c, :], in0=dt_tm[:, c, :], in1=abc[:], op=ALU.mult), reads=[r_dt, r_abc], writes=[r_dta])
                            (bA, rbA), (bB, rbB), (bC, rbC), (bD, rbD) = S.bank(), S.bank(), S.bank(), S.bank()
                            du, rdu = dup[c % 2]
                            for d in range(2):
                                S.op("dve", lambda: V.tensor_copy(out=du[:, d, :].rearrange("s (r h) -> s r h", r=2),
                                                                  in_=dta[:, c, d * 64:(d + 1) * 64].unsqueeze(1).to_broadcast([128, 2, 64])),
                                     reads=[r_dta], writes=[rdu])
                            S.op("pe", lambda: PE.matmul(bA[:, 0:64], cf[:, 0, :], dta[:, c, 0:64], start=True, stop=True), reads=[r_cf, r_dta], writes=[rbA], inc=False)
                            S.op("pe", lambda: PE.matmul(bA[:, 64:128], cf[:, 1, :], dta[:, c, 64:128], start=True, stop=True), reads=[r_cf, r_dta], writes=[rbA], inc=False)
                            S.op("pe", lambda: PE.matmul(bB[:, 0:128], ones_f[:], dta[:, c, :], start=True, stop=True), reads=[r_onesf, r_dta], writes=[rbB], inc=False)
                            S.op("pe", lambda: PE.matmul(bC[:, 0:128], du[:, 0, :], cf[:, 0, :], start=True, stop=True), reads=[r_cf, rdu], writes=[rbC], inc=False)
                            S.op("pe", lambda: PE.matmul(bD[:, 0:128], du[:, 1, :], cf[:, 1, :], start=True, stop=True), reads=[r_cf, rdu], writes=[rbD], inc=True)
                            S.op("dve", lambda: V.tensor_scalar(out=nacs[:, c, :], in0=bA[:, 0:128], scalar1=-1.0, scalar2=None, op0=ALU.mult), reads=[rbA], writes=[r_nacs])
                            S.op("act", lambda: A.activation(out=ea[:, c, :], in_=bA[:, 0:128], func=AF.Exp), reads=[rbA], writes=[r_ea])
                            ld_, rld_ = lnd[c % 2]
                            S.op("act", lambda: A.activation(out=ld_[:], in_=dt_tm[:, c, :], func=AF.Ln), reads=[r_dt], writes=[rld_])
                            S.op("dve", lambda: V.tensor_tensor(out=nacsl[:, c, :], in0=nacs[:, c, :], in1=ld_[:], op=ALU.add), reads=[r_nacs, rld_], writes=[r_nacsl])
                            t_, rt_ = tw[c % 2]
                            S.op("dve", lambda: V.tensor_tensor(out=t_[:], in0=bB[:, 0:128], in1=nacs[:, c, :], op=ALU.add), reads=[rbB, r_nacs], writes=[rt_])
                            S.op("act", lambda: A.activation(out=t_[:], in_=t_[:], func=AF.Exp), reads=[rt_], writes=[rt_])
                            S.op("dve", lambda: V.tensor_tensor(out=dtw[:, c, :], in0=dt_tm[:, c, :], in1=t_[:], op=ALU.mult), reads=[r_dt, rt_], writes=[r_dtw])
                            S.op("act", lambda: A.activation(out=cd[:, c, :], in_=bB[:, 0:128], func=AF.Exp), reads=[rbB], writes=[r_cd])
                            S.op("act", lambda: A.activation(out=acsT[0][0][:, c * 128:(c + 1) * 128], in_=bC[:, 0:128], func=AF.Copy), reads=[rbC], writes=[acsT[0][1]])
                            S.op("dve", lambda: V.tensor_copy(out=acsT[1][0][:, c * 128:(c + 1) * 128], in_=bD[:, 0:128]), reads=[rbD], writes=[acsT[1][1]])

                        for d in range(2):
                            (a_, ra_), (a2, ra2) = acsT[d], acs2[d]
                            S.op("dve", lambda: V.tensor_copy(out=a2[:], in_=a_[:]), reads=[ra_], writes=[ra2])
                            S.op("dve", lambda: V.tensor_tensor(out=r1_[64:128, :], in0=a_[64:128, :], in1=a2[64:128, :], op=ALU.subtract), reads=[ra_, ra2], writes=[r_r1])
                            S.op("dve", lambda: V.tensor_copy(out=a2[64:128, :], in_=r1_[64:128, :]), reads=[r_r1], writes=[ra2])

                    if ssd_upto < 4:
                        raise _Stop()
                    with Phase(S) as pf:
                        Sf, r_Sf = pf.sb([128, DIN], F32, "Sf")
                        Sbf = [pf.sb([128, DIN], BF16, "Sbf") for _ in range(2)]
                        xsT, r_xsT = pf.sb([128, 32, 128], BF16, "xsT")
                        BT, r_BT = pf.sb([128, 8, 128], BF16, "BT")
                        xs_tm, _ = pf.sb([128, DIN], BF16, "xs_tm")
                        r_xs = [Res() for _ in range(8)]
                        B_tm, _ = pf.sb([128, GN], BF16, "B_tm")
                        r_Btm = [Res() for _ in range(2)]
                        xw = [pf.sb([128, 512], BF16, "xw") for _ in range(2)]
                        S.op("dve", lambda: V.memset(Sf[:], 0.0), writes=[r_Sf])
                        S.op("dve", lambda: V.memset(Sbf[0][0][:], 0.0), writes=[Sbf[0][1]])
                        order = [16, 17] + list(range(16))
                        for oi, c in enumerate(order):
                            c0, c1 = c * 128, (c + 1) * 128
                            S.dma("sp", xsT[:], XBC[0:DIN, c0:c1].rearrange("(fc p) t -> p fc t", p=128), reads=r_XBC[0:32], writes=[r_xsT])
                            S.dma("sp", BT[:], XBC[DIN:DIN + GN, c0:c1].rearrange("(fc p) t -> p fc t", p=128), reads=r_XBC[32:40], writes=[r_BT])
                            for fb in range(10):
                                bk, rbk = S.bank()
                                pb = bk[:].bitcast(BF16)
                                for i in range(4):
                                    src = xsT[:, fb * 4 + i, :] if fb < 8 else BT[:, (fb - 8) * 4 + i, :]
                                    S.op("pe", lambda: PE.transpose(out=pb[:, i * 128:(i + 1) * 128], in_=src, identity=identb[:]),
                                         reads=[r_xsT if fb < 8 else r_BT, r_identb], writes=[rbk], inc=(i == 3))
                                dst = xs_tm[:, fb * 512:(fb + 1) * 512] if fb < 8 else B_tm[:, (fb - 8) * 512:(fb - 7) * 512]
                                rd = r_xs[fb] if fb < 8 else r_Btm[fb - 8]
                                if fb % 2 == 0:
                                    S.op("act", lambda: A.activation(out=dst, in_=pb[:, 0:512], func=AF.Copy), reads=[rbk], writes=[rd])
                                else:
                                    S.op("dve", lambda: V.tensor_copy(out=dst, in_=pb[:, 0:512]), reads=[rbk], writes=[rd])
                            S.dma("sp", XSd[c0:c1, :], xs_tm[:], reads=r_xs, writes=[r_XS[c]])
                            S.dma("sp", BTMd[c0:c1, :], B_tm[:], reads=r_Btm, writes=[r_BTM[c]])
                            cur, rcur = Sbf[oi % 2]
                            nxt, rnxt = Sbf[(oi + 1) % 2]
                            S.dma("sp", HFd[c], cur[:], reads=[rcur], writes=[r_HF[c]])
                            for g in range(8):
                                gs = slice(g * 512, (g + 1) * 512)
                                x_, rx_ = xw[g % 2]
                                S.op("dve", lambda: V.tensor_tensor(out=v3(x_[:]), in0=v3(xs_tm[:, gs]), in1=bc(dtw[:, c, g * 8:(g + 1) * 8], 8, 64), op=ALU.mult),
                                     reads=[r_xs[g], r_dtw], writes=[rx_])
                                bk, rbk = S.bank()
                                S.op("pe", lambda: PE.matmul(bk[:, :], B_tm[:, g * 128:(g + 1) * 128], x_[:], start=True, stop=True),
                                     reads=[r_Btm[g // 4], rx_], writes=[rbk])
                                S.op("dve", lambda: V.tensor_tensor(out=v3(Sf[:, gs]), in0=v3(Sf[:, gs]), in1=bc(cd[:, c, g * 8:(g + 1) * 8], 8, 64), op=ALU.mult),
                                     reads=[r_Sf, r_cd], writes=[r_Sf])
                                S.op("dve", lambda: V.tensor_tensor(out=Sf[:, gs], in0=Sf[:, gs], in1=bk[:, :], op=ALU.add), reads=[r_Sf, rbk], writes=[r_Sf])
                                S.op("act", lambda: A.activation(out=nxt[:, gs], in_=Sf[:, gs], func=AF.Copy), reads=[r_Sf], writes=[rnxt])

                    if ssd_upto < 5:
                        raise _Stop()
                    with Phase(S) as pb_:
                        Sb_, r_Sb = pb_.sb([128, DIN], F32, "Sb")
                        Sbb, r_Sbb = pb_.sb([128, DIN], BF16, "Sbb")
                        xs2 = [pb_.sb([128, DIN], BF16, "xs_tm") for _ in range(2)]
                        Bt2 = [pb_.sb([128, GN], BF16, "B_tm") for _ in range(2)]
                        BT2 = [pb_.sb([128, 8, 128], BF16, "BT") for _ in range(2)]
                        CT2 = [pb_.sb([128, 8, 128], BF16, "CT") for _ in range(2)]
                        hf, r_hf = pb_.sb([128, DIN], BF16, "hf")
                        z_tm, r_z = pb_.sb([128, DIN], BF16, "z_tm")
                        scm, r_scm = pb_.sb([128, 2, 8, 128], BF16, "scm")
                        Db = [pb_.sb([128, 512], BF16, "Db") for _ in range(4)]
                        M4 = [pb_.sb([128, 512], BF16, "M4") for _ in range(4)]
                        xw = [pb_.sb([128, 512], BF16, "xw") for _ in range(2)]
                        t1b = [pb_.sb([128, 512], F32, "t1") for _ in range(2)]
                        t2b = [pb_.sb([128, 512], F32, "t2") for _ in range(2)]
                        gyb = [pb_.sb([128, 512], F32, "gy") for _ in range(2)]
                        t3b = [pb_.sb([128, 512], F32, "t3") for _ in range(2)]
                        ss, r_ss = pb_.sb([128, 8], F32, "ss")
                        g_tm, r_gtm = pb_.sb([128, DIN], BF16, "g_tm")
                        gTst, r_gTst = pb_.sb([128, 32, 128], BF16, "gTst")
                        r_Sbg = [Res() for _ in range(8)]
                        r_Sbbg = [Res() for _ in range(8)]
                        sq2 = [pb_.sb([128, 512], F32, "sq2") for _ in range(2)]
                        S.op("dve", lambda: V.memset(Sb_[:], 0.0), writes=r_Sbg)
                        S.op("dve", lambda: V.memset(Sbb[:], 0.0), writes=r_Sbbg)
                        (bOf, rbOf), (bOb, rbOb), (bSt, rbSt), (bY, rbY) = S.banks[4], S.banks[5], S.banks[6], S.banks[7]
                        order = [17, 16] + list(range(15, -1, -1))

                        def loads_a(ci):
                            c = order[ci]
                            c0, c1 = c * 128, (c + 1) * 128
                            S.dma("sp", BT2[ci % 2][0][:], XBC[DIN:DIN + GN, c0:c1].rearrange("(fc p) t -> p fc t", p=128), reads=r_XBC[32:40], writes=[BT2[ci % 2][1]])
                            S.dma("sp", CT2[ci % 2][0][:], XBC[DIN + GN:DIN + 2 * GN, c0:c1].rearrange("(fc p) t -> p fc t", p=128), reads=r_XBC[40:48], writes=[CT2[ci % 2][1]])
                            S.dma("sp", xs2[ci % 2][0][:], XSd[c0:c1, :], reads=[r_XS[c]], writes=[xs2[ci % 2][1]])
                            S.dma("sp", Bt2[ci % 2][0][:], BTMd[c0:c1, :], reads=[r_BTM[c]], writes=[Bt2[ci % 2][1]])

                        def load_hf(ci):
                            c = order[ci]
                            S.dma("sp", hf[:], HFd[c], reads=[r_HF[c]], writes=[r_hf])

                        def load_z(ci):
                            c = order[ci]
                            S.dma("sp", z_tm[:], Zd[c * 128:(c + 1) * 128, :], reads=[r_Z[c]], writes=[r_z])

                        loads_a(0)
                        load_hf(0)
                        load_z(0)
                        for ci, c in enumerate(order):
                            c0, c1 = c * 128, (c + 1) * 128
                            (xs_tm, r_xs), (B_tm, r_Btm), (BT, r_BT), (CT, r_CT) = xs2[ci % 2], Bt2[ci % 2], BT2[ci % 2], CT2[ci % 2]
                            if ci + 1 < len(order):
                                loads_a(ci + 1)
                            for half in range(2):
                                bkS, rbkS = S.banks[half]
                                for gi in range(4):
                                    g = half * 4 + gi
                                    S.op("pe", lambda: PE.matmul(bkS[:, gi * 128:(gi + 1) * 128], BT[:, g, :], CT[:, g, :], start=True, stop=True),
                                         reads=[r_BT, r_CT], writes=[rbkS], inc=(gi == 3))
                                for d in range(2):
                                    S.op("dve", lambda: V.tensor_tensor(out=scm[:, d, half * 4:(half + 1) * 4, :], in0=bkS[:, :].rearrange("s (g l) -> s g l", l=128),
                                                                        in1=mkb[:, d, :].unsqueeze(1).to_broadcast([128, 4, 128]), op=ALU.mult),
                                         reads=[rbkS, r_mkb], writes=[r_scm])

                            def RW(k):
                                g, pair = k // 4, k % 4
                                bk, rbk = S.banks[k % 4]
                                for hh in range(2):
                                    for d in range(2):
                                        h = g * 8 + pair * 2 + hh
                                        i = hh * 2 + d
                                        S.op("pe", lambda: PE.matmul(bk[:, i * 128:(i + 1) * 128], sel2[:, h:h + 1].to_broadcast([128, 128]),
                                                                     acs2[d][0][:, c0:c1], start=True, stop=True),
                                             reads=[r_sel2, acs2[d][1]], writes=[rbk], inc=(i == 3))

                            def SA(k):
                                g, pair = k // 4, k % 4
                                h0 = g * 8 + pair * 2
                                bk, rbk = S.banks[k % 4]
                                d_, rd_ = Db[k % 4]
                                for hh in range(2):
                                    for d in range(2):
                                        i = hh * 2 + d
                                        col = d * 64 + h0 + hh
                                        S.op("act", lambda: A.activation(out=d_[:, i * 128:(i + 1) * 128], in_=bk[:, i * 128:(i + 1) * 128], func=AF.Exp,
                                                                         bias=nacsl[:, c, col:col + 1], scale=1.0),
                                             reads=[rbk, r_nacsl], writes=[rd_])

                            def SC(k):
                                g, pair = k // 4, k % 4
                                d_, rd_ = Db[k % 4]
                                m_, rm_ = M4[k % 4]
                                for a in range(2):
                                    S.op("dve", lambda: V.scalar_tensor_tensor(out=m_[:, a * 256:(a + 1) * 256].rearrange("s (d l) -> s d l", d=2),
                                                                               in0=d_[:, a * 256:(a + 1) * 256].rearrange("s (d l) -> s d l", d=2), scalar=1.0e4,
                                                                               in1=scm[:, :, g, :], op0=ALU.min, op1=ALU.mult),
                                         reads=[rd_, r_scm], writes=[rm_])
                                for hh in range(2):
                                    r = pair * 2 + hh
                                    h = g * 8 + r
                                    xh = xs_tm[:, h * 64:(h + 1) * 64]
                                    for d in range(2):
                                        i = hh * 2 + d
                                        S.op("pe", lambda: PE.matmul(bY[:, r * 64:(r + 1) * 64], m_[:, i * 128:(i + 1) * 128], xh, start=(d == 0), stop=(d == 1)),
                                             reads=[rm_, r_xs], writes=[rbY], inc=(d == 1))

                            def grp_start(g):
                                gs = slice(g * 512, (g + 1) * 512)
                                S.op("pe", lambda: PE.matmul(bOf[:, :], CT[:, g, :], hf[:, gs], start=True, stop=True), reads=[r_CT, r_hf], writes=[rbOf])
                                S.op("pe", lambda: PE.matmul(bOb[:, :], CT[:, g, :], Sbb[:, gs], start=True, stop=True), reads=[r_CT, r_Sbbg[g]], writes=[rbOb])
                                x_, rx_ = xw[g % 2]
                                S.op("pool", lambda: nc.gpsimd.tensor_tensor(out=v3(x_[:]), in0=v3(xs_tm[:, gs]),
                                                                             in1=bc(dtw[:, c, 64 + g * 8:64 + (g + 1) * 8], 8, 64), op=ALU.mult),
                                     reads=[r_xs, r_dtw], writes=[rx_])
                                S.op("pe", lambda: PE.matmul(bSt[:, :], B_tm[:, g * 128:(g + 1) * 128], x_[:], start=True, stop=True),
                                     reads=[r_Btm, rx_], writes=[rbSt])
                                t3, rt3 = t3b[g % 2]
                                S.op("pool", lambda: nc.gpsimd.tensor_tensor(out=v3(t3[:]), in0=v3(xs_tm[:, gs]), in1=bc(dsum[:, g * 8:(g + 1) * 8], 8, 64), op=ALU.mult),
                                     reads=[r_xs, r_dsum], writes=[rt3])

                            def grp_early(g):
                                gs = slice(g * 512, (g + 1) * 512)
                                (t1, rt1), (t2, rt2), (t3, rt3) = t1b[g % 2], t2b[g % 2], t3b[g % 2]
                                S.op("dve", lambda: V.tensor_tensor(out=v3(t1[:]), in0=v3(bOf[:, :]), in1=bc(ea[:, c, g * 8:(g + 1) * 8], 8, 64), op=ALU.mult),
                                     reads=[rbOf, r_ea], writes=[rt1])
                                S.op("dve", lambda: V.tensor_tensor(out=v3(t2[:]), in0=v3(bOb[:, :]), in1=bc(ea[:, c, 64 + g * 8:64 + (g + 1) * 8], 8, 64), op=ALU.mult),
                                     reads=[rbOb, r_ea], writes=[rt2])
                                S.op("pool", lambda: nc.gpsimd.tensor_tensor(out=t2[:], in0=t2[:], in1=t3[:], op=ALU.add), reads=[rt2, rt3], writes=[rt2])
                                S.op("dve", lambda: V.tensor_tensor(out=v3(Sb_[:, gs]), in0=v3(Sb_[:, gs]), in1=bc(cd[:, c, 64 + g * 8:64 + (g + 1) * 8], 8, 64), op=ALU.mult),
                                     reads=[r_Sbg[g], r_cd], writes=[r_Sbg[g]])
                                S.op("dve", lambda: V.tensor_tensor(out=Sb_[:, gs], in0=Sb_[:, gs], in1=bSt[:, :], op=ALU.add), reads=[r_Sbg[g], rbSt], writes=[r_Sbg[g]])
                                S.op("act", lambda: A.activation(out=Sbb[:, gs], in_=Sb_[:, gs], func=AF.Copy), reads=[r_Sbg[g]], writes=[r_Sbbg[g]])

                            def grp_endA(g):
                                gs = slice(g * 512, (g + 1) * 512)
                                (t1, rt1), (t2, rt2), (gy, rgy), (sq_, rsq_) = t1b[g % 2], t2b[g % 2], gyb[g % 2], sq2[g % 2]
                                S.op("dve", lambda: V.tensor_tensor(out=t1[:], in0=t1[:], in1=bY[:, :], op=ALU.add), reads=[rt1, rbY], writes=[rt1])
                                S.op("pool", lambda: nc.gpsimd.tensor_tensor(out=t1[:], in0=t1[:], in1=t2[:], op=ALU.add), reads=[rt1, rt2], writes=[rt1])
                                S.op("pool", lambda: nc.gpsimd.tensor_tensor(out=gy[:], in0=t1[:], in1=z_tm[:, gs], op=ALU.mult), reads=[rt1, r_z], writes=[rgy])
                                S.op("act", lambda: A.activation(out=sq_[:], in_=gy[:], func=AF.Square), reads=[rgy], writes=[rsq_])

                            def grp_endB(g):
                                gs = slice(g * 512, (g + 1) * 512)
                                (gy, rgy), (sq_, rsq_) = gyb[g % 2], sq2[g % 2]
                                S.op("dve", lambda: V.reduce_sum(out=ss[:, g:g + 1], in_=sq_[:], axis=AX.X), reads=[rsq_], writes=[r_ss])
                                S.op("act", lambda: A.activation(out=ss[:, g:g + 1], in_=ss[:, g:g + 1], func=AF.Ln, bias=epsc[:, 0:1], scale=1.0 / 512.0),
                                     reads=[r_ss, r_epsc], writes=[r_ss])
                                S.op("act", lambda: A.activation(out=ss[:, g:g + 1], in_=ss[:, g:g + 1], func=AF.Exp, scale=-0.5), reads=[r_ss], writes=[r_ss])
                                S.op("act", lambda: A.activation(out=g_tm[:, gs], in_=gy[:], func=AF.Identity, scale=ss[:, g:g + 1]), reads=[rgy, r_ss], writes=[r_gtm])

                            for k in range(4):
                                RW(k)
                            SA(0)
                            SA(1)
                            for k in range(32):
                                g = k // 4
                                if k % 4 == 0:
                                    grp_start(g)
                                    if g == 7 and ci + 1 < len(order):
                                        load_hf(ci + 1)
                                if k % 4 == 1:
                                    grp_early(g)
                                    if g > 0:
                                        grp_endB(g - 1)
                                if k + 2 < 32:
                                    SA(k + 2)
                                SC(k)
                                if k + 4 < 32:
                                    RW(k + 4)
                                if k % 4 == 3:
                                    grp_endA(g)
                            if ci + 1 < len(order):
                                load_z(ci + 1)
                            grp_endB(7)
                            for fb in range(8):
                                bk, rbk = S.banks[fb % 4]
                                pbv = bk[:].bitcast(BF16)
                                for i in range(4):
                                    fc = fb * 4 + i
                                    S.op("pe", lambda: PE.transpose(out=pbv[:, i * 128:(i + 1) * 128], in_=g_tm[:, fc * 128:(fc + 1) * 128], identity=identb[:]),
                                         reads=[r_gtm, r_identb], writes=[rbk], inc=(i == 3))
                                dst = gTst[:, fb * 4:(fb + 1) * 4, :]
                                if fb % 2 == 0:
                                    S.op("act", lambda: A.activation(out=dst, in_=pbv[:, 0:512].rearrange("f (i t) -> f i t", t=128), func=AF.Copy), reads=[rbk], writes=[r_gTst])
                                else:
                                    S.op("dve", lambda: V.tensor_copy(out=dst, in_=pbv[:, 0:512].rearrange("f (i t) -> f i t", t=128)), reads=[rbk], writes=[r_gTst])
                            S.dma("sp", GTd[:, c0:c1].rearrange("(fc p) t -> p fc t", p=128), gTst[:], reads=[r_gTst], writes=[r_GT[c]])

            if ssd_upto < 6:
                raise _Stop()
            with Phase(S) as po:
                gT, r_gT = po.sb([128, 32, T], BF16, "gT")
                sng, r_sng = po.sb([128, 32], F32, "sngT")
                S.dma("sp", sng[:], sng_in[:, js, :], writes=[r_sng])

                def wfix(w, rw):
                    S.op("dve", lambda: V.tensor_tensor(out=w[:], in0=w[:], in1=sng[:].unsqueeze(2).to_broadcast([128, 32, 128]), op=ALU.mult),
                         reads=[rw, r_sng], writes=[rw])
                S.dma("sp", gT[:], GTd.rearrange("(fc p) t -> p fc t", p=128), reads=r_GT, writes=[r_gT])
                linear(po, lambda col: w_out[js, :, col:col + 128], [m * 128 for m in range(KC)], 32, gT, r_gT,
                       resid_epilogue(po, lambda m, j: MOD[:, l, 32 + m, j:j + 1]), wtag="wo", wfix=wfix)

        r_UV = Res("UV")

        stage_mod()
        stage_i = 0
        for l in range(NL):
            if stage_i < upto:
                if l % 2 == 0:
                    stage_fourier(l)
                else:
                    try:
                        stage_ssd(l)
                    except _Stop:
                        pass
            stage_i += 1
            if stage_i < upto:
                stage_ffn(l)
            stage_i += 1

        if final and upto >= 2 * NL:
            norm_phase(None, None, lambda c, j: AFN[:, c, 0:1], lambda c, j: ZB[:, c, 0:1], [r_AFN, r_ZB], out_f32=out_d)
        else:
            with Phase(S) as ph:
                xb = [ph.sb([128, T], F32, "cp") for _ in range(2)]
                for m in range(KC):
                    x, rx = xb[m % 2]
                    S.dma("sp", x[:], XT[m * 128:(m + 1) * 128, :], reads=[r_XT[m]], writes=[rx])
                    S.dma("sp", out_d[m * 128:(m + 1) * 128, :], x[:], reads=[rx])
        S.barrier()
    return nc


_CONST = {}


def _consts():
    if _CONST:
        return _CONST
    bf = ml_dtypes.bfloat16
    k = np.arange(256)
    ang = 2.0 * np.pi * ((k[:, None] * k[None, :]) % 256) / 256.0
    c256 = np.cos(ang) / 16.0
    s256 = np.sin(ang) / 16.0
    cs1 = np.concatenate([c256, s256], axis=1)
    cs2c = np.concatenate([c256, -s256], axis=1)
    _CONST["cs1"] = np.ascontiguousarray(cs1.reshape(2, 128, 512).transpose(1, 0, 2)).astype(bf)
    _CONST["cs2c"] = np.ascontiguousarray(cs2c.reshape(2, 128, 512).transpose(1, 0, 2)).astype(bf)
    t = np.arange(L)
    angL = 2.0 * np.pi * ((t[:, None] * t[None, :]) % L) / float(L)
    _CONST["cl"] = (np.cos(angL) / math.sqrt(L)).astype(bf)
    _CONST["nsl"] = (-np.sin(angL) / math.sqrt(L)).astype(bf)
    _CONST["identb"] = np.eye(128, dtype=np.float32).astype(bf)
    s = np.arange(128)
    triU = (s[:, None] <= s[None, :]).astype(np.float32)
    triL = (s[:, None] >= s[None, :]).astype(np.float32)
    _CONST["cf32"] = np.ascontiguousarray(np.stack([triU, triL, triU, triL], axis=1)).astype(np.float32)
    return _CONST


def _pm(v, nchunk):
    v = np.asarray(v, dtype=np.float32)
    lead = v.shape[:-1]
    r = v.reshape(lead + (nchunk, 128))
    r = np.moveaxis(r, -1, 0)
    return np.ascontiguousarray(r)


def make_in_maps(inputs, cores):
    cst = _consts()
    f = lambda k: np.asarray(inputs[k], dtype=np.float32)
    shared = {
        "w_mod": f("w_mod"), "four_w": f("four_w"), "ssd_w_in": f("ssd_w_in"), "ssd_w_out": f("ssd_w_out"),
        "ffn_w_up": f("ffn_w_up"), "ffn_w_down": f("ffn_w_down"),
        "bmodT": _pm(f("b_mod"), 96), "gmixT": _pm(f("norm_mix_g"), KC), "gffnT": _pm(f("norm_ffn_g"), KC),
        "gfinT": _pm(f("final_g"), KC),
        "scwT": np.ascontiguousarray(_pm(f("ssd_conv_w"), 48).transpose(0, 1, 3, 2)),
        "scbT": _pm(f("ssd_conv_b"), 48),
        "dtb_bc": np.ascontiguousarray(np.broadcast_to(f("ssd_dt_bias").reshape(1, 2, 128), (128, 2, 128))),
        "alog_bc": np.ascontiguousarray(np.broadcast_to(f("ssd_a_log").reshape(1, 2, 128), (128, 2, 128))),
        "dsk_bc": np.ascontiguousarray(np.broadcast_to(f("ssd_d").reshape(1, 2, 128), (128, 2, 128))),
        "sngT": _pm(f("ssd_norm_g"), 32),
        "fcwT": np.ascontiguousarray(_pm(f("ffn_conv_w").reshape(NL, 9, DFF), 44).transpose(0, 1, 3, 2)),
        "fcbT": _pm(f("ffn_conv_b"), 44),
        "cs1": cst["cs1"], "cs2c": cst["cs2c"], "cl": cst["cl"], "nsl": cst["nsl"], "identb": cst["identb"], "cf32": cst["cf32"],
    }
    x, c, ctx, cc = f("x"), f("c"), f("ctx"), f("c_ctx")
    maps = []
    for b in cores:
        m = dict(shared)
        m["xc"] = np.ascontiguousarray(np.concatenate([x[b].T, ctx[b].T], axis=1))
        m["cv"] = np.ascontiguousarray(np.stack([c[b], cc], axis=-1).reshape(KC, 128, 2).transpose(1, 0, 2))
        maps.append(m)
    return maps


def kernel(**inputs):
    nc = build()
    maps = make_in_maps(inputs, list(range(8)))
    res = run_bass_kernel_spmd(nc, maps, core_ids=list(range(8)))
    out = np.stack([np.ascontiguousarray(res.results[b]["out"][:, :L].T) for b in range(8)], axis=0)
    return out.astype(np.float32)
```

```python
import math
from contextlib import ExitStack

import ml_dtypes
import numpy as np

import concourse.bass as bass
import concourse.mybir as mybir
from concourse.bass_utils import run_bass_kernel_spmd

F32 = mybir.dt.float32
BF16 = mybir.dt.bfloat16
AF = mybir.ActivationFunctionType
ALU = mybir.AluOpType
AX = mybir.AxisListType

D = 2048
L = 2048
LC = 256
T = L + LC
DFF = 5632
NL = 4
KC = 16
DIN = 4096
GN = 1024
H = 64
SSD_IN = 10368
STATE_COLS = DIN + GN + 2 * H
EPS = 1e-6
TT = [(0, 512), (512, 512), (1024, 512), (1536, 512), (2048, 256)]
NCH = 18
GS = 11
GPL = 34 * 66
GPW = GPL + 258


class Res:
    __slots__ = ("w", "r", "name", "excl")

    def __init__(self, name="", excl=False):
        self.w = None
        self.r = {}
        self.name = name
        self.excl = excl


class _Q:
    def __init__(self, name, eng, sem):
        self.name = name
        self.eng = eng
        self.sem = sem
        self.cnt = 0
        self.seen = {}
        self.slots = []
        self.di = 0


class Sched:
    NSLOT = 12

    def __init__(self, nc, es):
        self.nc = nc
        self.q = {}
        for name, eng in (("pe", nc.tensor), ("act", nc.scalar), ("dve", nc.vector), ("pool", nc.gpsimd), ("sp", nc.sync)):
            self.q[name] = _Q(name, eng, es.enter_context(nc.semaphore("s_" + name)))
        for qn in ("sp", "pool"):
            for i in range(self.NSLOT):
                self.q[qn].slots.append([es.enter_context(nc.semaphore(f"d_{qn}{i}")), 0, f"d_{qn}{i}"])
        self.banks = []
        self.bi = 0

    def _wait(self, q, tok):
        key, sem, val = tok
        if q.seen.get(key, 0) >= val:
            return
        q.eng.wait_ge(sem, val)
        q.seen[key] = val

    def _deps(self, q, reads, writes, skip_self, skip_same=False):
        for r in reads:
            if r.w is not None and not (skip_self and r.w[0] == q.name):
                self._wait(q, r.w)
            if r.excl:
                for tok in r.r.values():
                    if tok[0] != q.name:
                        self._wait(q, tok)
        for w in writes:
            if w.w is not None and not (skip_same and w.w[0] == q.name):
                self._wait(q, w.w)
            for tok in w.r.values():
                if not (skip_same and tok[0] == q.name):
                    self._wait(q, tok)

    def _mark(self, tok, reads, writes):
        for r in reads:
            old = r.r.get(tok[0])
            if old is None or old[2] < tok[2]:
                r.r[tok[0]] = tok
        for w in writes:
            w.w = tok
            w.r = {}

    def op(self, E, fn, reads=(), writes=(), inc=True):
        q = self.q[E]
        self._deps(q, reads, writes, E == "pe", True)
        ins = fn()
        if inc:
            ins.then_inc(q.sem, 1)
            q.cnt += 1
            tok = (q.name, q.sem, q.cnt)
        else:
            tok = (q.name, q.sem, q.cnt + 1)
        self._mark(tok, reads, writes)
        return tok

    def dma(self, Qn, out, in_, reads=(), writes=(), **kw):
        q = self.q[Qn]
        slot = q.slots[q.di % self.NSLOT]
        q.di += 1
        if slot[1] > 0:
            self._wait(q, (slot[2], slot[0], 16 * slot[1]))
        self._deps(q, reads, writes, False)
        q.eng.dma_start(out=out, in_=in_, **kw).then_inc(slot[0], 16)
        slot[1] += 1
        tok = (slot[2], slot[0], 16 * slot[1])
        self._mark(tok, reads, writes)
        return tok

    def barrier(self):
        toks = []
        for q in self.q.values():
            if q.cnt > 0:
                toks.append((q.name, q.sem, q.cnt))
            for s in q.slots:
                if s[1] > 0:
                    toks.append((s[2], s[0], 16 * s[1]))
        for q in self.q.values():
            for t in toks:
                if t[0] != q.name:
                    self._wait(q, t)

    def bank(self):
        b = self.banks[self.bi % len(self.banks)]
        self.bi += 1
        return b


class _Stop(Exception):
    pass


class Phase:
    uid = 0

    def __init__(self, S):
        self.S = S
        self.es = ExitStack()
        self.n = 0

    def __enter__(self):
        self.es.__enter__()
        return self

    def sb(self, shape, dt, name="t"):
        Phase.uid += 1
        t = self.es.enter_context(self.S.nc.sbuf_tensor(f"{name}_{Phase.uid}", list(shape), dt))
        return t, Res(name)

    def __exit__(self, *a):
        self.S.barrier()
        self.es.__exit__(None, None, None)
        return False


def build(upto=999, final=True, ssd_upto=99):
    nc = bass.Bass("TRN2", target_bir_lowering=False)

    def din(name, shape, dt=F32):
        return nc.dram_tensor(name, list(shape), dt, kind="ExternalInput").ap()

    xc_in = din("xc", [D, T])
    cv_in = din("cv", [128, KC, 2])
    w_mod = din("w_mod", [NL, D, 6 * D])
    bmod_in = din("bmodT", [128, NL, 96])
    gmix_in = din("gmixT", [128, NL, KC])
    gffn_in = din("gffnT", [128, NL, KC])
    gfin_in = din("gfinT", [128, KC])
    four_w = din("four_w", [2, D, D])
    w_in = din("ssd_w_in", [2, D, SSD_IN])
    scw_in = din("scwT", [128, 2, 48, 3])
    scb_in = din("scbT", [128, 2, 48])
    dtb_in = din("dtb_bc", [128, 2, 128])
    alog_in = din("alog_bc", [128, 2, 128])
    dsk_in = din("dsk_bc", [128, 2, 128])
    sng_in = din("sngT", [128, 2, 32])
    w_out = din("ssd_w_out", [2, DIN, D])
    w_up = din("ffn_w_up", [NL, D, 2 * DFF])
    fcw_in = din("fcwT", [128, NL, 44, 9])
    fcb_in = din("fcbT", [128, NL, 44])
    w_down = din("ffn_w_down", [NL, DFF, D])
    cs1_in = din("cs1", [128, 2, 512], BF16)
    cs2c_in = din("cs2c", [128, 2, 512], BF16)
    cl_in = din("cl", [L, L], BF16)
    nsl_in = din("nsl", [L, L], BF16)
    identb_in = din("identb", [128, 128], BF16)
    cf_in = din("cf32", [128, 4, 128])
    out_d = nc.dram_tensor("out", [D, T], F32, kind="ExternalOutput").ap()

    XT = nc.dram_tensor("XT", [D, T], F32).ap()
    UVd = nc.dram_tensor("UVd", [T, 8, 512], BF16).ap()
    XBC = nc.dram_tensor("XBC", [48 * 128, T], BF16).ap()
    Zd = nc.dram_tensor("Zd", [T, DIN], BF16).ap()
    XSd = nc.dram_tensor("XSd", [T, DIN], BF16).ap()
    BTMd = nc.dram_tensor("BTMd", [T, GN], BF16).ap()
    HFd = nc.dram_tensor("HFd", [NCH, 128, DIN], BF16).ap()
    GTd = nc.dram_tensor("GTd", [DIN, T], BF16).ap()

    with ExitStack() as es:
        S = Sched(nc, es)
        for i in range(8):
            S.banks.append((es.enter_context(nc.psum_tensor(f"bank{i}", [128, 512], F32)), Res(f"bank{i}", excl=True)))

        def psb(shape, dt, name):
            return es.enter_context(nc.sbuf_tensor("sb_" + name, list(shape), dt)), Res(name)

        ones_bf, r_ones = psb([128, 128], BF16, "ones_bf")
        ones_f, r_onesf = psb([128, 128], F32, "ones_f")
        identb, r_identb = psb([128, 128], BF16, "identb")
        cf, r_cf = psb([128, 4, 128], F32, "cf")
        mkb, r_mkb = psb([128, 2, 128], BF16, "mkb")
        sT, r_sT = psb([128, KC, 2], BF16, "sT")
        MOD, r_MOD = psb([128, NL, 96, 2], F32, "MOD")
        A1, r_A1 = psb([128, NL, KC, 2], F32, "A1")
        A2, r_A2 = psb([128, NL, KC, 2], F32, "A2")
        AFN, r_AFN = psb([128, KC, 2], F32, "AFN")
        ZB, r_ZB = psb([128, KC, 2], F32, "ZB")
        bmod, r_bmod = psb([128, NL, 96], F32, "bmod")
        gmix, r_gmix = psb([128, NL, KC], F32, "gmix")
        gffn, r_gffn = psb([128, NL, KC], F32, "gffn")
        gfin, r_gfin = psb([128, KC], F32, "gfin")
        scw, r_scw = psb([128, 2, 48, 3], F32, "scw")
        scb, r_scb = psb([128, 2, 48], F32, "scb")
        cvt, r_cvt = psb([128, KC, 2], F32, "cvt")

        V = nc.vector
        A = nc.scalar
        PE = nc.tensor
        r_XT = [Res(f"XT{m}") for m in range(KC)]

        S.op("dve", lambda: V.memset(ones_bf[:], 1.0), writes=[r_ones])
        S.op("dve", lambda: V.memset(ones_f[:], 1.0), writes=[r_onesf])
        S.op("dve", lambda: V.memset(ZB[:], 0.0), writes=[r_ZB])
        for dst, rr, src in ((identb, r_identb, identb_in), (cf, r_cf, cf_in), (bmod, r_bmod, bmod_in),
                             (gmix, r_gmix, gmix_in), (gffn, r_gffn, gffn_in), (gfin, r_gfin, gfin_in),
                             (scw, r_scw, scw_in), (scb, r_scb, scb_in),
                             (cvt, r_cvt, cv_in)):
            S.dma("sp", dst[:], src, writes=[rr])
        S.op("dve", lambda: V.tensor_copy(out=mkb[:], in_=cf[:, 2:4, :]), reads=[r_cf], writes=[r_mkb])
        S.op("act", lambda: A.activation(out=sT[:], in_=cvt[:], func=AF.Silu), reads=[r_cvt], writes=[r_sT])
        for m in range(KC):
            S.dma("sp", XT[m * 128:(m + 1) * 128, :], xc_in[m * 128:(m + 1) * 128, :], writes=[r_XT[m]])

        def stage_mod():
            with Phase(S) as ph:
                wb = [ph.sb([128, KC, 512], BF16, "modw") for _ in range(3)]
                for l in range(NL):
                    bk, rbk = S.bank()
                    for r4 in range(24):
                        w, rw = wb[(l * 24 + r4) % 3]
                        S.dma("pool", w[:], w_mod[l, :, r4 * 512:(r4 + 1) * 512].rearrange("(kc p) c -> p kc c", p=128), writes=[rw])
                        for rr in range(4):
                            r = r4 * 4 + rr
                            for kc in range(KC):
                                S.op("pe", lambda: PE.matmul(bk[:, 2 * r:2 * r + 2], w[:, kc, rr * 128:(rr + 1) * 128], sT[:, kc, :],
                                                             start=(kc == 0), stop=(kc == KC - 1)),
                                     reads=[rw, r_sT], writes=[rbk], inc=(kc == KC - 1))
                    for j in range(2):
                        S.op("dve", lambda: V.tensor_tensor(out=MOD[:, l, :, j], in0=bk[:, 0:192].rearrange("p (r j) -> p r j", j=2)[:, :, j],
                                                            in1=bmod[:, l, :], op=ALU.add),
                             reads=[rbk, r_bmod], writes=[r_MOD])
                    sq = math.sqrt(float(D))
                    for j in range(2):
                        for (AA, rA, gg, rg, off) in ((A1, r_A1, gmix, r_gmix, 16), (A2, r_A2, gffn, r_gffn, 64)):
                            S.op("dve", lambda: V.tensor_scalar(out=AA[:, l, :, j], in0=MOD[:, l, off:off + 16, j], scalar1=1.0, scalar2=sq,
                                                                op0=ALU.add, op1=ALU.mult), reads=[r_MOD], writes=[rA])
                            S.op("dve", lambda: V.tensor_tensor(out=AA[:, l, :, j], in0=AA[:, l, :, j], in1=gg[:, l, :], op=ALU.mult),
                                 reads=[rA, rg], writes=[rA])
                for j in range(2):
                    S.op("dve", lambda: V.tensor_scalar(out=AFN[:, :, j], in0=gfin[:, :], scalar1=math.sqrt(float(D)), scalar2=None, op0=ALU.mult),
                         reads=[r_gfin], writes=[r_AFN])

        def norm_phase(aT, r_aT, scale_ap, bias_ap, rs, out_f32=None):
            with Phase(S) as ph:
                xb = [ph.sb([128, T], F32, "nx") for _ in range(3)]
                sqb = [ph.sb([128, T], BF16, "nsq") for _ in range(2)]
                rstd, r_rstd = ph.sb([128, T], F32, "rstd")
                ob = [ph.sb([128, T], F32, "nout") for _ in range(2)] if out_f32 is not None else None
                bks = [S.bank() for _ in range(5)]
                for c in range(KC):
                    x, rx = xb[c % 3]
                    s, rsq = sqb[c % 2]
                    S.dma("sp", x[:], XT[c * 128:(c + 1) * 128, :], reads=[r_XT[c]], writes=[rx])
                    S.op("act", lambda: A.activation(out=s[:], in_=x[:], func=AF.Square), reads=[rx], writes=[rsq])
                    for ti, (t0, n) in enumerate(TT):
                        bk, rbk = bks[ti]
                        S.op("pe", lambda: PE.matmul(bk[:, :n], ones_bf[:], s[:, t0:t0 + n], start=(c == 0), stop=(c == KC - 1)),
                             reads=[rsq, r_ones], writes=[rbk], inc=(ti == 4))
                for ti, (t0, n) in enumerate(TT):
                    bk, rbk = bks[ti]
                    S.op("dve", lambda: V.tensor_scalar(out=rstd[:, t0:t0 + n], in0=bk[:, :n], scalar1=float(D) * EPS, scalar2=None,
                                                        op0=ALU.add), reads=[rbk], writes=[r_rstd])
                    S.op("act", lambda: A.activation(out=rstd[:, t0:t0 + n], in_=rstd[:, t0:t0 + n], func=AF.Sqrt), reads=[r_rstd], writes=[r_rstd])
                    S.op("dve", lambda: V.reciprocal(out=rstd[:, t0:t0 + n], in_=rstd[:, t0:t0 + n]), reads=[r_rstd], writes=[r_rstd])
                for c in range(KC):
                    x, rx = xb[(KC + c) % 3]
                    S.dma("sp", x[:], XT[c * 128:(c + 1) * 128, :], reads=[r_XT[c]], writes=[rx])
                    S.op("dve", lambda: V.tensor_tensor(out=x[:], in0=x[:], in1=rstd[:], op=ALU.mult), reads=[rx, r_rstd], writes=[rx])
                    if out_f32 is None:
                        for j, (a, b) in enumerate(((0, L), (L, T))):
                            S.op("act", lambda: A.activation(out=aT[:, c, a:b], in_=x[:, a:b], func=AF.Identity,
                                                             bias=bias_ap(c, j), scale=scale_ap(c, j)),
                                 reads=[rx] + rs, writes=[r_aT[c]])
                    else:
                        o, ro = ob[c % 2]
                        S.op("act", lambda: A.activation(out=o[:], in_=x[:], func=AF.Identity, bias=bias_ap(c, 0), scale=scale_ap(c, 0)),
                             reads=[rx] + rs, writes=[ro])
                        S.dma("sp", out_f32[c * 128:(c + 1) * 128, :], o[:], reads=[ro])

        def linear(ph, Wsrc, cols, kcn, inT, r_in, epilogue, wtag="lw", nbuf=3, wfix=None):
            wb = [ph.sb([128, kcn, 128], BF16, wtag) for _ in range(nbuf)]
            for mi, col in enumerate(cols):
                w, rw = wb[mi % nbuf]
                S.dma("pool", w[:], Wsrc(col).rearrange("(kc p) c -> p kc c", p=128), writes=[rw])
                if wfix is not None:
                    wfix(w, rw)
                bl = []
                for ti, (t0, n) in enumerate(TT):
                    bk, rbk = S.bank()
                    for kc in range(kcn):
                        S.op("pe", lambda: PE.matmul(bk[:, :n], w[:, kc, :], inT[:, kc, t0:t0 + n], start=(kc == 0), stop=(kc == kcn - 1)),
                             reads=[rw, r_in[kc] if isinstance(r_in, list) else r_in], writes=[rbk], inc=(kc == kcn - 1))
                    bl.append((bk, rbk))
                epilogue(mi, col, bl)

        def resid_epilogue(ph, gate_ap):
            xt = [ph.sb([128, T], F32, "rxt") for _ in range(2)]

            def ep(mi, col, bl):
                m = col // 128
                x, rx = xt[mi % 2]
                S.dma("sp", x[:], XT[m * 128:(m + 1) * 128, :], reads=[r_XT[m]], writes=[rx])
                for ti, (t0, n) in enumerate(TT):
                    bk, rbk = bl[ti]
                    j = 0 if t0 < L else 1
                    S.op("dve", lambda: V.scalar_tensor_tensor(out=x[:, t0:t0 + n], in0=bk[:, :n], scalar=gate_ap(m, j), in1=x[:, t0:t0 + n],
                                                               op0=ALU.mult, op1=ALU.add), reads=[rbk, rx, r_MOD], writes=[rx])
                S.dma("sp", XT[m * 128:(m + 1) * 128, :], x[:], reads=[rx], writes=[r_XT[m]])
            return ep

        def stage_ffn(l):
            with Phase(S) as pho:
                aT, _ = pho.sb([128, KC, T], BF16, "aT")
                r_aT = [Res() for _ in range(KC)]
                norm_phase(aT, r_aT, lambda c, j: A2[:, l, c, j:j + 1], lambda c, j: MOD[:, l, 48 + c, j:j + 1], [r_A2, r_MOD])
                hT, _ = pho.sb([128, GS, T], BF16, "hT")
                fcw, r_fcw = pho.sb([128, 44, 9], F32, "fcw")
                fcb, r_fcb = pho.sb([128, 44], F32, "fcb")
                S.dma("sp", fcw[:], fcw_in[:, l, :, :], writes=[r_fcw])
                S.dma("sp", fcb[:], fcb_in[:, l, :], writes=[r_fcb])
                groups = [list(range(g0, min(g0 + GS, 44))) for g0 in range(0, 44, GS)]
                for grp in groups:
                    r_h = [Res() for _ in grp]
                    with Phase(S) as ph:
                        gp = [ph.sb([128, GPW], F32, "gp") for _ in range(2)]
                        acc = [ph.sb([128, T], F32, "acc") for _ in range(1)]
                        vb = [ph.sb([128, T], F32, "vb") for _ in range(2)]
                        wg = [ph.sb([128, KC, 128], BF16, "wg") for _ in range(2)]
                        wv = [ph.sb([128, KC, 128], BF16, "wv") for _ in range(2)]
                        for g_, rg_ in gp:
                            S.op("dve", lambda: V.memset(g_[:], 0.0), writes=[rg_])
                        def u_vars(ji):
                            return gp[ji % 2], acc[0], vb[ji % 2], wg[ji % 2], wv[ji % 2]

                        def u_pe(ji, j):
                            (g_, rg_), (ac, rac), (vv, rvv), (wgt, rwg), (wvt, rwv) = u_vars(ji)
                            S.dma("pool", wgt[:], w_up[l, :, j * 128:(j + 1) * 128].rearrange("(kc p) c -> p kc c", p=128), writes=[rwg])
                            S.dma("pool", wvt[:], w_up[l, :, DFF + j * 128:DFF + (j + 1) * 128].rearrange("(kc p) c -> p kc c", p=128), writes=[rwv])
                            glat = g_[:, 0:GPL].rearrange("p (r c) -> p r c", c=66)
                            for ti, (t0, n) in enumerate(TT):
                                bg, rbg = S.bank()
                                for kc in range(KC):
                                    S.op("pe", lambda: PE.matmul(bg[:, :n], wgt[:, kc, :], aT[:, kc, t0:t0 + n], start=(kc == 0), stop=(kc == KC - 1)),
                                         reads=[rwg, r_aT[kc]], writes=[rbg], inc=(kc == KC - 1))
                                if t0 < L:
                                    S.op("act", lambda: A.activation(out=glat[:, 1 + 8 * ti:9 + 8 * ti, 1:65],
                                                                     in_=bg[:, :512].rearrange("p (r c) -> p r c", c=64), func=AF.Copy),
                                         reads=[rbg], writes=[rg_])
                                else:
                                    S.op("act", lambda: A.activation(out=g_[:, GPL + 1:GPL + 257], in_=bg[:, :256], func=AF.Copy),
                                         reads=[rbg], writes=[rg_])
                                bv, rbv = S.bank()
                                for kc in range(KC):
                                    S.op("pe", lambda: PE.matmul(bv[:, :n], wvt[:, kc, :], aT[:, kc, t0:t0 + n], start=(kc == 0), stop=(kc == KC - 1)),
                                         reads=[rwv, r_aT[kc]], writes=[rbv], inc=(kc == KC - 1))
                                S.op("act", lambda: A.activation(out=vv[:, t0:t0 + n], in_=bv[:, :n], func=AF.Copy), reads=[rbv], writes=[rvv])

                        def u_post(ji, j):
                            (g_, rg_), (ac, rac), (vv, rvv), (wgt, rwg), (wvt, rwv) = u_vars(ji)
                            glat = g_[:, 0:GPL].rearrange("p (r c) -> p r c", c=66)
                            alat = ac[:, 0:L].rearrange("p (r c) -> p r c", c=64)
                            first = True
                            for di in range(3):
                                for dj in range(3):
                                    wsc = fcw[:, j, di * 3 + dj:di * 3 + dj + 1]
                                    src = glat[:, di:di + 32, dj:dj + 64]
                                    if first:
                                        S.op("dve", lambda: V.tensor_scalar(out=alat, in0=src, scalar1=wsc, scalar2=fcb[:, j:j + 1],
                                                                            op0=ALU.mult, op1=ALU.add), reads=[rg_, r_fcw, r_fcb], writes=[rac])
                                        first = False
                                    else:
                                        S.op("dve", lambda: V.scalar_tensor_tensor(out=alat, in0=src, scalar=wsc, in1=alat, op0=ALU.mult, op1=ALU.add),
                                             reads=[rg_, rac, r_fcw], writes=[rac])
                            for dj in range(3):
                                wsc = fcw[:, j, 3 + dj:3 + dj + 1]
                                src = g_[:, GPL + dj:GPL + dj + 256]
                                if dj == 0:
                                    S.op("dve", lambda: V.tensor_scalar(out=ac[:, L:T], in0=src, scalar1=wsc, scalar2=fcb[:, j:j + 1],
                                                                        op0=ALU.mult, op1=ALU.add), reads=[rg_, r_fcw, r_fcb], writes=[rac])
                                else:
                                    S.op("dve", lambda: V.scalar_tensor_tensor(out=ac[:, L:T], in0=src, scalar=wsc, in1=ac[:, L:T], op0=ALU.mult, op1=ALU.add),
                                         reads=[rg_, rac, r_fcw], writes=[rac])
                            S.op("act", lambda: A.activation(out=ac[:], in_=ac[:], func=AF.Silu), reads=[rac], writes=[rac])
                            S.op("dve", lambda: V.tensor_tensor(out=hT[:, ji, :], in0=ac[:], in1=vv[:], op=ALU.mult), reads=[rac, rvv], writes=[r_h[ji]])

                        for ji, j in enumerate(grp):
                            u_pe(ji, j)
                            if ji > 0:
                                u_post(ji - 1, grp[ji - 1])
                        u_post(len(grp) - 1, grp[-1])
                    with Phase(S) as ph:
                        linear(ph, lambda col: w_down[l, grp[0] * 128:(grp[-1] + 1) * 128, col:col + 128], [m * 128 for m in range(KC)], len(grp),
                               hT, r_h, resid_epilogue(ph, lambda m, j: MOD[:, l, 80 + m, j:j + 1]), wtag="wd")

        def stage_fourier(l):
            jf = l // 2
            with Phase(S) as pho:
                fT, _ = pho.sb([128, KC, T], BF16, "fT")
                r_fT = [Res() for _ in range(KC)]
                with Phase(S) as ph1:
                    aT, _ = ph1.sb([128, KC, T], BF16, "aT")
                    r_aT = [Res() for _ in range(KC)]
                    norm_phase(aT, r_aT, lambda c, j: A1[:, l, c, j:j + 1], lambda c, j: MOD[:, l, c, j:j + 1], [r_A1, r_MOD])
                    cs1, r_cs1 = ph1.sb([128, 2, 512], BF16, "cs1")
                    S.dma("sp", cs1[:], cs1_in, writes=[r_cs1])
                    st = [ph1.sb([128, 8, 512], BF16, "uvst") for _ in range(2)]
                    for tt in range(NCH):
                        s_, rs_ = st[tt % 2]
                        for g in range(8):
                            bk, rbk = S.bank()
                            for kc in range(2):
                                S.op("pe", lambda: PE.matmul(bk[:, :], aT[:, 2 * g + kc, tt * 128:(tt + 1) * 128], cs1[:, kc, :], start=(kc == 0), stop=(kc == 1)),
                                     reads=[r_aT[2 * g + kc], r_cs1], writes=[rbk], inc=(kc == 1))
                            if g % 2 == 0:
                                S.op("act", lambda: A.activation(out=s_[:, g, :], in_=bk[:, :], func=AF.Copy), reads=[rbk], writes=[rs_])
                            else:
                                S.op("dve", lambda: V.tensor_copy(out=s_[:, g, :], in_=bk[:, :]), reads=[rbk], writes=[rs_])
                        S.dma("sp", UVd[tt * 128:(tt + 1) * 128, :, :], s_[:], reads=[rs_], writes=[r_UV])
                with Phase(S) as ph2:
                    tb = [(ph2.sb([128, KC, 512], BF16, "cl"), ph2.sb([128, KC, 512], BF16, "nsl")) for _ in range(2)]
                    uv = [ph2.sb([128, KC, 512], BF16, "uvg") for _ in range(2)]
                    cnt = 0
                    for tp in range(4):
                        (clt, rcl), (nst, rns) = tb[tp % 2]
                        S.dma("sp", clt[:], cl_in[:, tp * 512:(tp + 1) * 512].rearrange("(tt p) c -> p tt c", p=128), writes=[rcl])
                        S.dma("sp", nst[:], nsl_in[:, tp * 512:(tp + 1) * 512].rearrange("(tt p) c -> p tt c", p=128), writes=[rns])
                        for g in range(8):
                            u_, ru_ = uv[cnt % 2]
                            cnt += 1
                            S.dma("sp", u_[:], UVd[0:L, g, :].rearrange("(tt p) c -> p tt c", p=128), reads=[r_UV], writes=[ru_])
                            for hh in range(2):
                                bk, rbk = S.bank()
                                for tt in range(KC):
                                    S.op("pe", lambda: PE.matmul(bk[:, :], u_[:, tt, hh * 128:(hh + 1) * 128], clt[:, tt, :], start=(tt == 0), stop=False),
                                         reads=[ru_, rcl], writes=[rbk], inc=False)
                                    S.op("pe", lambda: PE.matmul(bk[:, :], u_[:, tt, 256 + hh * 128:256 + (hh + 1) * 128], nst[:, tt, :], start=False, stop=(tt == KC - 1)),
                                         reads=[ru_, rns], writes=[rbk], inc=(tt == KC - 1))
                                cp = 2 * g + hh
                                if hh == 0:
                                    S.op("act", lambda: A.activation(out=fT[:, cp, tp * 512:(tp + 1) * 512], in_=bk[:, :], func=AF.Copy), reads=[rbk], writes=[r_fT[cp]])
                                else:
                                    S.op("dve", lambda: V.tensor_copy(out=fT[:, cp, tp * 512:(tp + 1) * 512], in_=bk[:, :]), reads=[rbk], writes=[r_fT[cp]])
                    c2, r_c2 = ph2.sb([128, 2, 512], BF16, "cs2c")
                    S.dma("sp", c2[:], cs2c_in, writes=[r_c2])
                    uc, r_uc = ph2.sb([128, 2, 8, 512], BF16, "uvc")
                    for tt in range(2):
                        S.dma("sp", uc[:, tt, :, :], UVd[L + tt * 128:L + (tt + 1) * 128, :, :], reads=[r_UV], writes=[r_uc])
                    for cp in range(KC):
                        g, hh = cp // 2, cp % 2
                        bk, rbk = S.bank()
                        for tt in range(2):
                            S.op("pe", lambda: PE.matmul(bk[:, :256], uc[:, tt, g, hh * 128:(hh + 1) * 128], c2[:, tt, 0:256], start=(tt == 0), stop=False),
                                 reads=[r_uc, r_c2], writes=[rbk], inc=False)
                            S.op("pe", lambda: PE.matmul(bk[:, :256], uc[:, tt, g, 256 + hh * 128:256 + (hh + 1) * 128], c2[:, tt, 256:512], start=False, stop=(tt == 1)),
                                 reads=[r_uc, r_c2], writes=[rbk], inc=(tt == 1))
                        S.op("act", lambda: A.activation(out=fT[:, cp, L:T], in_=bk[:, :256], func=AF.Copy), reads=[rbk], writes=[r_fT[cp]])
                with Phase(S) as ph3:
                    linear(ph3, lambda col: four_w[jf, :, col:col + 128], [m * 128 for m in range(KC)], KC, fT, r_fT,
                           resid_epilogue(ph3, lambda m, j: MOD[:, l, 32 + m, j:j + 1]), wtag="fw")


        def stage_ssd(l):
            js = l // 2
            ZC0 = STATE_COLS + GN
            cols_xbc = [i * 128 for i in range(32)] + [DIN + i * 128 for i in range(8)] + [STATE_COLS + i * 128 for i in range(8)]
            r_XBC = [Res() for _ in range(48)]
            r_Z = [Res() for _ in range(NCH)]
            r_XS = [Res() for _ in range(NCH)]
            r_BTM = [Res() for _ in range(NCH)]
            r_HF = [Res() for _ in range(NCH)]
            r_GT = [Res() for _ in range(NCH)]

            def bc(ap, n0, n1):
                return ap.unsqueeze(2).to_broadcast([128, n0, n1])

            def v3(ap, n1=64):
                return ap.rearrange("s (h p) -> s h p", p=n1)

            with Phase(S) as pho:
                dt_tm, r_dt = pho.sb([128, NCH, 128], F32, "dt_tm")
                with Phase(S) as ph1:
                    aT, _ = ph1.sb([128, KC, T], BF16, "aT")
                    r_aT = [Res() for _ in range(KC)]
                    norm_phase(aT, r_aT, lambda c, j: A1[:, l, c, j:j + 1], lambda c, j: MOD[:, l, c, j:j + 1], [r_A1, r_MOD])
                    with Phase(S) as ph:
                        P1 = [ph.sb([128, 2308], F32, "p1") for _ in range(2)]
                        acc = [ph.sb([128, T], F32, "cacc") for _ in range(2)]
                        stg = [ph.sb([128, T], BF16, "cst") for _ in range(2)]
                        for p_, rp_ in P1:
                            S.op("dve", lambda: V.memset(p_[:], 0.0), writes=[rp_])

                        def ep_conv(mi, col, bl):
                            p_, rp_ = P1[mi % 2]
                            ac, rac = acc[mi % 2]
                            st_, rst = stg[mi % 2]
                            for ti, (t0, n) in enumerate(TT):
                                bk, rbk = bl[ti]
                                dst = p_[:, 1 + t0:1 + t0 + n] if t0 < L else p_[:, 2051:2307]
                                S.op("act", lambda: A.activation(out=dst, in_=bk[:, :n], func=AF.Copy), reads=[rbk], writes=[rp_])
                            for (o0, o1, base) in ((0, L, 0), (L, T, 2050)):
                                wd_ = o1 - o0
                                S.op("dve", lambda: V.tensor_scalar(out=ac[:, o0:o1], in0=p_[:, base:base + wd_], scalar1=scw[:, js, mi, 0:1],
                                                                    scalar2=scb[:, js, mi:mi + 1], op0=ALU.mult, op1=ALU.add),
                                     reads=[rp_, r_scw, r_scb], writes=[rac])
                                for k in (1, 2):
                                    S.op("dve", lambda: V.scalar_tensor_tensor(out=ac[:, o0:o1], in0=p_[:, base + k:base + k + wd_], scalar=scw[:, js, mi, k:k + 1],
                                                                               in1=ac[:, o0:o1], op0=ALU.mult, op1=ALU.add),
                                         reads=[rp_, rac, r_scw], writes=[rac])
                            S.op("act", lambda: A.activation(out=st_[:], in_=ac[:], func=AF.Silu), reads=[rac], writes=[rst])
                            S.dma("sp", XBC[mi * 128:(mi + 1) * 128, :], st_[:], reads=[rst], writes=[r_XBC[mi]])

                        linear(ph, lambda col: w_in[js, :, col:col + 128], cols_xbc, KC, aT, r_aT, ep_conv, wtag="win")
                    if ssd_upto < 2:
                        raise _Stop()
                    with Phase(S) as ph:
                        wz = [ph.sb([128, KC, 512], BF16, "wz") for _ in range(2)]
                        zst = [ph.sb([128, 512], BF16, "zst") for _ in range(4)]
                        k = 0
                        for ct in range(8):
                            w, rw = wz[ct % 2]
                            S.dma("pool", w[:], w_in[js, :, ZC0 + ct * 512:ZC0 + (ct + 1) * 512].rearrange("(kc p) c -> p kc c", p=128), writes=[rw])
                            for tt in range(NCH):
                                bk, rbk = S.bank()
                                for kc in range(KC):
                                    S.op("pe", lambda: PE.matmul(bk[:, :], aT[:, kc, tt * 128:(tt + 1) * 128], w[:, kc, :], start=(kc == 0), stop=(kc == KC - 1)),
                                         reads=[rw, r_aT[kc]], writes=[rbk], inc=(kc == KC - 1))
                                z_, rz_ = zst[k % 4]
                                k += 1
                                S.op("act", lambda: A.activation(out=z_[:], in_=bk[:, :], func=AF.Silu), reads=[rbk], writes=[rz_])
                                S.dma("sp", Zd[tt * 128:(tt + 1) * 128, ct * 512:(ct + 1) * 512], z_[:], reads=[rz_], writes=[r_Z[tt]])
                        wdt, rwdt = ph.sb([128, KC, 128], BF16, "wdt")
                        dtb, rdtb = ph.sb([128, 128], F32, "dtb")
                        S.dma("sp", dtb[:], dtb_in[:, js, :], writes=[rdtb])
                        S.dma("pool", wdt[:], w_in[js, :, DIN + GN:DIN + GN + 128].rearrange("(kc p) c -> p kc c", p=128), writes=[rwdt])
                        tmp = [ph.sb([128, 128], F32, "dtt") for _ in range(2)]
                        for tt in range(NCH):
                            bk, rbk = S.bank()
                            for kc in range(KC):
                                S.op("pe", lambda: PE.matmul(bk[:, 0:128], aT[:, kc, tt * 128:(tt + 1) * 128], wdt[:, kc, :], start=(kc == 0), stop=(kc == KC - 1)),
                                     reads=[rwdt, r_aT[kc]], writes=[rbk], inc=(kc == KC - 1))
                            t_, rt_ = tmp[tt % 2]
                            S.op("dve", lambda: V.tensor_tensor(out=t_[:], in0=bk[:, 0:128], in1=dtb[:], op=ALU.add), reads=[rbk, rdtb], writes=[rt_])
                            S.op("act", lambda: A.activation(out=t_[:], in_=t_[:], func=AF.Exp), reads=[rt_], writes=[rt_])
                            S.op("act", lambda: A.activation(out=dt_tm[:, tt, :], in_=t_[:], func=AF.Ln, bias=ones_f[:, 0:1]), reads=[rt_, r_onesf], writes=[r_dt])

                if ssd_upto < 3:
                    raise _Stop()
                with Phase(S) as pd:
                    nacsl, r_nacsl = pd.sb([128, NCH, 128], F32, "nacsl")
                    ea, r_ea = pd.sb([128, NCH, 128], F32, "ea")
                    dtw, r_dtw = pd.sb([128, NCH, 128], F32, "dtw")
                    cd, r_cd = pd.sb([128, NCH, 128], F32, "cd")
                    acs2 = [pd.sb([128, T], BF16, "acs2") for _ in range(2)]
                    sel2, r_sel2 = pd.sb([128, 64], BF16, "sel2")
                    S.op("dve", lambda: V.tensor_copy(out=sel2[0:64, :], in_=identb[0:64, 0:64]), reads=[r_identb], writes=[r_sel2])
                    S.op("dve", lambda: V.tensor_copy(out=sel2[64:128, :], in_=identb[64:128, 64:128]), reads=[r_identb], writes=[r_sel2])
                    epsc, r_epsc = pd.sb([128, 1], F32, "epsc")
                    S.op("dve", lambda: V.memset(epsc[:], EPS), writes=[r_epsc])
                    abc, r_abc = pd.sb([128, 128], F32, "abc")
                    dsk, r_dsk = pd.sb([128, 128], F32, "dsk")
                    dsum, r_dsum = pd.sb([128, 64], F32, "dsum")
                    identf, r_identf = pd.sb([128, 128], F32, "identf")
                    S.op("dve", lambda: V.tensor_tensor(out=identf[:], in0=cf[:, 0, :], in1=cf[:, 1, :], op=ALU.mult), reads=[r_cf], writes=[r_identf])
                    S.dma("sp", abc[:], alog_in[:, js, :], writes=[r_abc])
                    S.dma("sp", dsk[:], dsk_in[:, js, :], writes=[r_dsk])
                    S.op("act", lambda: A.activation(out=abc[:], in_=abc[:], func=AF.Exp), reads=[r_abc], writes=[r_abc])
                    S.op("dve", lambda: V.tensor_scalar(out=abc[:], in0=abc[:], scalar1=-1.0, scalar2=None, op0=ALU.mult), reads=[r_abc], writes=[r_abc])
                    S.op("dve", lambda: V.tensor_tensor(out=dsum[:], in0=dsk[:, 0:64], in1=dsk[:, 64:128], op=ALU.add), reads=[r_dsk], writes=[r_dsum])
                    with Phase(S) as pp:
                        dta, r_dta = pp.sb([128, NCH, 128], F32, "dta")
                        nacs, r_nacs = pp.sb([128, NCH, 128], F32, "nacs")
                        lnd = [pp.sb([128, 128], F32, "lnd") for _ in range(2)]
                        acsT = [pp.sb([128, T], F32, "acsT") for _ in range(2)]
                        r1_, r_r1 = pp.sb([128, T], F32, "r1")
                        dup = [pp.sb([128, 2, 128], F32, "dup") for _ in range(2)]
                        tw = [pp.sb([128, 128], F32, "tw") for _ in range(2)]
                        for c in range(NCH):
                            S.op("dve", lambda: V.tensor_tensor(out=dta[:, c, :], in0=dt_tm[:, c, :], in1=abc[:], op=ALU.mult), reads=[r_dt, r_abc], writes=[r_dta])
                            (bA, rbA), (bB, rbB), (bC, rbC), (bD, rbD) = S.bank(), S.bank(), S.bank(), S.bank()
                            du, rdu = dup[c % 2]
                            for d in range(2):
                                S.op("dve", lambda: V.tensor_copy(out=du[:, d, :].rearrange("s (r h) -> s r h", r=2),
                                                                  in_=dta[:, c, d * 64:(d + 1) * 64].unsqueeze(1).to_broadcast([128, 2, 64])),
                                     reads=[r_dta], writes=[rdu])
                            S.op("pe", lambda: PE.matmul(bA[:, 0:64], cf[:, 0, :], dta[:, c, 0:64], start=True, stop=True), reads=[r_cf, r_dta], writes=[rbA], inc=False)
                            S.op("pe", lambda: PE.matmul(bA[:, 64:128], cf[:, 1, :], dta[:, c, 64:128], start=True, stop=True), reads=[r_cf, r_dta], writes=[rbA], inc=False)
                            S.op("pe", lambda: PE.matmul(bB[:, 0:128], ones_f[:], dta[:, c, :], start=True, stop=True), reads=[r_onesf, r_dta], writes=[rbB], inc=False)
                            S.op("pe", lambda: PE.matmul(bC[:, 0:128], du[:, 0, :], cf[:, 0, :], start=True, stop=True), reads=[r_cf, rdu], writes=[rbC], inc=False)
                            S.op("pe", lambda: PE.matmul(bD[:, 0:128], du[:, 1, :], cf[:, 1, :], start=True, stop=True), reads=[r_cf, rdu], writes=[rbD], inc=True)
                            S.op("dve", lambda: V.tensor_scalar(out=nacs[:, c, :], in0=bA[:, 0:128], scalar1=-1.0, scalar2=None, op0=ALU.mult), reads=[rbA], writes=[r_nacs])
                            S.op("act", lambda: A.activation(out=ea[:, c, :], in_=bA[:, 0:128], func=AF.Exp), reads=[rbA], writes=[r_ea])
                            ld_, rld_ = lnd[c % 2]
                            S.op("act", lambda: A.activation(out=ld_[:], in_=dt_tm[:, c, :], func=AF.Ln), reads=[r_dt], writes=[rld_])
                            S.op("dve", lambda: V.tensor_tensor(out=nacsl[:, c, :], in0=nacs[:, c, :], in1=ld_[:], op=ALU.add), reads=[r_nacs, rld_], writes=[r_nacsl])
                            t_, rt_ = tw[c % 2]
                            S.op("dve", lambda: V.tensor_tensor(out=t_[:], in0=bB[:, 0:128], in1=nacs[:, c, :], op=ALU.add), reads=[rbB, r_nacs], writes=[rt_])
                            S.op("act", lambda: A.activation(out=t_[:], in_=t_[:], func=AF.Exp), reads=[rt_], writes=[rt_])
                            S.op("dve", lambda: V.tensor_tensor(out=dtw[:, c, :], in0=dt_tm[:, c, :], in1=t_[:], op=ALU.mult), reads=[r_dt, rt_], writes=[r_dtw])
                            S.op("act", lambda: A.activation(out=cd[:, c, :], in_=bB[:, 0:128], func=AF.Exp), reads=[rbB], writes=[r_cd])
                            S.op("act", lambda: A.activation(out=acsT[0][0][:, c * 128:(c + 1) * 128], in_=bC[:, 0:128], func=AF.Copy), reads=[rbC], writes=[acsT[0][1]])
                            S.op("dve", lambda: V.tensor_copy(out=acsT[1][0][:, c * 128:(c + 1) * 128], in_=bD[:, 0:128]), reads=[rbD], writes=[acsT[1][1]])

                        for d in range(2):
                            (a_, ra_), (a2, ra2) = acsT[d], acs2[d]
                            S.op("dve", lambda: V.tensor_copy(out=a2[:], in_=a_[:]), reads=[ra_], writes=[ra2])
                            S.op("dve", lambda: V.tensor_tensor(out=r1_[64:128, :], in0=a_[64:128, :], in1=a2[64:128, :], op=ALU.subtract), reads=[ra_, ra2], writes=[r_r1])
                            S.op("dve", lambda: V.tensor_copy(out=a2[64:128, :], in_=r1_[64:128, :]), reads=[r_r1], writes=[ra2])

                    if ssd_upto < 4:
                        raise _Stop()
                    with Phase(S) as pf:
                        Sf, r_Sf = pf.sb([128, DIN], F32, "Sf")
                        Sbf = [pf.sb([128, DIN], BF16, "Sbf") for _ in range(2)]
                        xsT, r_xsT = pf.sb([128, 32, 128], BF16, "xsT")
                        BT, r_BT = pf.sb([128, 8, 128], BF16, "BT")
                        xs_tm, _ = pf.sb([128, DIN], BF16, "xs_tm")
                        r_xs = [Res() for _ in range(8)]
                        B_tm, _ = pf.sb([128, GN], BF16, "B_tm")
                        r_Btm = [Res() for _ in range(2)]
                        xw = [pf.sb([128, 512], BF16, "xw") for _ in range(2)]
                        S.op("dve", lambda: V.memset(Sf[:], 0.0), writes=[r_Sf])
                        S.op("dve", lambda: V.memset(Sbf[0][0][:], 0.0), writes=[Sbf[0][1]])
                        order = [16, 17] + list(range(16))
                        for oi, c in enumerate(order):
                            c0, c1 = c * 128, (c + 1) * 128
                            S.dma("sp", xsT[:], XBC[0:DIN, c0:c1].rearrange("(fc p) t -> p fc t", p=128), reads=r_XBC[0:32], writes=[r_xsT])
                            S.dma("sp", BT[:], XBC[DIN:DIN + GN, c0:c1].rearrange("(fc p) t -> p fc t", p=128), reads=r_XBC[32:40], writes=[r_BT])
                            for fb in range(10):
                                bk, rbk = S.bank()
                                pb = bk[:].bitcast(BF16)
                                for i in range(4):
                                    src = xsT[:, fb * 4 + i, :] if fb < 8 else BT[:, (fb - 8) * 4 + i, :]
                                    S.op("pe", lambda: PE.transpose(out=pb[:, i * 128:(i + 1) * 128], in_=src, identity=identb[:]),
                                         reads=[r_xsT if fb < 8 else r_BT, r_identb], writes=[rbk], inc=(i == 3))
                                dst = xs_tm[:, fb * 512:(fb + 1) * 512] if fb < 8 else B_tm[:, (fb - 8) * 512:(fb - 7) * 512]
                                rd = r_xs[fb] if fb < 8 else r_Btm[fb - 8]
                                if fb % 2 == 0:
                                    S.op("act", lambda: A.activation(out=dst, in_=pb[:, 0:512], func=AF.Copy), reads=[rbk], writes=[rd])
                                else:
                                    S.op("dve", lambda: V.tensor_copy(out=dst, in_=pb[:, 0:512]), reads=[rbk], writes=[rd])
                            S.dma("sp", XSd[c0:c1, :], xs_tm[:], reads=r_xs, writes=[r_XS[c]])
                            S.dma("sp", BTMd[c0:c1, :], B_tm[:], reads=r_Btm, writes=[r_BTM[c]])
                            cur, rcur = Sbf[oi % 2]
                            nxt, rnxt = Sbf[(oi + 1) % 2]
                            S.dma("sp", HFd[c], cur[:], reads=[rcur], writes=[r_HF[c]])
                            for g in range(8):
                                gs = slice(g * 512, (g + 1) * 512)
                                x_, rx_ = xw[g % 2]
                                S.op("dve", lambda: V.tensor_tensor(out=v3(x_[:]), in0=v3(xs_tm[:, gs]), in1=bc(dtw[:, c, g * 8:(g + 1) * 8], 8, 64), op=ALU.mult),
                                     reads=[r_xs[g], r_dtw], writes=[rx_])
                                bk, rbk = S.bank()
                                S.op("pe", lambda: PE.matmul(bk[:, :], B_tm[:, g * 128:(g + 1) * 128], x_[:], start=True, stop=True),
                                     reads=[r_Btm[g // 4], rx_], writes=[rbk])
                                S.op("dve", lambda: V.tensor_tensor(out=v3(Sf[:, gs]), in0=v3(Sf[:, gs]), in1=bc(cd[:, c, g * 8:(g + 1) * 8], 8, 64), op=ALU.mult),
                                     reads=[r_Sf, r_cd], writes=[r_Sf])
                                S.op("dve", lambda: V.tensor_tensor(out=Sf[:, gs], in0=Sf[:, gs], in1=bk[:, :], op=ALU.add), reads=[r_Sf, rbk], writes=[r_Sf])
                                S.op("act", lambda: A.activation(out=nxt[:, gs], in_=Sf[:, gs], func=AF.Copy), reads=[r_Sf], writes=[rnxt])

                    if ssd_upto < 5:
                        raise _Stop()
                    with Phase(S) as pb_:
                        Sb_, r_Sb = pb_.sb([128, DIN], F32, "Sb")
                        Sbb, r_Sbb = pb_.sb([128, DIN], BF16, "Sbb")
                        xs2 = [pb_.sb([128, DIN], BF16, "xs_tm") for _ in range(2)]
                        Bt2 = [pb_.sb([128, GN], BF16, "B_tm") for _ in range(2)]
                        BT2 = [pb_.sb([128, 8, 128], BF16, "BT") for _ in range(2)]
                        CT2 = [pb_.sb([128, 8, 128], BF16, "CT") for _ in range(2)]
                        hf, r_hf = pb_.sb([128, DIN], BF16, "hf")
                        z_tm, r_z = pb_.sb([128, DIN], BF16, "z_tm")
                        scm, r_scm = pb_.sb([128, 2, 8, 128], BF16, "scm")
                        Db = [pb_.sb([128, 512], BF16, "Db") for _ in range(4)]
                        M4 = [pb_.sb([128, 512], BF16, "M4") for _ in range(4)]
                        xw = [pb_.sb([128, 512], BF16, "xw") for _ in range(2)]
                        t1b = [pb_.sb([128, 512], F32, "t1") for _ in range(2)]
                        t2b = [pb_.sb([128, 512], F32, "t2") for _ in range(2)]
                        gyb = [pb_.sb([128, 512], F32, "gy") for _ in range(2)]
                        t3b = [pb_.sb([128, 512], F32, "t3") for _ in range(2)]
                        ss, r_ss = pb_.sb([128, 8], F32, "ss")
                        g_tm, r_gtm = pb_.sb([128, DIN], BF16, "g_tm")
                        gTst, r_gTst = pb_.sb([128, 32, 128], BF16, "gTst")
                        r_Sbg = [Res() for _ in range(8)]
                        r_Sbbg = [Res() for _ in range(8)]
                        sq2 = [pb_.sb([128, 512], F32, "sq2") for _ in range(2)]
                        S.op("dve", lambda: V.memset(Sb_[:], 0.0), writes=r_Sbg)
                        S.op("dve", lambda: V.memset(Sbb[:], 0.0), writes=r_Sbbg)
                        (bOf, rbOf), (bOb, rbOb), (bSt, rbSt), (bY, rbY) = S.banks[4], S.banks[5], S.banks[6], S.banks[7]
                        order = [17, 16] + list(range(15, -1, -1))

                        def loads_a(ci):
                            c = order[ci]
                            c0, c1 = c * 128, (c + 1) * 128
                            S.dma("sp", BT2[ci % 2][0][:], XBC[DIN:DIN + GN, c0:c1].rearrange("(fc p) t -> p fc t", p=128), reads=r_XBC[32:40], writes=[BT2[ci % 2][1]])
                            S.dma("sp", CT2[ci % 2][0][:], XBC[DIN + GN:DIN + 2 * GN, c0:c1].rearrange("(fc p) t -> p fc t", p=128), reads=r_XBC[40:48], writes=[CT2[ci % 2][1]])
                            S.dma("sp", xs2[ci % 2][0][:], XSd[c0:c1, :], reads=[r_XS[c]], writes=[xs2[ci % 2][1]])
                            S.dma("sp", Bt2[ci % 2][0][:], BTMd[c0:c1, :], reads=[r_BTM[c]], writes=[Bt2[ci % 2][1]])

                        def load_hf(ci):
                            c = order[ci]
                            S.dma("sp", hf[:], HFd[c], reads=[r_HF[c]], writes=[r_hf])

                        def load_z(ci):
                            c = order[ci]
                            S.dma("sp", z_tm[:], Zd[c * 128:(c + 1) * 128, :], reads=[r_Z[c]], writes=[r_z])

                        loads_a(0)
                        load_hf(0)
                        load_z(0)
                        for ci, c in enumerate(order):
                            c0, c1 = c * 128, (c + 1) * 128
                            (xs_tm, r_xs), (B_tm, r_Btm), (BT, r_BT), (CT, r_CT) = xs2[ci % 2], Bt2[ci % 2], BT2[ci % 2], CT2[ci % 2]
                            if ci + 1 < len(order):
                                loads_a(ci + 1)
                            for half in range(2):
                                bkS, rbkS = S.banks[half]
                                for gi in range(4):
                                    g = half * 4 + gi
                                    S.op("pe", lambda: PE.matmul(bkS[:, gi * 128:(gi + 1) * 128], BT[:, g, :], CT[:, g, :], start=True, stop=True),
                                         reads=[r_BT, r_CT], writes=[rbkS], inc=(gi == 3))
                                for d in range(2):
                                    S.op("dve", lambda: V.tensor_tensor(out=scm[:, d, half * 4:(half + 1) * 4, :], in0=bkS[:, :].rearrange("s (g l) -> s g l", l=128),
                                                                        in1=mkb[:, d, :].unsqueeze(1).to_broadcast([128, 4, 128]), op=ALU.mult),
                                         reads=[rbkS, r_mkb], writes=[r_scm])

                            def RW(k):
                                g, pair = k // 4, k % 4
                                bk, rbk = S.banks[k % 4]
                                for hh in range(2):
                                    for d in range(2):
                                        h = g * 8 + pair * 2 + hh
                                        i = hh * 2 + d
                                        S.op("pe", lambda: PE.matmul(bk[:, i * 128:(i + 1) * 128], sel2[:, h:h + 1].to_broadcast([128, 128]),
                                                                     acs2[d][0][:, c0:c1], start=True, stop=True),
                                             reads=[r_sel2, acs2[d][1]], writes=[rbk], inc=(i == 3))

                            def SA(k):
                                g, pair = k // 4, k % 4
                                h0 = g * 8 + pair * 2
                                bk, rbk = S.banks[k % 4]
                                d_, rd_ = Db[k % 4]
                                for hh in range(2):
                                    for d in range(2):
                                        i = hh * 2 + d
                                        col = d * 64 + h0 + hh
                                        S.op("act", lambda: A.activation(out=d_[:, i * 128:(i + 1) * 128], in_=bk[:, i * 128:(i + 1) * 128], func=AF.Exp,
                                                                         bias=nacsl[:, c, col:col + 1], scale=1.0),
                                             reads=[rbk, r_nacsl], writes=[rd_])

                            def SC(k):
                                g, pair = k // 4, k % 4
                                d_, rd_ = Db[k % 4]
                                m_, rm_ = M4[k % 4]
                                for a in range(2):
                                    S.op("dve", lambda: V.scalar_tensor_tensor(out=m_[:, a * 256:(a + 1) * 256].rearrange("s (d l) -> s d l", d=2),
                                                                               in0=d_[:, a * 256:(a + 1) * 256].rearrange("s (d l) -> s d l", d=2), scalar=1.0e4,
                                                                               in1=scm[:, :, g, :], op0=ALU.min, op1=ALU.mult),
                                         reads=[rd_, r_scm], writes=[rm_])
                                for hh in range(2):
                                    r = pair * 2 + hh
                                    h = g * 8 + r
                                    xh = xs_tm[:, h * 64:(h + 1) * 64]
                                    for d in range(2):
                                        i = hh * 2 + d
                                        S.op("pe", lambda: PE.matmul(bY[:, r * 64:(r + 1) * 64], m_[:, i * 128:(i + 1) * 128], xh, start=(d == 0), stop=(d == 1)),
                                             reads=[rm_, r_xs], writes=[rbY], inc=(d == 1))

                            def grp_pool(g):
                                gs = slice(g * 512, (g + 1) * 512)
                                x_, rx_ = xw[g % 2]
                                S.op("pool", lambda: nc.gpsimd.tensor_tensor(out=v3(x_[:]), in0=v3(xs_tm[:, gs]),
                                                                             in1=bc(dtw[:, c, 64 + g * 8:64 + (g + 1) * 8], 8, 64), op=ALU.mult),
                                     reads=[r_xs, r_dtw], writes=[rx_])
                                t3, rt3 = t3b[g % 2]
                                S.op("pool", lambda: nc.gpsimd.tensor_tensor(out=v3(t3[:]), in0=v3(xs_tm[:, gs]), in1=bc(dsum[:, g * 8:(g + 1) * 8], 8, 64), op=ALU.mult),
                                     reads=[r_xs, r_dsum], writes=[rt3])

                            def grp_start(g):
                                gs = slice(g * 512, (g + 1) * 512)
                                x_, rx_ = xw[g % 2]
                                S.op("pe", lambda: PE.matmul(bOf[:, :], CT[:, g, :], hf[:, gs], start=True, stop=True), reads=[r_CT, r_hf], writes=[rbOf])
                                S.op("pe", lambda: PE.matmul(bOb[:, :], CT[:, g, :], Sbb[:, gs], start=True, stop=True), reads=[r_CT, r_Sbbg[g]], writes=[rbOb])
                                S.op("pe", lambda: PE.matmul(bSt[:, :], B_tm[:, g * 128:(g + 1) * 128], x_[:], start=True, stop=True),
                                     reads=[r_Btm, rx_], writes=[rbSt])

                            def grp_early(g):
                                gs = slice(g * 512, (g + 1) * 512)
                                (t1, rt1), (t2, rt2), (t3, rt3) = t1b[g % 2], t2b[g % 2], t3b[g % 2]
                                S.op("dve", lambda: V.tensor_tensor(out=v3(t1[:]), in0=v3(bOf[:, :]), in1=bc(ea[:, c, g * 8:(g + 1) * 8], 8, 64), op=ALU.mult),
                                     reads=[rbOf, r_ea], writes=[rt1])
                                S.op("dve", lambda: V.tensor_tensor(out=v3(t2[:]), in0=v3(bOb[:, :]), in1=bc(ea[:, c, 64 + g * 8:64 + (g + 1) * 8], 8, 64), op=ALU.mult),
                                     reads=[rbOb, r_ea], writes=[rt2])
                                S.op("pool", lambda: nc.gpsimd.tensor_tensor(out=t2[:], in0=t2[:], in1=t3[:], op=ALU.add), reads=[rt2, rt3], writes=[rt2])
                                S.op("dve", lambda: V.tensor_tensor(out=v3(Sb_[:, gs]), in0=v3(Sb_[:, gs]), in1=bc(cd[:, c, 64 + g * 8:64 + (g + 1) * 8], 8, 64), op=ALU.mult),
                                     reads=[r_Sbg[g], r_cd], writes=[r_Sbg[g]])
                                S.op("dve", lambda: V.tensor_tensor(out=Sb_[:, gs], in0=Sb_[:, gs], in1=bSt[:, :], op=ALU.add), reads=[r_Sbg[g], rbSt], writes=[r_Sbg[g]])
                                S.op("act", lambda: A.activation(out=Sbb[:, gs], in_=Sb_[:, gs], func=AF.Copy), reads=[r_Sbg[g]], writes=[r_Sbbg[g]])

                            def grp_endA(g):
                                gs = slice(g * 512, (g + 1) * 512)
                                (t1, rt1), (t2, rt2), (gy, rgy), (sq_, rsq_) = t1b[g % 2], t2b[g % 2], gyb[g % 2], sq2[g % 2]
                                S.op("dve", lambda: V.tensor_tensor(out=t1[:], in0=t1[:], in1=bY[:, :], op=ALU.add), reads=[rt1, rbY], writes=[rt1])
                                S.op("pool", lambda: nc.gpsimd.tensor_tensor(out=t1[:], in0=t1[:], in1=t2[:], op=ALU.add), reads=[rt1, rt2], writes=[rt1])
                                S.op("pool", lambda: nc.gpsimd.tensor_tensor(out=gy[:], in0=t1[:], in1=z_tm[:, gs], op=ALU.mult), reads=[rt1, r_z], writes=[rgy])

                            def grp_endB(g):
                                gs = slice(g * 512, (g + 1) * 512)
                                (gy, rgy), (sq_, rsq_) = gyb[g % 2], sq2[g % 2]
                                S.op("act", lambda: A.activation(out=sq_[:], in_=gy[:], func=AF.Square), reads=[rgy], writes=[rsq_])
                                S.op("dve", lambda: V.reduce_sum(out=ss[:, g:g + 1], in_=sq_[:], axis=AX.X), reads=[rsq_], writes=[r_ss])
                                S.op("act", lambda: A.activation(out=ss[:, g:g + 1], in_=ss[:, g:g + 1], func=AF.Ln, bias=epsc[:, 0:1], scale=1.0 / 512.0),
                                     reads=[r_ss, r_epsc], writes=[r_ss])
                                S.op("act", lambda: A.activation(out=ss[:, g:g + 1], in_=ss[:, g:g + 1], func=AF.Exp, scale=-0.5), reads=[r_ss], writes=[r_ss])
                                S.op("act", lambda: A.activation(out=g_tm[:, gs], in_=gy[:], func=AF.Identity, scale=ss[:, g:g + 1]), reads=[rgy, r_ss], writes=[r_gtm])

                            grp_pool(0)
                            for k in range(4):
                                RW(k)
                            SA(0)
                            SA(1)
                            for k in range(32):
                                g = k // 4
                                if k % 4 == 0:
                                    grp_start(g)
                                    if g + 1 < 8:
                                        grp_pool(g + 1)
                                    if g == 7 and ci + 1 < len(order):
                                        load_hf(ci + 1)
                                if k % 4 == 1:
                                    grp_early(g)
                                if k % 4 == 2 and g > 0:
                                    grp_endB(g - 1)
                                if k + 2 < 32:
                                    SA(k + 2)
                                SC(k)
                                if k + 4 < 32:
                                    RW(k + 4)
                                if k % 4 == 3:
                                    grp_endA(g)
                            if ci + 1 < len(order):
                                load_z(ci + 1)
                            grp_endB(7)
                            for fb in range(8):
                                bk, rbk = S.banks[fb % 4]
                                pbv = bk[:].bitcast(BF16)
                                for i in range(4):
                                    fc = fb * 4 + i
                                    S.op("pe", lambda: PE.transpose(out=pbv[:, i * 128:(i + 1) * 128], in_=g_tm[:, fc * 128:(fc + 1) * 128], identity=identb[:]),
                                         reads=[r_gtm, r_identb], writes=[rbk], inc=(i == 3))
                                dst = gTst[:, fb * 4:(fb + 1) * 4, :]
                                if fb % 2 == 0:
                                    S.op("act", lambda: A.activation(out=dst, in_=pbv[:, 0:512].rearrange("f (i t) -> f i t", t=128), func=AF.Copy), reads=[rbk], writes=[r_gTst])
                                else:
                                    S.op("dve", lambda: V.tensor_copy(out=dst, in_=pbv[:, 0:512].rearrange("f (i t) -> f i t", t=128)), reads=[rbk], writes=[r_gTst])
                            S.dma("sp", GTd[:, c0:c1].rearrange("(fc p) t -> p fc t", p=128), gTst[:], reads=[r_gTst], writes=[r_GT[c]])

            if ssd_upto < 6:
                raise _Stop()
            with Phase(S) as po:
                gT, r_gT = po.sb([128, 32, T], BF16, "gT")
                sng, r_sng = po.sb([128, 32], F32, "sngT")
                S.dma("sp", sng[:], sng_in[:, js, :], writes=[r_sng])

                def wfix(w, rw):
                    S.op("dve", lambda: V.tensor_tensor(out=w[:], in0=w[:], in1=sng[:].unsqueeze(2).to_broadcast([128, 32, 128]), op=ALU.mult),
                         reads=[rw, r_sng], writes=[rw])
                S.dma("sp", gT[:], GTd.rearrange("(fc p) t -> p fc t", p=128), reads=r_GT, writes=[r_gT])
                linear(po, lambda col: w_out[js, :, col:col + 128], [m * 128 for m in range(KC)], 32, gT, r_gT,
                       resid_epilogue(po, lambda m, j: MOD[:, l, 32 + m, j:j + 1]), wtag="wo", wfix=wfix)

        r_UV = Res("UV")

        stage_mod()
        stage_i = 0
        for l in range(NL):
            if stage_i < upto:
                if l % 2 == 0:
                    stage_fourier(l)
                else:
                    try:
                        stage_ssd(l)
                    except _Stop:
                        pass
            stage_i += 1
            if stage_i < upto:
                stage_ffn(l)
            stage_i += 1

        if final and upto >= 2 * NL:
            norm_phase(None, None, lambda c, j: AFN[:, c, 0:1], lambda c, j: ZB[:, c, 0:1], [r_AFN, r_ZB], out_f32=out_d)
        else:
            with Phase(S) as ph:
                xb = [ph.sb([128, T], F32, "cp") for _ in range(2)]
                for m in range(KC):
                    x, rx = xb[m % 2]
                    S.dma("sp", x[:], XT[m * 128:(m + 1) * 128, :], reads=[r_XT[m]], writes=[rx])
                    S.dma("sp", out_d[m * 128:(m + 1) * 128, :], x[:], reads=[rx])
        S.barrier()
    return nc


_CONST = {}


def _consts():
    if _CONST:
        return _CONST
    bf = ml_dtypes.bfloat16
    k = np.arange(256)
    ang = 2.0 * np.pi * ((k[:, None] * k[None, :]) % 256) / 256.0
    c256 = np.cos(ang) / 16.0
    s256 = np.sin(ang) / 16.0
    cs1 = np.concatenate([c256, s256], axis=1)
    cs2c = np.concatenate([c256, -s256], axis=1)
    _CONST["cs1"] = np.ascontiguousarray(cs1.reshape(2, 128, 512).transpose(1, 0, 2)).astype(bf)
    _CONST["cs2c"] = np.ascontiguousarray(cs2c.reshape(2, 128, 512).transpose(1, 0, 2)).astype(bf)
    t = np.arange(L)
    angL = 2.0 * np.pi * ((t[:, None] * t[None, :]) % L) / float(L)
    _CONST["cl"] = (np.cos(angL) / math.sqrt(L)).astype(bf)
    _CONST["nsl"] = (-np.sin(angL) / math.sqrt(L)).astype(bf)
    _CONST["identb"] = np.eye(128, dtype=np.float32).astype(bf)
    s = np.arange(128)
    triU = (s[:, None] <= s[None, :]).astype(np.float32)
    triL = (s[:, None] >= s[None, :]).astype(np.float32)
    _CONST["cf32"] = np.ascontiguousarray(np.stack([triU, triL, triU, triL], axis=1)).astype(np.float32)
    return _CONST


def _pm(v, nchunk):
    v = np.asarray(v, dtype=np.float32)
    lead = v.shape[:-1]
    r = v.reshape(lead + (nchunk, 128))
    r = np.moveaxis(r, -1, 0)
    return np.ascontiguousarray(r)


def make_in_maps(inputs, cores):
    cst = _consts()
    f = lambda k: np.asarray(inputs[k], dtype=np.float32)
    shared = {
        "w_mod": f("w_mod"), "four_w": f("four_w"), "ssd_w_in": f("ssd_w_in"), "ssd_w_out": f("ssd_w_out"),
        "ffn_w_up": f("ffn_w_up"), "ffn_w_down": f("ffn_w_down"),
        "bmodT": _pm(f("b_mod"), 96), "gmixT": _pm(f("norm_mix_g"), KC), "gffnT": _pm(f("norm_ffn_g"), KC),
        "gfinT": _pm(f("final_g"), KC),
        "scwT": np.ascontiguousarray(_pm(f("ssd_conv_w"), 48).transpose(0, 1, 3, 2)),
        "scbT": _pm(f("ssd_conv_b"), 48),
        "dtb_bc": np.ascontiguousarray(np.broadcast_to(f("ssd_dt_bias").reshape(1, 2, 128), (128, 2, 128))),
        "alog_bc": np.ascontiguousarray(np.broadcast_to(f("ssd_a_log").reshape(1, 2, 128), (128, 2, 128))),
        "dsk_bc": np.ascontiguousarray(np.broadcast_to(f("ssd_d").reshape(1, 2, 128), (128, 2, 128))),
        "sngT": _pm(f("ssd_norm_g"), 32),
        "fcwT": np.ascontiguousarray(_pm(f("ffn_conv_w").reshape(NL, 9, DFF), 44).transpose(0, 1, 3, 2)),
        "fcbT": _pm(f("ffn_conv_b"), 44),
        "cs1": cst["cs1"], "cs2c": cst["cs2c"], "cl": cst["cl"], "nsl": cst["nsl"], "identb": cst["identb"], "cf32": cst["cf32"],
    }
    x, c, ctx, cc = f("x"), f("c"), f("ctx"), f("c_ctx")
    maps = []
    for b in cores:
        m = dict(shared)
        m["xc"] = np.ascontiguousarray(np.concatenate([x[b].T, ctx[b].T], axis=1))
        m["cv"] = np.ascontiguousarray(np.stack([c[b], cc], axis=-1).reshape(KC, 128, 2).transpose(1, 0, 2))
        maps.append(m)
    return maps


def kernel(**inputs):
    nc = build()
    maps = make_in_maps(inputs, list(range(8)))
    res = run_bass_kernel_spmd(nc, maps, core_ids=list(range(8)))
    out = np.stack([np.ascontiguousarray(res.results[b]["out"][:, :L].T) for b in range(8)], axis=0)
    return out.astype(np.float32)
```

```python
import math
from contextlib import ExitStack

import ml_dtypes
import numpy as np

import concourse.bass as bass
import concourse.mybir as mybir
from concourse.bass_utils import run_bass_kernel_spmd

F32 = mybir.dt.float32
BF16 = mybir.dt.bfloat16
AF = mybir.ActivationFunctionType
ALU = mybir.AluOpType
AX = mybir.AxisListType

D = 2048
L = 2048
LC = 256
T = L + LC
DFF = 5632
NL = 4
KC = 16
DIN = 4096
GN = 1024
H = 64
SSD_IN = 10368
STATE_COLS = DIN + GN + 2 * H
EPS = 1e-6
TT = [(0, 512), (512, 512), (1024, 512), (1536, 512), (2048, 256)]
NCH = 18
GS = 11
GPL = 34 * 66
GPW = GPL + 258


class Res:
    __slots__ = ("w", "r", "name", "excl")

    def __init__(self, name="", excl=False):
        self.w = None
        self.r = {}
        self.name = name
        self.excl = excl


class _Q:
    def __init__(self, name, eng, sem):
        self.name = name
        self.eng = eng
        self.sem = sem
        self.cnt = 0
        self.seen = {}
        self.slots = []
        self.di = 0


class Sched:
    NSLOT = 12

    def __init__(self, nc, es):
        self.nc = nc
        self.q = {}
        for name, eng in (("pe", nc.tensor), ("act", nc.scalar), ("dve", nc.vector), ("pool", nc.gpsimd), ("sp", nc.sync)):
            self.q[name] = _Q(name, eng, es.enter_context(nc.semaphore("s_" + name)))
        for qn in ("sp", "pool"):
            for i in range(self.NSLOT):
                self.q[qn].slots.append([es.enter_context(nc.semaphore(f"d_{qn}{i}")), 0, f"d_{qn}{i}"])
        self.banks = []
        self.bi = 0

    def _wait(self, q, tok):
        key, sem, val = tok
        if q.seen.get(key, 0) >= val:
            return
        q.eng.wait_ge(sem, val)
        q.seen[key] = val

    def _deps(self, q, reads, writes, skip_self, skip_same=False):
        for r in reads:
            if r.w is not None and not (skip_self and r.w[0] == q.name):
                self._wait(q, r.w)
            if r.excl:
                for tok in r.r.values():
                    if tok[0] != q.name:
                        self._wait(q, tok)
        for w in writes:
            if w.w is not None and not (skip_same and w.w[0] == q.name):
                self._wait(q, w.w)
            for tok in w.r.values():
                if not (skip_same and tok[0] == q.name):
                    self._wait(q, tok)

    def _mark(self, tok, reads, writes):
        for r in reads:
            old = r.r.get(tok[0])
            if old is None or old[2] < tok[2]:
                r.r[tok[0]] = tok
        for w in writes:
            w.w = tok
            w.r = {}

    def op(self, E, fn, reads=(), writes=(), inc=True):
        q = self.q[E]
        self._deps(q, reads, writes, E == "pe", True)
        ins = fn()
        if inc:
            ins.then_inc(q.sem, 1)
            q.cnt += 1
            tok = (q.name, q.sem, q.cnt)
        else:
            tok = (q.name, q.sem, q.cnt + 1)
        self._mark(tok, reads, writes)
        return tok

    def dma(self, Qn, out, in_, reads=(), writes=(), **kw):
        q = self.q[Qn]
        slot = q.slots[q.di % self.NSLOT]
        q.di += 1
        if slot[1] > 0:
            self._wait(q, (slot[2], slot[0], 16 * slot[1]))
        self._deps(q, reads, writes, False)
        q.eng.dma_start(out=out, in_=in_, **kw).then_inc(slot[0], 16)
        slot[1] += 1
        tok = (slot[2], slot[0], 16 * slot[1])
        self._mark(tok, reads, writes)
        return tok

    def barrier(self):
        toks = []
        for q in self.q.values():
            if q.cnt > 0:
                toks.append((q.name, q.sem, q.cnt))
            for s in q.slots:
                if s[1] > 0:
                    toks.append((s[2], s[0], 16 * s[1]))
        for q in self.q.values():
            for t in toks:
                if t[0] != q.name:
                    self._wait(q, t)

    def bank(self):
        b = self.banks[self.bi % len(self.banks)]
        self.bi += 1
        return b


class _Stop(Exception):
    pass


class Phase:
    uid = 0

    def __init__(self, S):
        self.S = S
        self.es = ExitStack()
        self.n = 0

    def __enter__(self):
        self.es.__enter__()
        return self

    def sb(self, shape, dt, name="t"):
        Phase.uid += 1
        t = self.es.enter_context(self.S.nc.sbuf_tensor(f"{name}_{Phase.uid}", list(shape), dt))
        return t, Res(name)

    def __exit__(self, *a):
        self.S.barrier()
        self.es.__exit__(None, None, None)
        return False


def build(upto=999, final=True, ssd_upto=99):
    nc = bass.Bass("TRN2", target_bir_lowering=False)

    def din(name, shape, dt=F32):
        return nc.dram_tensor(name, list(shape), dt, kind="ExternalInput").ap()

    xc_in = din("xc", [D, T])
    cv_in = din("cv", [128, KC, 2])
    w_mod = din("w_mod", [NL, D, 6 * D])
    bmod_in = din("bmodT", [128, NL, 96])
    gmix_in = din("gmixT", [128, NL, KC])
    gffn_in = din("gffnT", [128, NL, KC])
    gfin_in = din("gfinT", [128, KC])
    four_w = din("four_w", [2, D, D])
    w_in = din("ssd_w_in", [2, D, SSD_IN])
    scw_in = din("scwT", [128, 2, 48, 3])
    scb_in = din("scbT", [128, 2, 48])
    dtb_in = din("dtb_bc", [128, 2, 128])
    alog_in = din("alog_bc", [128, 2, 128])
    dsk_in = din("dsk_bc", [128, 2, 128])
    sng_in = din("sngT", [128, 2, 32])
    w_out = din("ssd_w_out", [2, DIN, D])
    w_up = din("ffn_w_up", [NL, D, 2 * DFF])
    fcw_in = din("fcwT", [128, NL, 44, 9])
    fcb_in = din("fcbT", [128, NL, 44])
    w_down = din("ffn_w_down", [NL, DFF, D])
    cs1_in = din("cs1", [128, 2, 512], BF16)
    cs2c_in = din("cs2c", [128, 2, 512], BF16)
    cl_in = din("cl", [L, L], BF16)
    nsl_in = din("nsl", [L, L], BF16)
    identb_in = din("identb", [128, 128], BF16)
    cf_in = din("cf32", [128, 4, 128])
    out_d = nc.dram_tensor("out", [D, T], F32, kind="ExternalOutput").ap()

    XT = nc.dram_tensor("XT", [D, T], F32).ap()
    UVd = nc.dram_tensor("UVd", [T, 8, 512], BF16).ap()
    XBC = nc.dram_tensor("XBC", [48 * 128, T], BF16).ap()
    Zd = nc.dram_tensor("Zd", [T, DIN], BF16).ap()
    XSd = nc.dram_tensor("XSd", [T, DIN], BF16).ap()
    BTMd = nc.dram_tensor("BTMd", [T, GN], BF16).ap()
    HFd = nc.dram_tensor("HFd", [NCH, 128, DIN], BF16).ap()
    GTd = nc.dram_tensor("GTd", [DIN, T], BF16).ap()

    with ExitStack() as es:
        S = Sched(nc, es)
        for i in range(8):
            S.banks.append((es.enter_context(nc.psum_tensor(f"bank{i}", [128, 512], F32)), Res(f"bank{i}", excl=True)))

        def psb(shape, dt, name):
            return es.enter_context(nc.sbuf_tensor("sb_" + name, list(shape), dt)), Res(name)

        ones_bf, r_ones = psb([128, 128], BF16, "ones_bf")
        ones_f, r_onesf = psb([128, 128], F32, "ones_f")
        identb, r_identb = psb([128, 128], BF16, "identb")
        cf, r_cf = psb([128, 4, 128], F32, "cf")
        mkb, r_mkb = psb([128, 2, 128], BF16, "mkb")
        sT, r_sT = psb([128, KC, 2], BF16, "sT")
        MOD, r_MOD = psb([128, NL, 96, 2], F32, "MOD")
        A1, r_A1 = psb([128, NL, KC, 2], F32, "A1")
        A2, r_A2 = psb([128, NL, KC, 2], F32, "A2")
        AFN, r_AFN = psb([128, KC, 2], F32, "AFN")
        ZB, r_ZB = psb([128, KC, 2], F32, "ZB")
        bmod, r_bmod = psb([128, NL, 96], F32, "bmod")
        gmix, r_gmix = psb([128, NL, KC], F32, "gmix")
        gffn, r_gffn = psb([128, NL, KC], F32, "gffn")
        gfin, r_gfin = psb([128, KC], F32, "gfin")
        scw, r_scw = psb([128, 2, 48, 3], F32, "scw")
        scb, r_scb = psb([128, 2, 48], F32, "scb")
        cvt, r_cvt = psb([128, KC, 2], F32, "cvt")

        V = nc.vector
        A = nc.scalar
        PE = nc.tensor
        r_XT = [Res(f"XT{m}") for m in range(KC)]

        S.op("dve", lambda: V.memset(ones_bf[:], 1.0), writes=[r_ones])
        S.op("dve", lambda: V.memset(ones_f[:], 1.0), writes=[r_onesf])
        S.op("dve", lambda: V.memset(ZB[:], 0.0), writes=[r_ZB])
        for dst, rr, src in ((identb, r_identb, identb_in), (cf, r_cf, cf_in), (bmod, r_bmod, bmod_in),
                             (gmix, r_gmix, gmix_in), (gffn, r_gffn, gffn_in), (gfin, r_gfin, gfin_in),
                             (scw, r_scw, scw_in), (scb, r_scb, scb_in),
                             (cvt, r_cvt, cv_in)):
            S.dma("sp", dst[:], src, writes=[rr])
        S.op("dve", lambda: V.tensor_copy(out=mkb[:], in_=cf[:, 2:4, :]), reads=[r_cf], writes=[r_mkb])
        S.op("act", lambda: A.activation(out=sT[:], in_=cvt[:], func=AF.Silu), reads=[r_cvt], writes=[r_sT])
        for m in range(KC):
            S.dma("sp", XT[m * 128:(m + 1) * 128, :], xc_in[m * 128:(m + 1) * 128, :], writes=[r_XT[m]])

        def stage_mod():
            with Phase(S) as ph:
                wb = [ph.sb([128, KC, 512], BF16, "modw") for _ in range(3)]
                for l in range(NL):
                    bk, rbk = S.bank()
                    for r4 in range(24):
                        w, rw = wb[(l * 24 + r4) % 3]
                        S.dma("pool", w[:], w_mod[l, :, r4 * 512:(r4 + 1) * 512].rearrange("(kc p) c -> p kc c", p=128), writes=[rw])
                        for rr in range(4):
                            r = r4 * 4 + rr
                            for kc in range(KC):
                                S.op("pe", lambda: PE.matmul(bk[:, 2 * r:2 * r + 2], w[:, kc, rr * 128:(rr + 1) * 128], sT[:, kc, :],
                                                             start=(kc == 0), stop=(kc == KC - 1)),
                                     reads=[rw, r_sT], writes=[rbk], inc=(kc == KC - 1))
                    for j in range(2):
                        S.op("dve", lambda: V.tensor_tensor(out=MOD[:, l, :, j], in0=bk[:, 0:192].rearrange("p (r j) -> p r j", j=2)[:, :, j],
                                                            in1=bmod[:, l, :], op=ALU.add),
                             reads=[rbk, r_bmod], writes=[r_MOD])
                    sq = math.sqrt(float(D))
                    for j in range(2):
                        for (AA, rA, gg, rg, off) in ((A1, r_A1, gmix, r_gmix, 16), (A2, r_A2, gffn, r_gffn, 64)):
                            S.op("dve", lambda: V.tensor_scalar(out=AA[:, l, :, j], in0=MOD[:, l, off:off + 16, j], scalar1=1.0, scalar2=sq,
                                                                op0=ALU.add, op1=ALU.mult), reads=[r_MOD], writes=[rA])
                            S.op("dve", lambda: V.tensor_tensor(out=AA[:, l, :, j], in0=AA[:, l, :, j], in1=gg[:, l, :], op=ALU.mult),
                                 reads=[rA, rg], writes=[rA])
                for j in range(2):
                    S.op("dve", lambda: V.tensor_scalar(out=AFN[:, :, j], in0=gfin[:, :], scalar1=math.sqrt(float(D)), scalar2=None, op0=ALU.mult),
                         reads=[r_gfin], writes=[r_AFN])

        def norm_phase(aT, r_aT, scale_ap, bias_ap, rs, out_f32=None):
            with Phase(S) as ph:
                xb = [ph.sb([128, T], F32, "nx") for _ in range(3)]
                sqb = [ph.sb([128, T], BF16, "nsq") for _ in range(2)]
                rstd, r_rstd = ph.sb([128, T], F32, "rstd")
                ob = [ph.sb([128, T], F32, "nout") for _ in range(2)] if out_f32 is not None else None
                bks = [S.bank() for _ in range(5)]
                for c in range(KC):
                    x, rx = xb[c % 3]
                    s, rsq = sqb[c % 2]
                    S.dma("sp", x[:], XT[c * 128:(c + 1) * 128, :], reads=[r_XT[c]], writes=[rx])
                    S.op("act", lambda: A.activation(out=s[:], in_=x[:], func=AF.Square), reads=[rx], writes=[rsq])
                    for ti, (t0, n) in enumerate(TT):
                        bk, rbk = bks[ti]
                        S.op("pe", lambda: PE.matmul(bk[:, :n], ones_bf[:], s[:, t0:t0 + n], start=(c == 0), stop=(c == KC - 1)),
                             reads=[rsq, r_ones], writes=[rbk], inc=(ti == 4))
                for ti, (t0, n) in enumerate(TT):
                    bk, rbk = bks[ti]
                    S.op("dve", lambda: V.tensor_scalar(out=rstd[:, t0:t0 + n], in0=bk[:, :n], scalar1=float(D) * EPS, scalar2=None,
                                                        op0=ALU.add), reads=[rbk], writes=[r_rstd])
                    S.op("act", lambda: A.activation(out=rstd[:, t0:t0 + n], in_=rstd[:, t0:t0 + n], func=AF.Sqrt), reads=[r_rstd], writes=[r_rstd])
                    S.op("dve", lambda: V.reciprocal(out=rstd[:, t0:t0 + n], in_=rstd[:, t0:t0 + n]), reads=[r_rstd], writes=[r_rstd])
                for c in range(KC):
                    x, rx = xb[(KC + c) % 3]
                    S.dma("sp", x[:], XT[c * 128:(c + 1) * 128, :], reads=[r_XT[c]], writes=[rx])
                    S.op("dve", lambda: V.tensor_tensor(out=x[:], in0=x[:], in1=rstd[:], op=ALU.mult), reads=[rx, r_rstd], writes=[rx])
                    if out_f32 is None:
                        for j, (a, b) in enumerate(((0, L), (L, T))):
                            S.op("act", lambda: A.activation(out=aT[:, c, a:b], in_=x[:, a:b], func=AF.Identity,
                                                             bias=bias_ap(c, j), scale=scale_ap(c, j)),
                                 reads=[rx] + rs, writes=[r_aT[c]])
                    else:
                        o, ro = ob[c % 2]
                        S.op("act", lambda: A.activation(out=o[:], in_=x[:], func=AF.Identity, bias=bias_ap(c, 0), scale=scale_ap(c, 0)),
                             reads=[rx] + rs, writes=[ro])
                        S.dma("sp", out_f32[c * 128:(c + 1) * 128, :], o[:], reads=[ro])

        def linear(ph, Wsrc, cols, kcn, inT, r_in, epilogue, wtag="lw", nbuf=3, wfix=None):
            wb = [ph.sb([128, kcn, 128], BF16, wtag) for _ in range(nbuf)]
            if hasattr(epilogue, "load"):
                epilogue.load(0)
            for mi, col in enumerate(cols):
                w, rw = wb[mi % nbuf]
                S.dma("pool", w[:], Wsrc(col).rearrange("(kc p) c -> p kc c", p=128), writes=[rw])
                if wfix is not None:
                    wfix(w, rw)
                bl = []
                for ti, (t0, n) in enumerate(TT):
                    bk, rbk = S.bank()
                    for kc in range(kcn):
                        S.op("pe", lambda: PE.matmul(bk[:, :n], w[:, kc, :], inT[:, kc, t0:t0 + n], start=(kc == 0), stop=(kc == kcn - 1)),
                             reads=[rw, r_in[kc] if isinstance(r_in, list) else r_in], writes=[rbk], inc=(kc == kcn - 1))
                    bl.append((bk, rbk))
                epilogue(mi, col, bl)

        def resid_epilogue(ph, gate_ap, cols=None, nbuf=3):
            xt = [ph.sb([128, T], F32, "rxt") for _ in range(nbuf)]
            cols = cols if cols is not None else [m * 128 for m in range(KC)]
            issued = set()

            def load(mi):
                if mi in issued or mi >= len(cols):
                    return
                issued.add(mi)
                m = cols[mi] // 128
                x, rx = xt[mi % nbuf]
                S.dma("sp", x[:], XT[m * 128:(m + 1) * 128, :], reads=[r_XT[m]], writes=[rx])

            def ep(mi, col, bl):
                m = col // 128
                x, rx = xt[mi % nbuf]
                load(mi)
                load(mi + 1)
                for ti, (t0, n) in enumerate(TT):
                    bk, rbk = bl[ti]
                    j = 0 if t0 < L else 1
                    S.op("dve", lambda: V.scalar_tensor_tensor(out=x[:, t0:t0 + n], in0=bk[:, :n], scalar=gate_ap(m, j), in1=x[:, t0:t0 + n],
                                                               op0=ALU.mult, op1=ALU.add), reads=[rbk, rx, r_MOD], writes=[rx])
                S.dma("sp", XT[m * 128:(m + 1) * 128, :], x[:], reads=[rx], writes=[r_XT[m]])
            ep.load = load
            return ep

        def stage_ffn(l):
            with Phase(S) as pho:
                aT, _ = pho.sb([128, KC, T], BF16, "aT")
                r_aT = [Res() for _ in range(KC)]
                norm_phase(aT, r_aT, lambda c, j: A2[:, l, c, j:j + 1], lambda c, j: MOD[:, l, 48 + c, j:j + 1], [r_A2, r_MOD])
                hT, _ = pho.sb([128, GS, T], BF16, "hT")
                fcw, r_fcw = pho.sb([128, 44, 9], F32, "fcw")
                fcb, r_fcb = pho.sb([128, 44], F32, "fcb")
                S.dma("sp", fcw[:], fcw_in[:, l, :, :], writes=[r_fcw])
                S.dma("sp", fcb[:], fcb_in[:, l, :], writes=[r_fcb])
                groups = [list(range(g0, min(g0 + GS, 44))) for g0 in range(0, 44, GS)]
                for grp in groups:
                    r_h = [Res() for _ in grp]
                    with Phase(S) as ph:
                        gp = [ph.sb([128, GPW], F32, "gp") for _ in range(2)]
                        acc = [ph.sb([128, T], F32, "acc") for _ in range(1)]
                        vb = [ph.sb([128, T], F32, "vb") for _ in range(2)]
                        wg = [ph.sb([128, KC, 128], BF16, "wg") for _ in range(2)]
                        wv = [ph.sb([128, KC, 128], BF16, "wv") for _ in range(2)]
                        for g_, rg_ in gp:
                            S.op("dve", lambda: V.memset(g_[:], 0.0), writes=[rg_])
                        def u_vars(ji):
                            return gp[ji % 2], acc[0], vb[ji % 2], wg[ji % 2], wv[ji % 2]

                        def u_pe(ji, j):
                            (g_, rg_), (ac, rac), (vv, rvv), (wgt, rwg), (wvt, rwv) = u_vars(ji)
                            S.dma("pool", wgt[:], w_up[l, :, j * 128:(j + 1) * 128].rearrange("(kc p) c -> p kc c", p=128), writes=[rwg])
                            S.dma("pool", wvt[:], w_up[l, :, DFF + j * 128:DFF + (j + 1) * 128].rearrange("(kc p) c -> p kc c", p=128), writes=[rwv])
                            glat = g_[:, 0:GPL].rearrange("p (r c) -> p r c", c=66)
                            for ti, (t0, n) in enumerate(TT):
                                bg, rbg = S.bank()
                                for kc in range(KC):
                                    S.op("pe", lambda: PE.matmul(bg[:, :n], wgt[:, kc, :], aT[:, kc, t0:t0 + n], start=(kc == 0), stop=(kc == KC - 1)),
                                         reads=[rwg, r_aT[kc]], writes=[rbg], inc=(kc == KC - 1))
                                if t0 < L:
                                    S.op("act", lambda: A.activation(out=glat[:, 1 + 8 * ti:9 + 8 * ti, 1:65],
                                                                     in_=bg[:, :512].rearrange("p (r c) -> p r c", c=64), func=AF.Copy),
                                         reads=[rbg], writes=[rg_])
                                else:
                                    S.op("act", lambda: A.activation(out=g_[:, GPL + 1:GPL + 257], in_=bg[:, :256], func=AF.Copy),
                                         reads=[rbg], writes=[rg_])
                                bv, rbv = S.bank()
                                for kc in range(KC):
                                    S.op("pe", lambda: PE.matmul(bv[:, :n], wvt[:, kc, :], aT[:, kc, t0:t0 + n], start=(kc == 0), stop=(kc == KC - 1)),
                                         reads=[rwv, r_aT[kc]], writes=[rbv], inc=(kc == KC - 1))
                                S.op("act", lambda: A.activation(out=vv[:, t0:t0 + n], in_=bv[:, :n], func=AF.Copy), reads=[rbv], writes=[rvv])

                        def u_post(ji, j):
                            (g_, rg_), (ac, rac), (vv, rvv), (wgt, rwg), (wvt, rwv) = u_vars(ji)
                            glat = g_[:, 0:GPL].rearrange("p (r c) -> p r c", c=66)
                            alat = ac[:, 0:L].rearrange("p (r c) -> p r c", c=64)
                            first = True
                            for di in range(3):
                                for dj in range(3):
                                    wsc = fcw[:, j, di * 3 + dj:di * 3 + dj + 1]
                                    src = glat[:, di:di + 32, dj:dj + 64]
                                    if first:
                                        S.op("dve", lambda: V.tensor_scalar(out=alat, in0=src, scalar1=wsc, scalar2=fcb[:, j:j + 1],
                                                                            op0=ALU.mult, op1=ALU.add), reads=[rg_, r_fcw, r_fcb], writes=[rac])
                                        first = False
                                    else:
                                        S.op("dve", lambda: V.scalar_tensor_tensor(out=alat, in0=src, scalar=wsc, in1=alat, op0=ALU.mult, op1=ALU.add),
                                             reads=[rg_, rac, r_fcw], writes=[rac])
                            for dj in range(3):
                                wsc = fcw[:, j, 3 + dj:3 + dj + 1]
                                src = g_[:, GPL + dj:GPL + dj + 256]
                                if dj == 0:
                                    S.op("dve", lambda: V.tensor_scalar(out=ac[:, L:T], in0=src, scalar1=wsc, scalar2=fcb[:, j:j + 1],
                                                                        op0=ALU.mult, op1=ALU.add), reads=[rg_, r_fcw, r_fcb], writes=[rac])
                                else:
                                    S.op("dve", lambda: V.scalar_tensor_tensor(out=ac[:, L:T], in0=src, scalar=wsc, in1=ac[:, L:T], op0=ALU.mult, op1=ALU.add),
                                         reads=[rg_, rac, r_fcw], writes=[rac])
                            S.op("act", lambda: A.activation(out=ac[:], in_=ac[:], func=AF.Silu), reads=[rac], writes=[rac])
                            S.op("dve", lambda: V.tensor_tensor(out=hT[:, ji, :], in0=ac[:], in1=vv[:], op=ALU.mult), reads=[rac, rvv], writes=[r_h[ji]])

                        for ji, j in enumerate(grp):
                            u_pe(ji, j)
                            if ji > 0:
                                u_post(ji - 1, grp[ji - 1])
                        u_post(len(grp) - 1, grp[-1])
                    with Phase(S) as ph:
                        linear(ph, lambda col: w_down[l, grp[0] * 128:(grp[-1] + 1) * 128, col:col + 128], [m * 128 for m in range(KC)], len(grp),
                               hT, r_h, resid_epilogue(ph, lambda m, j: MOD[:, l, 80 + m, j:j + 1]), wtag="wd")

        def stage_fourier(l):
            jf = l // 2
            with Phase(S) as pho:
                fT, _ = pho.sb([128, KC, T], BF16, "fT")
                r_fT = [Res() for _ in range(KC)]
                with Phase(S) as ph1:
                    aT, _ = ph1.sb([128, KC, T], BF16, "aT")
                    r_aT = [Res() for _ in range(KC)]
                    norm_phase(aT, r_aT, lambda c, j: A1[:, l, c, j:j + 1], lambda c, j: MOD[:, l, c, j:j + 1], [r_A1, r_MOD])
                    cs1, r_cs1 = ph1.sb([128, 2, 512], BF16, "cs1")
                    S.dma("sp", cs1[:], cs1_in, writes=[r_cs1])
                    st = [ph1.sb([128, 8, 512], BF16, "uvst") for _ in range(2)]
                    for tt in range(NCH):
                        s_, rs_ = st[tt % 2]
                        for g in range(8):
                            bk, rbk = S.bank()
                            for kc in range(2):
                                S.op("pe", lambda: PE.matmul(bk[:, :], aT[:, 2 * g + kc, tt * 128:(tt + 1) * 128], cs1[:, kc, :], start=(kc == 0), stop=(kc == 1)),
                                     reads=[r_aT[2 * g + kc], r_cs1], writes=[rbk], inc=(kc == 1))
                            if g % 2 == 0:
                                S.op("act", lambda: A.activation(out=s_[:, g, :], in_=bk[:, :], func=AF.Copy), reads=[rbk], writes=[rs_])
                            else:
                                S.op("dve", lambda: V.tensor_copy(out=s_[:, g, :], in_=bk[:, :]), reads=[rbk], writes=[rs_])
                        S.dma("sp", UVd[tt * 128:(tt + 1) * 128, :, :], s_[:], reads=[rs_], writes=[r_UV])
                with Phase(S) as ph2:
                    tb = [(ph2.sb([128, KC, 512], BF16, "cl"), ph2.sb([128, KC, 512], BF16, "nsl")) for _ in range(2)]
                    uv = [ph2.sb([128, KC, 512], BF16, "uvg") for _ in range(2)]
                    cnt = 0
                    for tp in range(4):
                        (clt, rcl), (nst, rns) = tb[tp % 2]
                        S.dma("sp", clt[:], cl_in[:, tp * 512:(tp + 1) * 512].rearrange("(tt p) c -> p tt c", p=128), writes=[rcl])
                        S.dma("sp", nst[:], nsl_in[:, tp * 512:(tp + 1) * 512].rearrange("(tt p) c -> p tt c", p=128), writes=[rns])
                        for g in range(8):
                            u_, ru_ = uv[cnt % 2]
                            cnt += 1
                            S.dma("sp", u_[:], UVd[0:L, g, :].rearrange("(tt p) c -> p tt c", p=128), reads=[r_UV], writes=[ru_])
                            for hh in range(2):
                                bk, rbk = S.bank()
                                for tt in range(KC):
                                    S.op("pe", lambda: PE.matmul(bk[:, :], u_[:, tt, hh * 128:(hh + 1) * 128], clt[:, tt, :], start=(tt == 0), stop=False),
                                         reads=[ru_, rcl], writes=[rbk], inc=False)
                                    S.op("pe", lambda: PE.matmul(bk[:, :], u_[:, tt, 256 + hh * 128:256 + (hh + 1) * 128], nst[:, tt, :], start=False, stop=(tt == KC - 1)),
                                         reads=[ru_, rns], writes=[rbk], inc=(tt == KC - 1))
                                cp = 2 * g + hh
                                if hh == 0:
                                    S.op("act", lambda: A.activation(out=fT[:, cp, tp * 512:(tp + 1) * 512], in_=bk[:, :], func=AF.Copy), reads=[rbk], writes=[r_fT[cp]])
                                else:
                                    S.op("dve", lambda: V.tensor_copy(out=fT[:, cp, tp * 512:(tp + 1) * 512], in_=bk[:, :]), reads=[rbk], writes=[r_fT[cp]])
                    c2, r_c2 = ph2.sb([128, 2, 512], BF16, "cs2c")
                    S.dma("sp", c2[:], cs2c_in, writes=[r_c2])
                    uc, r_uc = ph2.sb([128, 2, 8, 512], BF16, "uvc")
                    for tt in range(2):
                        S.dma("sp", uc[:, tt, :, :], UVd[L + tt * 128:L + (tt + 1) * 128, :, :], reads=[r_UV], writes=[r_uc])
                    for cp in range(KC):
                        g, hh = cp // 2, cp % 2
                        bk, rbk = S.bank()
                        for tt in range(2):
                            S.op("pe", lambda: PE.matmul(bk[:, :256], uc[:, tt, g, hh * 128:(hh + 1) * 128], c2[:, tt, 0:256], start=(tt == 0), stop=False),
                                 reads=[r_uc, r_c2], writes=[rbk], inc=False)
                            S.op("pe", lambda: PE.matmul(bk[:, :256], uc[:, tt, g, 256 + hh * 128:256 + (hh + 1) * 128], c2[:, tt, 256:512], start=False, stop=(tt == 1)),
                                 reads=[r_uc, r_c2], writes=[rbk], inc=(tt == 1))
                        S.op("act", lambda: A.activation(out=fT[:, cp, L:T], in_=bk[:, :256], func=AF.Copy), reads=[rbk], writes=[r_fT[cp]])
                with Phase(S) as ph3:
                    linear(ph3, lambda col: four_w[jf, :, col:col + 128], [m * 128 for m in range(KC)], KC, fT, r_fT,
                           resid_epilogue(ph3, lambda m, j: MOD[:, l, 32 + m, j:j + 1]), wtag="fw")


        def stage_ssd(l):
            js = l // 2
            ZC0 = STATE_COLS + GN
            cols_xbc = [i * 128 for i in range(32)] + [DIN + i * 128 for i in range(8)] + [STATE_COLS + i * 128 for i in range(8)]
            r_XBC = [Res() for _ in range(48)]
            r_Z = [Res() for _ in range(NCH)]
            r_XS = [Res() for _ in range(NCH)]
            r_BTM = [Res() for _ in range(NCH)]
            r_HF = [Res() for _ in range(NCH)]
            r_GT = [Res() for _ in range(NCH)]

            def bc(ap, n0, n1):
                return ap.unsqueeze(2).to_broadcast([128, n0, n1])

            def v3(ap, n1=64):
                return ap.rearrange("s (h p) -> s h p", p=n1)

            with Phase(S) as pho:
                dt_tm, r_dt = pho.sb([128, NCH, 128], F32, "dt_tm")
                with Phase(S) as ph1:
                    aT, _ = ph1.sb([128, KC, T], BF16, "aT")
                    r_aT = [Res() for _ in range(KC)]
                    norm_phase(aT, r_aT, lambda c, j: A1[:, l, c, j:j + 1], lambda c, j: MOD[:, l, c, j:j + 1], [r_A1, r_MOD])
                    with Phase(S) as ph:
                        P1 = [ph.sb([128, 2308], F32, "p1") for _ in range(2)]
                        acc = [ph.sb([128, T], F32, "cacc") for _ in range(2)]
                        stg = [ph.sb([128, T], BF16, "cst") for _ in range(2)]
                        for p_, rp_ in P1:
                            S.op("dve", lambda: V.memset(p_[:], 0.0), writes=[rp_])

                        def ep_conv(mi, col, bl):
                            p_, rp_ = P1[mi % 2]
                            ac, rac = acc[mi % 2]
                            st_, rst = stg[mi % 2]
                            for ti, (t0, n) in enumerate(TT):
                                bk, rbk = bl[ti]
                                dst = p_[:, 1 + t0:1 + t0 + n] if t0 < L else p_[:, 2051:2307]
                                S.op("act", lambda: A.activation(out=dst, in_=bk[:, :n], func=AF.Copy), reads=[rbk], writes=[rp_])
                            for (o0, o1, base) in ((0, L, 0), (L, T, 2050)):
                                wd_ = o1 - o0
                                S.op("dve", lambda: V.tensor_scalar(out=ac[:, o0:o1], in0=p_[:, base:base + wd_], scalar1=scw[:, js, mi, 0:1],
                                                                    scalar2=scb[:, js, mi:mi + 1], op0=ALU.mult, op1=ALU.add),
                                     reads=[rp_, r_scw, r_scb], writes=[rac])
                                for k in (1, 2):
                                    S.op("dve", lambda: V.scalar_tensor_tensor(out=ac[:, o0:o1], in0=p_[:, base + k:base + k + wd_], scalar=scw[:, js, mi, k:k + 1],
                                                                               in1=ac[:, o0:o1], op0=ALU.mult, op1=ALU.add),
                                         reads=[rp_, rac, r_scw], writes=[rac])
                            S.op("act", lambda: A.activation(out=st_[:], in_=ac[:], func=AF.Silu), reads=[rac], writes=[rst])
                            S.dma("sp", XBC[mi * 128:(mi + 1) * 128, :], st_[:], reads=[rst], writes=[r_XBC[mi]])

                        linear(ph, lambda col: w_in[js, :, col:col + 128], cols_xbc, KC, aT, r_aT, ep_conv, wtag="win")
                    if ssd_upto < 2:
                        raise _Stop()
                    with Phase(S) as ph:
                        wz = [ph.sb([128, KC, 512], BF16, "wz") for _ in range(2)]
                        zst = [ph.sb([128, 512], BF16, "zst") for _ in range(4)]
                        k = 0
                        for ct in range(8):
                            w, rw = wz[ct % 2]
                            S.dma("pool", w[:], w_in[js, :, ZC0 + ct * 512:ZC0 + (ct + 1) * 512].rearrange("(kc p) c -> p kc c", p=128), writes=[rw])
                            for tt in range(NCH):
                                bk, rbk = S.bank()
                                for kc in range(KC):
                                    S.op("pe", lambda: PE.matmul(bk[:, :], aT[:, kc, tt * 128:(tt + 1) * 128], w[:, kc, :], start=(kc == 0), stop=(kc == KC - 1)),
                                         reads=[rw, r_aT[kc]], writes=[rbk], inc=(kc == KC - 1))
                                z_, rz_ = zst[k % 4]
                                k += 1
                                S.op("act", lambda: A.activation(out=z_[:], in_=bk[:, :], func=AF.Silu), reads=[rbk], writes=[rz_])
                                S.dma("sp", Zd[tt * 128:(tt + 1) * 128, ct * 512:(ct + 1) * 512], z_[:], reads=[rz_], writes=[r_Z[tt]])
                        wdt, rwdt = ph.sb([128, KC, 128], BF16, "wdt")
                        dtb, rdtb = ph.sb([128, 128], F32, "dtb")
                        S.dma("sp", dtb[:], dtb_in[:, js, :], writes=[rdtb])
                        S.dma("pool", wdt[:], w_in[js, :, DIN + GN:DIN + GN + 128].rearrange("(kc p) c -> p kc c", p=128), writes=[rwdt])
                        tmp = [ph.sb([128, 128], F32, "dtt") for _ in range(2)]
                        for tt in range(NCH):
                            bk, rbk = S.bank()
                            for kc in range(KC):
                                S.op("pe", lambda: PE.matmul(bk[:, 0:128], aT[:, kc, tt * 128:(tt + 1) * 128], wdt[:, kc, :], start=(kc == 0), stop=(kc == KC - 1)),
                                     reads=[rwdt, r_aT[kc]], writes=[rbk], inc=(kc == KC - 1))
                            t_, rt_ = tmp[tt % 2]
                            S.op("dve", lambda: V.tensor_tensor(out=t_[:], in0=bk[:, 0:128], in1=dtb[:], op=ALU.add), reads=[rbk, rdtb], writes=[rt_])
                            S.op("act", lambda: A.activation(out=t_[:], in_=t_[:], func=AF.Exp), reads=[rt_], writes=[rt_])
                            S.op("act", lambda: A.activation(out=dt_tm[:, tt, :], in_=t_[:], func=AF.Ln, bias=ones_f[:, 0:1]), reads=[rt_, r_onesf], writes=[r_dt])

                if ssd_upto < 3:
                    raise _Stop()
                with Phase(S) as pd:
                    nacsl, r_nacsl = pd.sb([128, NCH, 128], F32, "nacsl")
                    ea, r_ea = pd.sb([128, NCH, 128], F32, "ea")
                    dtw, r_dtw = pd.sb([128, NCH, 128], F32, "dtw")
                    cd, r_cd = pd.sb([128, NCH, 128], F32, "cd")
                    acs2 = [pd.sb([128, T], BF16, "acs2") for _ in range(2)]
                    sel2, r_sel2 = pd.sb([128, 64], BF16, "sel2")
                    S.op("dve", lambda: V.tensor_copy(out=sel2[0:64, :], in_=identb[0:64, 0:64]), reads=[r_identb], writes=[r_sel2])
                    S.op("dve", lambda: V.tensor_copy(out=sel2[64:128, :], in_=identb[64:128, 64:128]), reads=[r_identb], writes=[r_sel2])
                    epsc, r_epsc = pd.sb([128, 1], F32, "epsc")
                    S.op("dve", lambda: V.memset(epsc[:], EPS), writes=[r_epsc])
                    abc, r_abc = pd.sb([128, 128], F32, "abc")
                    dsk, r_dsk = pd.sb([128, 128], F32, "dsk")
                    dsum, r_dsum = pd.sb([128, 64], F32, "dsum")
                    identf, r_identf = pd.sb([128, 128], F32, "identf")
                    S.op("dve", lambda: V.tensor_tensor(out=identf[:], in0=cf[:, 0, :], in1=cf[:, 1, :], op=ALU.mult), reads=[r_cf], writes=[r_identf])
                    S.dma("sp", abc[:], alog_in[:, js, :], writes=[r_abc])
                    S.dma("sp", dsk[:], dsk_in[:, js, :], writes=[r_dsk])
                    S.op("act", lambda: A.activation(out=abc[:], in_=abc[:], func=AF.Exp), reads=[r_abc], writes=[r_abc])
                    S.op("dve", lambda: V.tensor_scalar(out=abc[:], in0=abc[:], scalar1=-1.0, scalar2=None, op0=ALU.mult), reads=[r_abc], writes=[r_abc])
                    S.op("dve", lambda: V.tensor_tensor(out=dsum[:], in0=dsk[:, 0:64], in1=dsk[:, 64:128], op=ALU.add), reads=[r_dsk], writes=[r_dsum])
                    with Phase(S) as pp:
                        dta, r_dta = pp.sb([128, NCH, 128], F32, "dta")
                        nacs, r_nacs = pp.sb([128, NCH, 128], F32, "nacs")
                        lnd = [pp.sb([128, 128], F32, "lnd") for _ in range(2)]
                        acsT = [pp.sb([128, T], F32, "acsT") for _ in range(2)]
                        r1_, r_r1 = pp.sb([128, T], F32, "r1")
                        dup = [pp.sb([128, 2, 128], F32, "dup") for _ in range(2)]
                        tw = [pp.sb([128, 128], F32, "tw") for _ in range(2)]
                        for c in range(NCH):
                            S.op("dve", lambda: V.tensor_tensor(out=dta[:, c, :], in0=dt_tm[:, c, :], in1=abc[:], op=ALU.mult), reads=[r_dt, r_abc], writes=[r_dta])
                            (bA, rbA), (bB, rbB), (bC, rbC), (bD, rbD) = S.bank(), S.bank(), S.bank(), S.bank()
                            du, rdu = dup[c % 2]
                            for d in range(2):
                                S.op("dve", lambda: V.tensor_copy(out=du[:, d, :].rearrange("s (r h) -> s r h", r=2),
                                                                  in_=dta[:, c, d * 64:(d + 1) * 64].unsqueeze(1).to_broadcast([128, 2, 64])),
                                     reads=[r_dta], writes=[rdu])
                            S.op("pe", lambda: PE.matmul(bA[:, 0:64], cf[:, 0, :], dta[:, c, 0:64], start=True, stop=True), reads=[r_cf, r_dta], writes=[rbA], inc=False)
                            S.op("pe", lambda: PE.matmul(bA[:, 64:128], cf[:, 1, :], dta[:, c, 64:128], start=True, stop=True), reads=[r_cf, r_dta], writes=[rbA], inc=False)
                            S.op("pe", lambda: PE.matmul(bB[:, 0:128], ones_f[:], dta[:, c, :], start=True, stop=True), reads=[r_onesf, r_dta], writes=[rbB], inc=False)
                            S.op("pe", lambda: PE.matmul(bC[:, 0:128], du[:, 0, :], cf[:, 0, :], start=True, stop=True), reads=[r_cf, rdu], writes=[rbC], inc=False)
                            S.op("pe", lambda: PE.matmul(bD[:, 0:128], du[:, 1, :], cf[:, 1, :], start=True, stop=True), reads=[r_cf, rdu], writes=[rbD], inc=True)
                            S.op("dve", lambda: V.tensor_scalar(out=nacs[:, c, :], in0=bA[:, 0:128], scalar1=-1.0, scalar2=None, op0=ALU.mult), reads=[rbA], writes=[r_nacs])
                            S.op("act", lambda: A.activation(out=ea[:, c, :], in_=bA[:, 0:128], func=AF.Exp), reads=[rbA], writes=[r_ea])
                            ld_, rld_ = lnd[c % 2]
                            S.op("act", lambda: A.activation(out=ld_[:], in_=dt_tm[:, c, :], func=AF.Ln), reads=[r_dt], writes=[rld_])
                            S.op("dve", lambda: V.tensor_tensor(out=nacsl[:, c, :], in0=nacs[:, c, :], in1=ld_[:], op=ALU.add), reads=[r_nacs, rld_], writes=[r_nacsl])
                            t_, rt_ = tw[c % 2]
                            S.op("dve", lambda: V.tensor_tensor(out=t_[:], in0=bB[:, 0:128], in1=nacs[:, c, :], op=ALU.add), reads=[rbB, r_nacs], writes=[rt_])
                            S.op("act", lambda: A.activation(out=t_[:], in_=t_[:], func=AF.Exp), reads=[rt_], writes=[rt_])
                            S.op("dve", lambda: V.tensor_tensor(out=dtw[:, c, :], in0=dt_tm[:, c, :], in1=t_[:], op=ALU.mult), reads=[r_dt, rt_], writes=[r_dtw])
                            S.op("act", lambda: A.activation(out=cd[:, c, :], in_=bB[:, 0:128], func=AF.Exp), reads=[rbB], writes=[r_cd])
                            S.op("act", lambda: A.activation(out=acsT[0][0][:, c * 128:(c + 1) * 128], in_=bC[:, 0:128], func=AF.Copy), reads=[rbC], writes=[acsT[0][1]])
                            S.op("dve", lambda: V.tensor_copy(out=acsT[1][0][:, c * 128:(c + 1) * 128], in_=bD[:, 0:128]), reads=[rbD], writes=[acsT[1][1]])

                        for d in range(2):
                            (a_, ra_), (a2, ra2) = acsT[d], acs2[d]
                            S.op("dve", lambda: V.tensor_copy(out=a2[:], in_=a_[:]), reads=[ra_], writes=[ra2])
                            S.op("dve", lambda: V.tensor_tensor(out=r1_[64:128, :], in0=a_[64:128, :], in1=a2[64:128, :], op=ALU.subtract), reads=[ra_, ra2], writes=[r_r1])
                            S.op("dve", lambda: V.tensor_copy(out=a2[64:128, :], in_=r1_[64:128, :]), reads=[r_r1], writes=[ra2])

                    if ssd_upto < 4:
                        raise _Stop()
                    with Phase(S) as pf:
                        Sf, r_Sf = pf.sb([128, DIN], F32, "Sf")
                        Sbf = [pf.sb([128, DIN], BF16, "Sbf") for _ in range(2)]
                        xsT, r_xsT = pf.sb([128, 32, 128], BF16, "xsT")
                        BT, r_BT = pf.sb([128, 8, 128], BF16, "BT")
                        xs_tm, _ = pf.sb([128, DIN], BF16, "xs_tm")
                        r_xs = [Res() for _ in range(8)]
                        B_tm, _ = pf.sb([128, GN], BF16, "B_tm")
                        r_Btm = [Res() for _ in range(2)]
                        xw = [pf.sb([128, 512], BF16, "xw") for _ in range(2)]
                        S.op("dve", lambda: V.memset(Sf[:], 0.0), writes=[r_Sf])
                        S.op("dve", lambda: V.memset(Sbf[0][0][:], 0.0), writes=[Sbf[0][1]])
                        order = [16, 17] + list(range(16))
                        for oi, c in enumerate(order):
                            c0, c1 = c * 128, (c + 1) * 128
                            S.dma("sp", xsT[:], XBC[0:DIN, c0:c1].rearrange("(fc p) t -> p fc t", p=128), reads=r_XBC[0:32], writes=[r_xsT])
                            S.dma("sp", BT[:], XBC[DIN:DIN + GN, c0:c1].rearrange("(fc p) t -> p fc t", p=128), reads=r_XBC[32:40], writes=[r_BT])
                            for fb in range(10):
                                bk, rbk = S.bank()
                                pb = bk[:].bitcast(BF16)
                                for i in range(4):
                                    src = xsT[:, fb * 4 + i, :] if fb < 8 else BT[:, (fb - 8) * 4 + i, :]
                                    S.op("pe", lambda: PE.transpose(out=pb[:, i * 128:(i + 1) * 128], in_=src, identity=identb[:]),
                                         reads=[r_xsT if fb < 8 else r_BT, r_identb], writes=[rbk], inc=(i == 3))
                                dst = xs_tm[:, fb * 512:(fb + 1) * 512] if fb < 8 else B_tm[:, (fb - 8) * 512:(fb - 7) * 512]
                                rd = r_xs[fb] if fb < 8 else r_Btm[fb - 8]
                                if fb % 2 == 0:
                                    S.op("act", lambda: A.activation(out=dst, in_=pb[:, 0:512], func=AF.Copy), reads=[rbk], writes=[rd])
                                else:
                                    S.op("dve", lambda: V.tensor_copy(out=dst, in_=pb[:, 0:512]), reads=[rbk], writes=[rd])
                            S.dma("sp", XSd[c0:c1, :], xs_tm[:], reads=r_xs, writes=[r_XS[c]])
                            S.dma("sp", BTMd[c0:c1, :], B_tm[:], reads=r_Btm, writes=[r_BTM[c]])
                            cur, rcur = Sbf[oi % 2]
                            nxt, rnxt = Sbf[(oi + 1) % 2]
                            S.dma("sp", HFd[c], cur[:], reads=[rcur], writes=[r_HF[c]])
                            for g in range(8):
                                gs = slice(g * 512, (g + 1) * 512)
                                x_, rx_ = xw[g % 2]
                                S.op("dve", lambda: V.tensor_tensor(out=v3(x_[:]), in0=v3(xs_tm[:, gs]), in1=bc(dtw[:, c, g * 8:(g + 1) * 8], 8, 64), op=ALU.mult),
                                     reads=[r_xs[g], r_dtw], writes=[rx_])
                                bk, rbk = S.bank()
                                S.op("pe", lambda: PE.matmul(bk[:, :], B_tm[:, g * 128:(g + 1) * 128], x_[:], start=True, stop=True),
                                     reads=[r_Btm[g // 4], rx_], writes=[rbk])
                                S.op("dve", lambda: V.tensor_tensor(out=v3(Sf[:, gs]), in0=v3(Sf[:, gs]), in1=bc(cd[:, c, g * 8:(g + 1) * 8], 8, 64), op=ALU.mult),
                                     reads=[r_Sf, r_cd], writes=[r_Sf])
                                S.op("dve", lambda: V.tensor_tensor(out=Sf[:, gs], in0=Sf[:, gs], in1=bk[:, :], op=ALU.add), reads=[r_Sf, rbk], writes=[r_Sf])
                                S.op("act", lambda: A.activation(out=nxt[:, gs], in_=Sf[:, gs], func=AF.Copy), reads=[r_Sf], writes=[rnxt])

                    if ssd_upto < 5:
                        raise _Stop()
                    with Phase(S) as pb_:
                        Sb_, r_Sb = pb_.sb([128, DIN], F32, "Sb")
                        Sbb, r_Sbb = pb_.sb([128, DIN], BF16, "Sbb")
                        xs2 = [pb_.sb([128, DIN], BF16, "xs_tm") for _ in range(2)]
                        Bt2 = [pb_.sb([128, GN], BF16, "B_tm") for _ in range(2)]
                        BT2 = [pb_.sb([128, 8, 128], BF16, "BT") for _ in range(2)]
                        CT2 = [pb_.sb([128, 8, 128], BF16, "CT") for _ in range(2)]
                        hf, r_hf = pb_.sb([128, DIN], BF16, "hf")
                        z_tm, r_z = pb_.sb([128, DIN], BF16, "z_tm")
                        scm, r_scm = pb_.sb([128, 2, 8, 128], BF16, "scm")
                        Db = [pb_.sb([128, 512], BF16, "Db") for _ in range(4)]
                        M4 = [pb_.sb([128, 512], BF16, "M4") for _ in range(4)]
                        xw = [pb_.sb([128, 512], BF16, "xw") for _ in range(2)]
                        t1b = [pb_.sb([128, 512], F32, "t1") for _ in range(2)]
                        t2b = [pb_.sb([128, 512], F32, "t2") for _ in range(2)]
                        gyb = [pb_.sb([128, 512], F32, "gy") for _ in range(2)]
                        t3b = [pb_.sb([128, 512], F32, "t3") for _ in range(2)]
                        ss, r_ss = pb_.sb([128, 8], F32, "ss")
                        g_tm, r_gtm = pb_.sb([128, DIN], BF16, "g_tm")
                        gTst, r_gTst = pb_.sb([128, 32, 128], BF16, "gTst")
                        r_Sbg = [Res() for _ in range(8)]
                        r_Sbbg = [Res() for _ in range(8)]
                        sq2 = [pb_.sb([128, 512], F32, "sq2") for _ in range(2)]
                        S.op("dve", lambda: V.memset(Sb_[:], 0.0), writes=r_Sbg)
                        S.op("dve", lambda: V.memset(Sbb[:], 0.0), writes=r_Sbbg)
                        (bOf, rbOf), (bOb, rbOb), (bSt, rbSt), (bY, rbY) = S.banks[4], S.banks[5], S.banks[6], S.banks[7]
                        order = [17, 16] + list(range(15, -1, -1))

                        def loads_a(ci):
                            c = order[ci]
                            c0, c1 = c * 128, (c + 1) * 128
                            S.dma("sp", BT2[ci % 2][0][:], XBC[DIN:DIN + GN, c0:c1].rearrange("(fc p) t -> p fc t", p=128), reads=r_XBC[32:40], writes=[BT2[ci % 2][1]])
                            S.dma("sp", CT2[ci % 2][0][:], XBC[DIN + GN:DIN + 2 * GN, c0:c1].rearrange("(fc p) t -> p fc t", p=128), reads=r_XBC[40:48], writes=[CT2[ci % 2][1]])
                            S.dma("sp", xs2[ci % 2][0][:], XSd[c0:c1, :], reads=[r_XS[c]], writes=[xs2[ci % 2][1]])
                            S.dma("sp", Bt2[ci % 2][0][:], BTMd[c0:c1, :], reads=[r_BTM[c]], writes=[Bt2[ci % 2][1]])

                        def load_hf(ci):
                            c = order[ci]
                            S.dma("sp", hf[:], HFd[c], reads=[r_HF[c]], writes=[r_hf])

                        def load_z(ci):
                            c = order[ci]
                            S.dma("sp", z_tm[:], Zd[c * 128:(c + 1) * 128, :], reads=[r_Z[c]], writes=[r_z])

                        loads_a(0)
                        load_hf(0)
                        load_z(0)
                        for ci, c in enumerate(order):
                            c0, c1 = c * 128, (c + 1) * 128
                            (xs_tm, r_xs), (B_tm, r_Btm), (BT, r_BT), (CT, r_CT) = xs2[ci % 2], Bt2[ci % 2], BT2[ci % 2], CT2[ci % 2]
                            if ci + 1 < len(order):
                                loads_a(ci + 1)
                            for half in range(2):
                                bkS, rbkS = S.banks[half]
                                for gi in range(4):
                                    g = half * 4 + gi
                                    S.op("pe", lambda: PE.matmul(bkS[:, gi * 128:(gi + 1) * 128], BT[:, g, :], CT[:, g, :], start=True, stop=True),
                                         reads=[r_BT, r_CT], writes=[rbkS], inc=(gi == 3))
                                for d in range(2):
                                    S.op("dve", lambda: V.tensor_tensor(out=scm[:, d, half * 4:(half + 1) * 4, :], in0=bkS[:, :].rearrange("s (g l) -> s g l", l=128),
                                                                        in1=mkb[:, d, :].unsqueeze(1).to_broadcast([128, 4, 128]), op=ALU.mult),
                                         reads=[rbkS, r_mkb], writes=[r_scm])

                            def RW(k):
                                g, pair = k // 4, k % 4
                                bk, rbk = S.banks[k % 4]
                                for hh in range(2):
                                    for d in range(2):
                                        h = g * 8 + pair * 2 + hh
                                        i = hh * 2 + d
                                        S.op("pe", lambda: PE.matmul(bk[:, i * 128:(i + 1) * 128], sel2[:, h:h + 1].to_broadcast([128, 128]),
                                                                     acs2[d][0][:, c0:c1], start=True, stop=True),
                                             reads=[r_sel2, acs2[d][1]], writes=[rbk], inc=(i == 3))

                            def SA(k):
                                g, pair = k // 4, k % 4
                                h0 = g * 8 + pair * 2
                                bk, rbk = S.banks[k % 4]
                                d_, rd_ = Db[k % 4]
                                for hh in range(2):
                                    for d in range(2):
                                        i = hh * 2 + d
                                        col = d * 64 + h0 + hh
                                        S.op("act", lambda: A.activation(out=d_[:, i * 128:(i + 1) * 128], in_=bk[:, i * 128:(i + 1) * 128], func=AF.Exp,
                                                                         bias=nacsl[:, c, col:col + 1], scale=1.0),
                                             reads=[rbk, r_nacsl], writes=[rd_])

                            def SC(k):
                                g, pair = k // 4, k % 4
                                d_, rd_ = Db[k % 4]
                                m_, rm_ = M4[k % 4]
                                for a in range(2):
                                    S.op("dve", lambda: V.scalar_tensor_tensor(out=m_[:, a * 256:(a + 1) * 256].rearrange("s (d l) -> s d l", d=2),
                                                                               in0=d_[:, a * 256:(a + 1) * 256].rearrange("s (d l) -> s d l", d=2), scalar=1.0e4,
                                                                               in1=scm[:, :, g, :], op0=ALU.min, op1=ALU.mult),
                                         reads=[rd_, r_scm], writes=[rm_])
                                for hh in range(2):
                                    r = pair * 2 + hh
                                    h = g * 8 + r
                                    xh = xs_tm[:, h * 64:(h + 1) * 64]
                                    for d in range(2):
                                        i = hh * 2 + d
                                        S.op("pe", lambda: PE.matmul(bY[:, r * 64:(r + 1) * 64], m_[:, i * 128:(i + 1) * 128], xh, start=(d == 0), stop=(d == 1)),
                                             reads=[rm_, r_xs], writes=[rbY], inc=(d == 1))

                            def grp_pool(g):
                                gs = slice(g * 512, (g + 1) * 512)
                                x_, rx_ = xw[g % 2]
                                S.op("pool", lambda: nc.gpsimd.tensor_tensor(out=v3(x_[:]), in0=v3(xs_tm[:, gs]),
                                                                             in1=bc(dtw[:, c, 64 + g * 8:64 + (g + 1) * 8], 8, 64), op=ALU.mult),
                                     reads=[r_xs, r_dtw], writes=[rx_])
                                t3, rt3 = t3b[g % 2]
                                S.op("pool", lambda: nc.gpsimd.tensor_tensor(out=v3(t3[:]), in0=v3(xs_tm[:, gs]), in1=bc(dsum[:, g * 8:(g + 1) * 8], 8, 64), op=ALU.mult),
                                     reads=[r_xs, r_dsum], writes=[rt3])

                            def grp_start(g):
                                gs = slice(g * 512, (g + 1) * 512)
                                x_, rx_ = xw[g % 2]
                                S.op("pe", lambda: PE.matmul(bOf[:, :], CT[:, g, :], hf[:, gs], start=True, stop=True), reads=[r_CT, r_hf], writes=[rbOf])
                                S.op("pe", lambda: PE.matmul(bOb[:, :], CT[:, g, :], Sbb[:, gs], start=True, stop=True), reads=[r_CT, r_Sbbg[g]], writes=[rbOb])
                                S.op("pe", lambda: PE.matmul(bSt[:, :], B_tm[:, g * 128:(g + 1) * 128], x_[:], start=True, stop=True),
                                     reads=[r_Btm, rx_], writes=[rbSt])

                            def grp_early(g):
                                gs = slice(g * 512, (g + 1) * 512)
                                (t1, rt1), (t2, rt2), (t3, rt3) = t1b[g % 2], t2b[g % 2], t3b[g % 2]
                                S.op("dve", lambda: V.tensor_tensor(out=v3(t1[:]), in0=v3(bOf[:, :]), in1=bc(ea[:, c, g * 8:(g + 1) * 8], 8, 64), op=ALU.mult),
                                     reads=[rbOf, r_ea], writes=[rt1])
                                S.op("dve", lambda: V.tensor_tensor(out=v3(t2[:]), in0=v3(bOb[:, :]), in1=bc(ea[:, c, 64 + g * 8:64 + (g + 1) * 8], 8, 64), op=ALU.mult),
                                     reads=[rbOb, r_ea], writes=[rt2])
                                S.op("pool", lambda: nc.gpsimd.tensor_tensor(out=t2[:], in0=t2[:], in1=t3[:], op=ALU.add), reads=[rt2, rt3], writes=[rt2])
                                S.op("dve", lambda: V.tensor_tensor(out=v3(Sb_[:, gs]), in0=v3(Sb_[:, gs]), in1=bc(cd[:, c, 64 + g * 8:64 + (g + 1) * 8], 8, 64), op=ALU.mult),
                                     reads=[r_Sbg[g], r_cd], writes=[r_Sbg[g]])
                                S.op("dve", lambda: V.tensor_tensor(out=Sb_[:, gs], in0=Sb_[:, gs], in1=bSt[:, :], op=ALU.add), reads=[r_Sbg[g], rbSt], writes=[r_Sbg[g]])
                                S.op("act", lambda: A.activation(out=Sbb[:, gs], in_=Sb_[:, gs], func=AF.Copy), reads=[r_Sbg[g]], writes=[r_Sbbg[g]])

                            def grp_endA(g):
                                gs = slice(g * 512, (g + 1) * 512)
                                (t1, rt1), (t2, rt2), (gy, rgy), (sq_, rsq_) = t1b[g % 2], t2b[g % 2], gyb[g % 2], sq2[g % 2]
                                S.op("dve", lambda: V.tensor_tensor(out=t1[:], in0=t1[:], in1=bY[:, :], op=ALU.add), reads=[rt1, rbY], writes=[rt1])
                                S.op("pool", lambda: nc.gpsimd.tensor_tensor(out=t1[:], in0=t1[:], in1=t2[:], op=ALU.add), reads=[rt1, rt2], writes=[rt1])
                                S.op("pool", lambda: nc.gpsimd.tensor_tensor(out=gy[:], in0=t1[:], in1=z_tm[:, gs], op=ALU.mult), reads=[rt1, r_z], writes=[rgy])

                            def grp_endB(g):
                                gs = slice(g * 512, (g + 1) * 512)
                                (gy, rgy), (sq_, rsq_) = gyb[g % 2], sq2[g % 2]
                                S.op("act", lambda: A.activation(out=sq_[:], in_=gy[:], func=AF.Square), reads=[rgy], writes=[rsq_])
                                S.op("dve", lambda: V.reduce_sum(out=ss[:, g:g + 1], in_=sq_[:], axis=AX.X), reads=[rsq_], writes=[r_ss])
                                S.op("act", lambda: A.activation(out=ss[:, g:g + 1], in_=ss[:, g:g + 1], func=AF.Ln, bias=epsc[:, 0:1], scale=1.0 / 512.0),
                                     reads=[r_ss, r_epsc], writes=[r_ss])
                                S.op("act", lambda: A.activation(out=ss[:, g:g + 1], in_=ss[:, g:g + 1], func=AF.Exp, scale=-0.5), reads=[r_ss], writes=[r_ss])
                                S.op("act", lambda: A.activation(out=g_tm[:, gs], in_=gy[:], func=AF.Identity, scale=ss[:, g:g + 1]), reads=[rgy, r_ss], writes=[r_gtm])

                            grp_pool(0)
                            for k in range(4):
                                RW(k)
                            SA(0)
                            SA(1)
                            for k in range(32):
                                g = k // 4
                                if k % 4 == 0:
                                    grp_start(g)
                                    if g + 1 < 8:
                                        grp_pool(g + 1)
                                    if g == 7 and ci + 1 < len(order):
                                        load_hf(ci + 1)
                                if k % 4 == 1:
                                    grp_early(g)
                                if k % 4 == 2 and g > 0:
                                    grp_endB(g - 1)
                                if k + 2 < 32:
                                    SA(k + 2)
                                SC(k)
                                if k + 4 < 32:
                                    RW(k + 4)
                                if k % 4 == 3:
                                    grp_endA(g)
                            if ci + 1 < len(order):
                                load_z(ci + 1)
                            grp_endB(7)
                            for fb in range(8):
                                bk, rbk = S.banks[fb % 4]
                                pbv = bk[:].bitcast(BF16)
                                for i in range(4):
                                    fc = fb * 4 + i
                                    S.op("pe", lambda: PE.transpose(out=pbv[:, i * 128:(i + 1) * 128], in_=g_tm[:, fc * 128:(fc + 1) * 128], identity=identb[:]),
                                         reads=[r_gtm, r_identb], writes=[rbk], inc=(i == 3))
                                dst = gTst[:, fb * 4:(fb + 1) * 4, :]
                                if fb % 2 == 0:
                                    S.op("act", lambda: A.activation(out=dst, in_=pbv[:, 0:512].rearrange("f (i t) -> f i t", t=128), func=AF.Copy), reads=[rbk], writes=[r_gTst])
                                else:
                                    S.op("dve", lambda: V.tensor_copy(out=dst, in_=pbv[:, 0:512].rearrange("f (i t) -> f i t", t=128)), reads=[rbk], writes=[r_gTst])
                            S.dma("sp", GTd[:, c0:c1].rearrange("(fc p) t -> p fc t", p=128), gTst[:], reads=[r_gTst], writes=[r_GT[c]])

            if ssd_upto < 6:
                raise _Stop()
            with Phase(S) as po:
                gT, r_gT = po.sb([128, 32, T], BF16, "gT")
                sng, r_sng = po.sb([128, 32], F32, "sngT")
                S.dma("sp", sng[:], sng_in[:, js, :], writes=[r_sng])

                def wfix(w, rw):
                    S.op("dve", lambda: V.tensor_tensor(out=w[:], in0=w[:], in1=sng[:].unsqueeze(2).to_broadcast([128, 32, 128]), op=ALU.mult),
                         reads=[rw, r_sng], writes=[rw])
                S.dma("sp", gT[:], GTd.rearrange("(fc p) t -> p fc t", p=128), reads=r_GT, writes=[r_gT])
                linear(po, lambda col: w_out[js, :, col:col + 128], [m * 128 for m in range(KC)], 32, gT, r_gT,
                       resid_epilogue(po, lambda m, j: MOD[:, l, 32 + m, j:j + 1], nbuf=2), wtag="wo", wfix=wfix)

        r_UV = Res("UV")

        stage_mod()
        stage_i = 0
        for l in range(NL):
            if stage_i < upto:
                if l % 2 == 0:
                    stage_fourier(l)
                else:
                    try:
                        stage_ssd(l)
                    except _Stop:
                        pass
            stage_i += 1
            if stage_i < upto:
                stage_ffn(l)
            stage_i += 1

        if final and upto >= 2 * NL:
            norm_phase(None, None, lambda c, j: AFN[:, c, 0:1], lambda c, j: ZB[:, c, 0:1], [r_AFN, r_ZB], out_f32=out_d)
        else:
            with Phase(S) as ph:
                xb = [ph.sb([128, T], F32, "cp") for _ in range(2)]
                for m in range(KC):
                    x, rx = xb[m % 2]
                    S.dma("sp", x[:], XT[m * 128:(m + 1) * 128, :], reads=[r_XT[m]], writes=[rx])
                    S.dma("sp", out_d[m * 128:(m + 1) * 128, :], x[:], reads=[rx])
        S.barrier()
    return nc


_CONST = {}


def _consts():
    if _CONST:
        return _CONST
    bf = ml_dtypes.bfloat16
    k = np.arange(256)
    ang = 2.0 * np.pi * ((k[:, None] * k[None, :]) % 256) / 256.0
    c256 = np.cos(ang) / 16.0
    s256 = np.sin(ang) / 16.0
    cs1 = np.concatenate([c256, s256], axis=1)
    cs2c = np.concatenate([c256, -s256], axis=1)
    _CONST["cs1"] = np.ascontiguousarray(cs1.reshape(2, 128, 512).transpose(1, 0, 2)).astype(bf)
    _CONST["cs2c"] = np.ascontiguousarray(cs2c.reshape(2, 128, 512).transpose(1, 0, 2)).astype(bf)
    t = np.arange(L)
    angL = 2.0 * np.pi * ((t[:, None] * t[None, :]) % L) / float(L)
    _CONST["cl"] = (np.cos(angL) / math.sqrt(L)).astype(bf)
    _CONST["nsl"] = (-np.sin(angL) / math.sqrt(L)).astype(bf)
    _CONST["identb"] = np.eye(128, dtype=np.float32).astype(bf)
    s = np.arange(128)
    triU = (s[:, None] <= s[None, :]).astype(np.float32)
    triL = (s[:, None] >= s[None, :]).astype(np.float32)
    _CONST["cf32"] = np.ascontiguousarray(np.stack([triU, triL, triU, triL], axis=1)).astype(np.float32)
    return _CONST


def _pm(v, nchunk):
    v = np.asarray(v, dtype=np.float32)
    lead = v.shape[:-1]
    r = v.reshape(lead + (nchunk, 128))
    r = np.moveaxis(r, -1, 0)
    return np.ascontiguousarray(r)


def make_in_maps(inputs, cores):
    cst = _consts()
    f = lambda k: np.asarray(inputs[k], dtype=np.float32)
    shared = {
        "w_mod": f("w_mod"), "four_w": f("four_w"), "ssd_w_in": f("ssd_w_in"), "ssd_w_out": f("ssd_w_out"),
        "ffn_w_up": f("ffn_w_up"), "ffn_w_down": f("ffn_w_down"),
        "bmodT": _pm(f("b_mod"), 96), "gmixT": _pm(f("norm_mix_g"), KC), "gffnT": _pm(f("norm_ffn_g"), KC),
        "gfinT": _pm(f("final_g"), KC),
        "scwT": np.ascontiguousarray(_pm(f("ssd_conv_w"), 48).transpose(0, 1, 3, 2)),
        "scbT": _pm(f("ssd_conv_b"), 48),
        "dtb_bc": np.ascontiguousarray(np.broadcast_to(f("ssd_dt_bias").reshape(1, 2, 128), (128, 2, 128))),
        "alog_bc": np.ascontiguousarray(np.broadcast_to(f("ssd_a_log").reshape(1, 2, 128), (128, 2, 128))),
        "dsk_bc": np.ascontiguousarray(np.broadcast_to(f("ssd_d").reshape(1, 2, 128), (128, 2, 128))),
        "sngT": _pm(f("ssd_norm_g"), 32),
        "fcwT": np.ascontiguousarray(_pm(f("ffn_conv_w").reshape(NL, 9, DFF), 44).transpose(0, 1, 3, 2)),
        "fcbT": _pm(f("ffn_conv_b"), 44),
        "cs1": cst["cs1"], "cs2c": cst["cs2c"], "cl": cst["cl"], "nsl": cst["nsl"], "identb": cst["identb"], "cf32": cst["cf32"],
    }
    x, c, ctx, cc = f("x"), f("c"), f("ctx"), f("c_ctx")
    maps = []
    for b in cores:
        m = dict(shared)
        m["xc"] = np.ascontiguousarray(np.concatenate([x[b].T, ctx[b].T], axis=1))
        m["cv"] = np.ascontiguousarray(np.stack([c[b], cc], axis=-1).reshape(KC, 128, 2).transpose(1, 0, 2))
        maps.append(m)
    return maps


def kernel(**inputs):
    nc = build()
    maps = make_in_maps(inputs, list(range(8)))
    res = run_bass_kernel_spmd(nc, maps, core_ids=list(range(8)))
    out = np.stack([np.ascontiguousarray(res.results[b]["out"][:, :L].T) for b in range(8)], axis=0)
    return out.astype(np.float32)
```
